# Optimizing a Trainium2 kernel written in Bass

```python
import jax
import jax.numpy as jnp
from jax import lax
import numpy as np

D_MODEL = 1024
BATCH = 8
SEQ = 2048
DEPTH = 2

HEAD_DIM = 64
N_MIX_HEADS = D_MODEL // HEAD_DIM
MLSTM_HEADS = (3 * N_MIX_HEADS) // 8
ATTN_Q_HEADS = (3 * N_MIX_HEADS) // 8
ATTN_KV_HEADS = ATTN_Q_HEADS // 3
ATTN_GROUP = ATTN_Q_HEADS // ATTN_KV_HEADS
SGU_GROUPS = N_MIX_HEADS - MLSTM_HEADS - ATTN_Q_HEADS
MLSTM_W = MLSTM_HEADS * HEAD_DIM
ATTN_W = ATTN_Q_HEADS * HEAD_DIM
ATTN_KV_W = ATTN_KV_HEADS * HEAD_DIM
SGU_W = SGU_GROUPS * HEAD_DIM
D_MIX = MLSTM_W + ATTN_W + SGU_W
MLSTM_CHUNK = 128
CONV_WIDTH = 4
WINDOW = 128
ROPE_DIM = HEAD_DIM // 4
ROPE_THETA = 500000.0
SGU_CHUNK = 128
D_FF = 256 * ((8 * D_MODEL // 3 + 255) // 256)
N_EXPERTS = 8
TOP_K = 2
D_FF_EXPERT = 7 * D_MODEL // 2
N_DENSE = (DEPTH + 1) // 2
N_MOE = DEPTH // 2
DN_ALPHA = (2.0 * DEPTH) ** 0.25
DN_BETA = (8.0 * DEPTH) ** -0.25
LN_EPS = 1e-5
PROJ_SIZES = (MLSTM_W, MLSTM_W, MLSTM_W, MLSTM_W, MLSTM_HEADS, MLSTM_HEADS,
              ATTN_W, ATTN_KV_W, ATTN_KV_W, SGU_W, SGU_W)
PROJ_SPLITS = tuple(int(s) for s in np.cumsum(PROJ_SIZES)[:-1])
D_PROJ = int(sum(PROJ_SIZES))

kernel_name = 'hybrid_mlstm_swa_gmlp_moe_block'


def layer_norm(x, g, b=None, eps=LN_EPS):
    xf = x.astype(jnp.float32)
    xc = xf - jnp.mean(xf, -1, keepdims=True)
    y = xc * lax.rsqrt(jnp.mean(xc * xc, -1, keepdims=True) + eps) * g.astype(jnp.float32)
    if b is not None:
        y = y + b.astype(jnp.float32)
    return y.astype(x.dtype)


def causal_conv(x, w):
    K = w.shape[0]
    S = x.shape[1]
    xp = jnp.pad(x, ((0, 0), (K - 1, 0), (0, 0)))
    return sum(xp[:, j:j + S, :] * w[j] for j in range(K))


def partial_rope(x, cos, sin):
    half = ROPE_DIM // 2
    x1, x2, xp = x[..., :half], x[..., half:ROPE_DIM], x[..., ROPE_DIM:]
    return jnp.concatenate([x1 * cos - x2 * sin, x2 * cos + x1 * sin, xp], -1)


def mlstm_chunkwise(q, k, v, log_i, log_f):
    B, H, S, d = q.shape
    L = MLSTM_CHUNK
    NC = S // L
    q = q.reshape(B, H, NC, L, d) * (d ** -0.5)
    k = k.reshape(B, H, NC, L, d)
    v = v.reshape(B, H, NC, L, d)
    li = log_i.reshape(B, H, NC, L)
    bcum = jnp.cumsum(log_f.reshape(B, H, NC, L), axis=-1)
    b_tot = bcum[..., -1]
    a = b_tot[..., None] - bcum + li
    m_loc = jnp.max(a, -1)
    w_loc = jnp.exp(a - m_loc[..., None])
    C_loc = jnp.einsum('bhcl,bhclv,bhclk->bhcvk', w_loc, v, k)
    n_loc = jnp.einsum('bhcl,bhclk->bhck', w_loc, k)

    def step(carry, inp):
        C, n, m = carry
        bt, Cl, nl, ml = inp
        m_new = jnp.maximum(bt + m, ml)
        s_old = jnp.exp(bt + m - m_new)
        s_loc = jnp.exp(ml - m_new)
        C_new = s_old[..., None, None] * C + s_loc[..., None, None] * Cl
        n_new = s_old[..., None] * n + s_loc[..., None] * nl
        return (C_new, n_new, m_new), (C, n, m)

    init = (jnp.zeros((B, H, d, d), jnp.float32), jnp.zeros((B, H, d), jnp.float32),
            jnp.zeros((B, H), jnp.float32))
    xs = (jnp.moveaxis(b_tot, 2, 0), jnp.moveaxis(C_loc, 2, 0),
          jnp.moveaxis(n_loc, 2, 0), jnp.moveaxis(m_loc, 2, 0))
    _, (C_prev, n_prev, m_prev) = lax.scan(step, init, xs)
    C_prev = jnp.moveaxis(C_prev, 0, 2)
    n_prev = jnp.moveaxis(n_prev, 0, 2)
    m_prev = jnp.moveaxis(m_prev, 0, 2)
    D = bcum[..., :, None] - bcum[..., None, :] + li[..., None, :]
    D = jnp.where(jnp.tril(jnp.ones((L, L), bool)), D, -jnp.inf)
    m_inter = bcum + m_prev[..., None]
    m = jnp.maximum(m_inter, jnp.max(D, -1))
    Sm = jnp.einsum('bhcld,bhcsd->bhcls', q, k) * jnp.exp(D - m[..., None])
    w_inter = jnp.exp(m_inter - m)
    num = (jnp.einsum('bhcls,bhcsd->bhcld', Sm, v)
           + w_inter[..., None] * jnp.einsum('bhcvk,bhclk->bhclv', C_prev, q))
    den = jnp.sum(Sm, -1) + w_inter * jnp.einsum('bhck,bhclk->bhcl', n_prev, q)
    h = num / jnp.maximum(jnp.abs(den), jnp.exp(-m))[..., None]
    return h.reshape(B, H, S, d)


def swa_with_sinks(q, k, v, sinks):
    B, S, _, d = q.shape
    Lb = WINDOW
    NB = S // Lb
    q = q.reshape(B, NB, Lb, ATTN_KV_HEADS, ATTN_GROUP, d)
    k = k.reshape(B, NB, Lb, ATTN_KV_HEADS, d)
    v = v.reshape(B, NB, Lb, ATTN_KV_HEADS, d)
    prev = lambda t: jnp.concatenate([jnp.zeros_like(t[:, :1]), t[:, :-1]], axis=1)
    kk = jnp.concatenate([prev(k), k], axis=2)
    vv = jnp.concatenate([prev(v), v], axis=2)
    s = jnp.einsum('bnqhgd,bnkhd->bnhgqk', q, kk) * (d ** -0.5)
    blk = jnp.arange(NB)[:, None] * Lb
    q_pos = blk + jnp.arange(Lb)[None, :]
    k_pos = blk - Lb + jnp.arange(2 * Lb)[None, :]
    rel = q_pos[:, :, None] - k_pos[:, None, :]
    mask = (rel >= 0) & (rel < WINDOW) & (k_pos[:, None, :] >= 0)
    s = jnp.where(mask[None, :, None, None], s, -jnp.inf)
    sink = sinks.astype(jnp.float32).reshape(1, 1, ATTN_KV_HEADS, ATTN_GROUP, 1, 1)
    mx = jnp.maximum(jnp.max(s, -1, keepdims=True), sink)
    p = jnp.exp(s - mx)
    denom = jnp.sum(p, -1, keepdims=True) + jnp.exp(sink - mx)
    o = jnp.einsum('bnhgqk,bnkhd->bnqhgd', p / denom, vv)
    return o.reshape(B, S, ATTN_W)


def spatial_gating(u, v, w_s, b_s, g, bn):
    B, S, _ = u.shape
    NCk = S // SGU_CHUNK
    v = layer_norm(v.reshape(B, S, SGU_GROUPS, HEAD_DIM),
                   g.reshape(SGU_GROUPS, HEAD_DIM), bn.reshape(SGU_GROUPS, HEAD_DIM))
    v = v.reshape(B, NCk, SGU_CHUNK, SGU_GROUPS, HEAD_DIM)
    w = jnp.where(jnp.tril(jnp.ones((SGU_CHUNK, SGU_CHUNK), bool)), w_s, 0.0)
    mixed = jnp.einsum('gts,bcsgd->bctgd', w, v) + b_s.T[None, None, :, :, None]
    return u * mixed.reshape(B, S, SGU_W).astype(u.dtype)


def hybrid_mixer(x, cos, sin, w_in, b_in, conv_w, mlstm_g, sinks, w_s, b_s, sgu_g, sgu_b, w_out):
    B, S, _ = x.shape
    f32 = jnp.float32
    proj = x @ w_in + b_in
    mq, mk, mv, mo, mi, mf, aq, ak, av, su, sv = jnp.split(proj, PROJ_SPLITS, axis=-1)
    qk = jax.nn.silu(causal_conv(jnp.concatenate([mq, mk], -1), conv_w))
    mq, mk = jnp.split(qk, 2, axis=-1)
    bhsd = lambda t: t.reshape(B, S, MLSTM_HEADS, HEAD_DIM).transpose(0, 2, 1, 3).astype(f32)
    h = mlstm_chunkwise(bhsd(mq), bhsd(mk), bhsd(mv),
                        mi.astype(f32).transpose(0, 2, 1),
                        jax.nn.log_sigmoid(mf.astype(f32)).transpose(0, 2, 1))
    h = layer_norm(h.transpose(0, 2, 1, 3), mlstm_g.reshape(MLSTM_HEADS, HEAD_DIM))
    h_a = (h.reshape(B, S, MLSTM_W) * jax.nn.sigmoid(mo.astype(f32))).astype(x.dtype)
    q = partial_rope(aq.reshape(B, S, ATTN_Q_HEADS, HEAD_DIM).astype(f32), cos, sin)
    k = partial_rope(ak.reshape(B, S, ATTN_KV_HEADS, HEAD_DIM).astype(f32), cos, sin)
    vv = av.reshape(B, S, ATTN_KV_HEADS, HEAD_DIM).astype(f32)
    h_b = swa_with_sinks(q, k, vv, sinks).astype(x.dtype)
    h_c = spatial_gating(jax.nn.gelu(su), jax.nn.gelu(sv), w_s, b_s, sgu_g, sgu_b).astype(x.dtype)
    return jnp.concatenate([h_a, h_b, h_c], -1) @ w_out


def swiglu(x, wg, wu, wd):
    return (jax.nn.silu(x @ wg) * (x @ wu)) @ wd


def moe_swiglu(x, w_router, b_router, wg, wu, wd):
    B, S, D = x.shape
    xt = x.reshape(B * S, D)
    logits = (xt @ w_router).astype(jnp.float32) + b_router.astype(jnp.float32)
    top_v, top_i = lax.top_k(logits, TOP_K)
    gates = jax.nn.softmax(top_v, axis=-1)
    combine = jnp.einsum('nk,nke->ne', gates,
                         jax.nn.one_hot(top_i, N_EXPERTS, dtype=jnp.float32)).astype(x.dtype)
    y = jnp.zeros_like(xt)
    for e in range(N_EXPERTS):
        y = y + combine[:, e:e + 1] * swiglu(xt, wg[e], wu[e], wd[e])
    return y.reshape(B, S, D)


def setup_inputs(seed: int = 0) -> dict:
    key = jax.random.key(seed)
    ks = jax.random.split(key, 32)
    nrm = lambda k, shape: jax.random.normal(k, shape, jnp.float32)
    x = nrm(ks[0], (BATCH, SEQ, D_MODEL))
    offsets = jax.random.randint(ks[1], (BATCH, 1), 0, 4096, dtype=jnp.int32)
    positions = offsets + jnp.arange(SEQ, dtype=jnp.int32)[None, :]
    w_in = nrm(ks[2], (DEPTH, D_MODEL, D_PROJ)) * D_MODEL ** -0.5
    f0 = PROJ_SPLITS[4]
    b_in = (0.02 * nrm(ks[3], (DEPTH, D_PROJ))).at[:, f0:f0 + MLSTM_HEADS].add(
        jnp.linspace(3.0, 6.0, MLSTM_HEADS, dtype=jnp.float32))
    conv_w = nrm(ks[4], (DEPTH, CONV_WIDTH, 2 * MLSTM_W)) * CONV_WIDTH ** -0.5
    mlstm_norm_g = 1.0 + 0.1 * nrm(ks[5], (DEPTH, MLSTM_W))
    attn_sinks = 0.5 * nrm(ks[6], (DEPTH, ATTN_Q_HEADS))
    sgu_w_s = nrm(ks[7], (DEPTH, SGU_GROUPS, SGU_CHUNK, SGU_CHUNK)) * SGU_CHUNK ** -0.5
    sgu_b_s = 1.0 + 0.1 * nrm(ks[8], (DEPTH, SGU_GROUPS, SGU_CHUNK))
    sgu_norm_g = 1.0 + 0.1 * nrm(ks[9], (DEPTH, SGU_W))
    sgu_norm_b = 0.02 * nrm(ks[10], (DEPTH, SGU_W))
    w_out = nrm(ks[11], (DEPTH, D_MIX, D_MODEL)) * D_MIX ** -0.5 * DN_BETA
    ln1_g = 1.0 + 0.1 * nrm(ks[12], (DEPTH, D_MODEL))
    ln1_b = 0.02 * nrm(ks[13], (DEPTH, D_MODEL))
    ln2_g = 1.0 + 0.1 * nrm(ks[14], (DEPTH, D_MODEL))
    ln2_b = 0.02 * nrm(ks[15], (DEPTH, D_MODEL))
    ffn_w_gate = nrm(ks[16], (N_DENSE, D_MODEL, D_FF)) * D_MODEL ** -0.5 * DN_BETA
    ffn_w_up = nrm(ks[17], (N_DENSE, D_MODEL, D_FF)) * D_MODEL ** -0.5 * DN_BETA
    ffn_w_down = nrm(ks[18], (N_DENSE, D_FF, D_MODEL)) * D_FF ** -0.5 * DN_BETA
    moe_w_router = nrm(ks[19], (N_MOE, D_MODEL, N_EXPERTS)) * D_MODEL ** -0.5
    moe_b_router = 0.01 * nrm(ks[20], (N_MOE, N_EXPERTS))
    moe_w_gate = nrm(ks[21], (N_MOE, N_EXPERTS, D_MODEL, D_FF_EXPERT)) * D_MODEL ** -0.5 * DN_BETA
    moe_w_up = nrm(ks[22], (N_MOE, N_EXPERTS, D_MODEL, D_FF_EXPERT)) * D_MODEL ** -0.5 * DN_BETA
    moe_w_down = nrm(ks[23], (N_MOE, N_EXPERTS, D_FF_EXPERT, D_MODEL)) * D_FF_EXPERT ** -0.5 * DN_BETA
    return {'x': x, 'positions': positions, 'w_in': w_in, 'b_in': b_in, 'conv_w': conv_w,
            'mlstm_norm_g': mlstm_norm_g, 'attn_sinks': attn_sinks, 'sgu_w_s': sgu_w_s,
            'sgu_b_s': sgu_b_s, 'sgu_norm_g': sgu_norm_g, 'sgu_norm_b': sgu_norm_b,
            'w_out': w_out, 'ln1_g': ln1_g, 'ln1_b': ln1_b, 'ln2_g': ln2_g, 'ln2_b': ln2_b,
            'ffn_w_gate': ffn_w_gate, 'ffn_w_up': ffn_w_up, 'ffn_w_down': ffn_w_down,
            'moe_w_router': moe_w_router, 'moe_b_router': moe_b_router,
            'moe_w_gate': moe_w_gate, 'moe_w_up': moe_w_up, 'moe_w_down': moe_w_down}


def reference(x, positions, w_in, b_in, conv_w, mlstm_norm_g, attn_sinks, sgu_w_s, sgu_b_s,
              sgu_norm_g, sgu_norm_b, w_out, ln1_g, ln1_b, ln2_g, ln2_b,
              ffn_w_gate, ffn_w_up, ffn_w_down, moe_w_router, moe_b_router,
              moe_w_gate, moe_w_up, moe_w_down):
    inv_freq = ROPE_THETA ** (-jnp.arange(0, ROPE_DIM, 2, dtype=jnp.float32) / ROPE_DIM)
    ang = positions.astype(jnp.float32)[..., None] * inv_freq
    cos = jnp.cos(ang)[:, :, None, :]
    sin = jnp.sin(ang)[:, :, None, :]
    for layer in range(DEPTH):
        mix = hybrid_mixer(x, cos, sin, w_in[layer], b_in[layer], conv_w[layer],
                           mlstm_norm_g[layer], attn_sinks[layer], sgu_w_s[layer],
                           sgu_b_s[layer], sgu_norm_g[layer], sgu_norm_b[layer], w_out[layer])
        x = layer_norm(DN_ALPHA * x + mix, ln1_g[layer], ln1_b[layer])
        j = layer // 2
        if layer % 2 == 0:
            ff = swiglu(x, ffn_w_gate[j], ffn_w_up[j], ffn_w_down[j])
        else:
            ff = moe_swiglu(x, moe_w_router[j], moe_b_router[j], moe_w_gate[j],
                            moe_w_up[j], moe_w_down[j])
        x = layer_norm(DN_ALPHA * x + ff, ln2_g[layer], ln2_b[layer])
    return x
```

```python
import math
from contextlib import ExitStack

import numpy as np
import concourse.bass as bass
import concourse.mybir as mybir
from concourse.bass_utils import run_bass_kernel_spmd

F32 = mybir.dt.float32
BF16 = mybir.dt.bfloat16
I32 = mybir.dt.int32
AF = mybir.ActivationFunctionType
ALU = mybir.AluOpType
AX = mybir.AxisListType

NCORES = 8
D = 1024
S = 2048
NCH = 16
DEPTH = 2
DPROJ = 2700
C_MQ, C_MK, C_MV, C_MO, C_MI, C_MF, C_AQ, C_AK, C_AV, C_SU, C_SV = (
    0, 384, 768, 1152, 1536, 1542, 1548, 1932, 2060, 2188, 2444)
TOKC0 = 768
NTOKC = DPROJ - TOKC0
DFF = 2816
NE = 8
DFE = 3584
ALPHA = (2.0 * DEPTH) ** 0.25
EPS = 1e-5
ROPE_THETA = 500000.0
NEG = -30000.0
NPP = 34
NSB = 922
FG = 512


class Tracker:
    def __init__(self, nc, es):
        self.nc = nc
        self.es = es
        self.eng = {"pe": nc.tensor, "dve": nc.vector, "act": nc.scalar, "pool": nc.gpsimd, "sp": nc.sync}
        self.sem = {}
        self.cnt = {}
        self.waited = {k: {} for k in self.eng}
        for k in ("pe", "dve", "act", "pool"):
            self.sem[k] = es.enter_context(nc.semaphore("sem_" + k))
            self.cnt[k] = 0
        self.res = {}
        self.nins = 0
        self.mangle_names = frozenset()
        self.suffix = "0"

    def _m(self, names):
        mn = self.mangle_names
        sfx = self.suffix
        return [n + sfx if n in mn else n for n in names]

    @staticmethod
    def _excl(reads, writes):
        ps = [r for r in reads if r.startswith("ps")]
        if ps:
            reads = [r for r in reads if not r.startswith("ps")]
            writes = list(writes) + ps
        return reads, writes

    def _deps(self, e, reads, writes):
        need = {}

        def add(tok, skip_same):
            if tok is None:
                return
            k, v = tok
            if skip_same and k == e:
                return
            if need.get(k, 0) < v:
                need[k] = v

        for r in reads:
            st = self.res.get(r)
            if st is not None:
                add(st[0], False)
        for w in writes:
            st = self.res.get(w)
            if st is not None:
                add(st[0], True)
                for k, v in st[1].items():
                    add((k, v), True)
        return need

    def _wait(self, e, need):
        wd = self.waited[e]
        for k, v in need.items():
            if wd.get(k, 0) < v:
                self.eng[e].wait_ge(self.sem[k], v)
                wd[k] = v

    def _commit(self, tok, reads, writes):
        for r in reads:
            st = self.res.get(r)
            if st is None:
                st = [None, {}]
                self.res[r] = st
            if st[1].get(tok[0], 0) < tok[1]:
                st[1][tok[0]] = tok[1]
        for w in writes:
            self.res[w] = [tok, {}]

    def op(self, e, fn, reads=(), writes=()):
        reads, writes = self._excl(self._m(reads), self._m(writes))
        self._wait(e, self._deps(e, reads, writes))
        ins = fn(self.eng[e])
        self.cnt[e] += 1
        ins.then_inc(self.sem[e], 1)
        self.nins += 1
        self._commit((e, self.cnt[e]), reads, writes)

    def group(self, e, fns, reads=(), writes=()):
        reads, writes = self._excl(self._m(reads), self._m(writes))
        self._wait(e, self._deps(e, reads, writes))
        ins = None
        for fn in fns:
            ins = fn(self.eng[e])
            self.nins += 1
        self.cnt[e] += 1
        ins.then_inc(self.sem[e], 1)
        self._commit((e, self.cnt[e]), reads, writes)

    def dma(self, q, pairs, reads, writes, key):
        if key not in self.sem:
            self.sem[key] = self.es.enter_context(self.nc.semaphore("sem_" + key))
            self.cnt[key] = 0
        reads, writes = self._m(reads), self._m(writes)
        self._wait(q, self._deps(q, reads, writes))
        for (o, i) in pairs:
            self.eng[q].dma_start(out=o, in_=i).then_inc(self.sem[key], 16)
            self.cnt[key] += 16
            self.nins += 1
        self._commit((key, self.cnt[key]), reads, writes)

    def wait_all(self, e, keys):
        for k in keys:
            if k in self.sem and self.waited[e].get(k, 0) < self.cnt[k]:
                self.eng[e].wait_ge(self.sem[k], self.cnt[k])
                self.waited[e][k] = self.cnt[k]


def build(stage="full"):
    if not hasattr(build, 'debug'):
        build.debug = False
    if not hasattr(build, 'ratio'):
        build.ratio = 30
    if not hasattr(build, 'hold'):
        build.hold = 50
    if not hasattr(build, 'ratio_b'):
        build.ratio_b = 3
    if not hasattr(build, 'dens'):
        build.dens = 2
    nc = bass.Bass("TRN2", target_bir_lowering=False)

    def din(name, shape, dt=F32):
        return nc.dram_tensor(name, shape, dt, kind="ExternalInput").ap()

    x_d = din("x", [S, D])
    pos_d = din("posT", [128, NCH], I32)
    w_in_d = din("w_in", [DEPTH, D, DPROJ])
    pp_d = din("pp", [DEPTH, 128, NPP])
    sbr_d = din("sbr", [DEPTH, 1, NSB])
    lnp_d = din("lnp", [DEPTH, 2, 1, 2 * D])
    btok_d = din("btok", [DEPTH, 1, NTOKC])
    ws_d = din("sgu_w_s", [DEPTH, 4, 128, 128])
    w_out_d = din("w_out", [DEPTH, D, D])
    fg_d = din("ffn_w_gate", [1, D, DFF])
    fu_d = din("ffn_w_up", [1, D, DFF])
    fd_d = din("ffn_w_down", [1, DFF, D])
    wr_d = din("moe_w_router", [1, D, NE])
    mg_d = din("moe_w_gate", [1, NE, D, DFE])
    mu_d = din("moe_w_up", [1, NE, D, DFE])
    md_d = din("moe_w_down", [1, NE, DFE, D])
    y_d = nc.dram_tensor("y", [S, D], F32, kind="ExternalOutput").ap()
    DBG = build.debug
    if DBG:
        dbg_d = nc.dram_tensor("dbg", [128, 16384], F32, kind="ExternalOutput").ap()
    build.dbg_slots = {}
    dbg_state = {"off": 0}

    with ExitStack() as es:
        T = Tracker(nc, es)
        T.mangle_names = frozenset(["gog", "vp", "u_sb", "gv", "aqkb_a", "aqkb_b", "aqkb_c", "st", "st_d", "st_r", "st_q",
                                    "st_s", "st_m", "st_n", "st_k", "st_l", "st_l2"])

        def sb(name, shape, dt=F32):
            return es.enter_context(nc.sbuf_tensor(name, shape, dt))

        x_tok = sb("x_tok", [128, NCH, D])
        xT = sb("xT", [128, 8, S], BF16)
        arena = sb("arena", [128, 29792], BF16)
        qkT = sb("qkT", [128, 6, 512], BF16)
        lnp = sb("lnp_sb", [128, 2 * D])
        w_in_sb = arena[:, 0:8 * DPROJ].rearrange("p (k n) -> p k n", k=8)
        w_out_sb = arena[:, 8 * DPROJ:8 * DPROJ + 8 * D].rearrange("p (k n) -> p k n", k=8)

        ident = sb("ident", [128, 128], BF16)
        U_f = sb("U_f", [128, 128])
        ones_f = sb("ones_f", [128, 128])
        mask3 = sb("mask3", [128, 384], BF16)
        cs = sb("cs", [128, 2, NCH, 8])
        ppt = sb("ppt", [128, NPP])
        sbt = sb("sbt", [128, NSB])
        brow = sb("brow", [65, 780], BF16)
        ones33 = sb("ones33", [65, 128], BF16)
        wsT = sb("wsT", [128, 4, 128], BF16)
        wr_sb = sb("wr_sb", [128, 8, NE], BF16)
        pconv = sb("pconv", [128, 515])
        yconv = sb("yconv", [128, 512])
        carry = sb("carry", [128, 6, 3])
        gogs = [sb(f"gog{i}", [128, 384], BF16) for i in range(2)]
        sts = [sb(f"st{i}", [128, 96]) for i in range(2)]
        gog, st = gogs[0], sts[0]
        bns = sb("bns", [128, 12, 6])
        bna = sb("bna", [128, 12, 2])
        vps = [sb(f"vp{i}", [128, 6, 65], BF16) for i in range(2)]
        vp = vps[0]
        ktok = sb("ktok", [128, 384], BF16)
        smT = sb("smT", [128, 6, 128], BF16)
        hm = sb("hm", [128, 384])
        Sf = sb("Sf", [128, 3, 65])
        CT8 = sb("CT8", [128, 3, 65], BF16)
        tr4 = sb("tr4", [128, 4, 8, 8])
        aqkbs = [sb(f"aqkb{i}", [128, 512], BF16) for i in range(2)]
        aqkb = aqkbs[0]
        qTat = sb("qTat", [64, 6, 128], BF16)
        kTat = sb("kTat", [64, 2, 256], BF16)
        vat = sb("vat", [128, 3, 128], BF16)
        wbuf = sb("wbuf", [128, 1536], BF16)
        pexp = wbuf[:, 0:768].rearrange("p (h k) -> p h k", h=3)
        pT = wbuf[:, 768:1536].rearrange("p (h b q) -> p h b q", h=3, b=2)
        hcatT = wbuf[:, 0:1024].rearrange("p (k t) -> p k t", k=8)
        u_sbs = [sb(f"u_sb{i}", [128, 256], BF16) for i in range(2)]
        gvs = [sb(f"gv{i}", [128, 256]) for i in range(2)]
        u_sb, gv = u_sbs[0], gvs[0]
        vn = sb("vn", [128, 256], BF16)
        hcat = sb("hcat", [128, D], BF16)

        PB = [es.enter_context(nc.psum_tensor(f"pb{i}", [128, 512], F32)) for i in range(4)]
        PO = [es.enter_context(nc.psum_tensor(f"po{i}", [128, 1024], F32)) for i in range(2)]
        RB = ["ps0", "ps1", "ps2", "ps3"]
        RO = [("ps4", "ps5"), ("ps6", "ps7")]

        def pb_bf(i):
            return PB[i][:, :].bitcast(BF16)

        def dbgdump(name, ap2d, reads):
            if not DBG or name in build.dbg_slots:
                return
            P, n = ap2d.shape[0], ap2d.shape[1]
            o = dbg_state["off"]
            build.dbg_slots[name] = (o, P, n)
            dbg_state["off"] = o + n
            T.dma("pool", [(dbg_d[0:P, o:o + n], ap2d)], reads, [], "d_dbg")

        def act(out, in_, func, reads, writes, **kw):
            T.op("act", lambda e: e.activation(out=out, in_=in_, func=func, **kw), reads, writes)

        def dve(fn, reads, writes):
            T.op("dve", fn, reads, writes)

        def pool(fn, reads, writes):
            T.op("pool", fn, reads, writes)

        def mm_group(mms, reads, writes):
            n = len(mms)
            fns = []
            for i, (o, l, r) in enumerate(mms):
                fns.append(lambda e, o=o, l=l, r=r, i=i: e.matmul(o, lhsT=l, rhs=r, start=(i == 0), stop=(i == n - 1)))
            T.group("pe", fns, reads, writes)

        with nc.allow_non_contiguous_dma(reason="small parameter loads"):
            posi = sb("posi", [128, NCH], I32)
            T.dma("sp", [(posi[:, :], pos_d)], [], ["posi"], "d_pos")

        pool(lambda e: e.memset(ones_f[:, :], 1.0), [], ["ones_f"])
        pool(lambda e: e.affine_select(out=U_f[:, :], in_=ones_f[:, :], pattern=[[1, 128]], compare_op=ALU.is_ge,
                                       fill=0.0, base=0, channel_multiplier=-1), ["ones_f"], ["U_f"])
        idf = hm[:, 0:128]
        pool(lambda e: e.affine_select(out=idf, in_=ones_f[:, :], pattern=[[1, 128]], compare_op=ALU.is_equal,
                                       fill=0.0, base=0, channel_multiplier=-1), ["ones_f"], ["hm"])
        pool(lambda e: e.tensor_copy(out=ident[:, :], in_=idf), ["hm"], ["ident"])
        mtmp = hm[:, 128:384]
        pool(lambda e: e.memset(mtmp, 0.0), [], ["hm"])
        curm = gv[:, 0:128]
        prevm = gv[:, 128:256]
        pool(lambda e: e.affine_select(out=curm, in_=mtmp[:, 0:128], pattern=[[-1, 128]], compare_op=ALU.is_ge,
                                       fill=NEG, base=0, channel_multiplier=1), ["hm"], ["gv"])
        pool(lambda e: e.affine_select(out=prevm, in_=mtmp[:, 0:128], pattern=[[1, 128]], compare_op=ALU.is_gt,
                                       fill=NEG, base=0, channel_multiplier=-1), ["hm"], ["gv"])
        pool(lambda e: e.tensor_copy(out=mask3[:, 0:128], in_=curm), ["gv"], ["mask3a"])
        pool(lambda e: e.tensor_copy(out=mask3[:, 128:256], in_=prevm), ["gv"], ["mask3b"])
        pool(lambda e: e.tensor_copy(out=mask3[:, 256:384], in_=curm), ["gv"], ["mask3c"])
        MASK3 = ["mask3a", "mask3b", "mask3c"]
        pool(lambda e: e.memset(ones33[:, :], 1.0), [], ["ones33"])
        pool(lambda e: e.memset(carry[:, :, :], 0.0), [], ["carry"])

        posf = st[:, 0:16]
        dve(lambda e: e.tensor_copy(out=posf, in_=posi[:, :]), ["posi"], ["st"])
        invf = st[:, 16:24]
        for j in range(8):
            v = float(np.float32(ROPE_THETA) ** np.float32(-(2.0 * j) / 16.0))
            pool(lambda e, j=j, v=v: e.memset(st[:, 16 + j:17 + j], v), [], [f"invf{j}"])
        INVF = [f"invf{j}" for j in range(8)]
        ang = hm[:, 0:128].rearrange("p (c j) -> p c j", j=8)
        dve(lambda e: e.tensor_tensor(out=ang, in0=posf.unsqueeze(2).to_broadcast([128, NCH, 8]),
                                      in1=invf.unsqueeze(1).to_broadcast([128, NCH, 8]), op=ALU.mult),
            ["st"] + INVF + ["ident"], ["hm"])
        C1 = 6.28125
        C2 = 2.0 * math.pi - 6.28125
        kint = gvs[1][:, 0:128].bitcast(I32)
        for which in range(2):
            shift = math.pi / 2 if which == 0 else 0.0
            a2 = hm[:, 128:256]
            kf = hm[:, 256:384]
            a1 = hm[:, 0:128]
            dve(lambda e: e.tensor_scalar(out=a2, in0=a1, scalar1=shift, scalar2=None, op0=ALU.add), ["hm"], ["hm"])
            dve(lambda e: e.tensor_scalar(out=kint[:, :], in0=a2, scalar1=1.0 / (2.0 * math.pi), scalar2=None, op0=ALU.mult),
                ["hm"], ["kint"])
            dve(lambda e: e.tensor_copy(out=kf, in_=kint[:, :]), ["kint"], ["hm"])
            dve(lambda e: e.scalar_tensor_tensor(out=a2, in0=kf, scalar=-C1, in1=a2, op0=ALU.mult, op1=ALU.add),
                ["hm"], ["hm"])
            dve(lambda e: e.scalar_tensor_tensor(out=a2, in0=kf, scalar=-C2, in1=a2, op0=ALU.mult, op1=ALU.add),
                ["hm"], ["hm"])
            dve(lambda e: e.tensor_scalar(out=a2, in0=a2, scalar1=-3.1415925, scalar2=3.1415925, op0=ALU.max, op1=ALU.min),
                ["hm"], ["hm"])
            act(cs[:, which, :, :].rearrange("p c j -> p (c j)"), a2, AF.Sin, ["hm"], ["cs"])

        BROW = {C_MO: (0, 0), C_MV: (0, 396), C_AQ: (32, 0), C_SV: (32, 512), C_AV: (64, 0)}
        TILE_END = {C_MO: C_AQ, C_MV: C_MO, C_AQ: C_AV, C_SV: DPROJ, C_AV: C_SV}

        def load_bias_rows(l):
            prs = []
            for c0, (P, o) in BROW.items():
                n = TILE_END[c0] - c0
                prs.append((brow[P:P + 1, o:o + n], btok_d[l][:, c0 - TOKC0:c0 - TOKC0 + n]))
            T.dma("pool", prs, [], ["brow"], "d_brow")
            pool(lambda e: e.memset(brow[0:1, 384:396], 0.0), ["brow"], ["brow"])

        HCATALL = ["hcat_a", "hcat_b0", "hcat_b1", "hcat_c"]
        ARENA_FFN = [f"w{x}{b}" for x in "gud" for b in range(2)] + [f"hT{i}_{j}" for i in range(2) for j in range(4)]

        def load_mixer_weights(l):
            wtmp = hcat[:, 0:512].rearrange("p (g s) -> p g s", g=4)
            T.dma("pool", [(wtmp, ws_d[l].rearrange("g t s -> t g s"))], [], HCATALL, "d_ws")
            srcw = w_in_d[l].rearrange("(k p) n -> p k n", p=128)
            T.dma("pool", [(w_in_sb[:, :, 0:TOKC0], srcw[:, :, 0:TOKC0])], [], ["w_in_qk"] + ARENA_FFN, "d_winqk")
            T.dma("pool", [(w_in_sb[:, k, TOKC0:DPROJ], srcw[:, k, TOKC0:DPROJ]) for k in range(8)], [], ["w_in_sb"] + ARENA_FFN, "d_win")
            srco = w_out_d[l].rearrange("(k p) n -> p k n", p=128)
            T.dma("pool", [(w_out_sb[:, 0:4, :], srco[:, 0:4, :]), (w_out_sb[:, 4:8, :], srco[:, 4:8, :])],
                  [], ["w_out_sb"] + ARENA_FFN, "d_wout")
            trb = pb_bf(2)
            T.group("pe", [lambda e, g=g: e.transpose(out=trb[:, g * 128:(g + 1) * 128], in_=wtmp[:, g, :], identity=ident[:, :])
                           for g in range(4)], HCATALL + ["ident"], ["ps2"])
            dve(lambda e: e.tensor_tensor(out=wsT[:, :, :], in0=trb[:, 0:512].rearrange("p (g t) -> p g t", g=4),
                                          in1=U_f[:, :].unsqueeze(1).to_broadcast([128, 4, 128]), op=ALU.mult),
                ["ps2", "U_f"], ["wsT"])

        def phaseA(l, j):
            for cc in range(6):
                bank = cc % 2
                mm_group([(PB[bank][:, :], w_in_sb[:, k, cc * 128:(cc + 1) * 128], xT[:, k, j * 512:(j + 1) * 512])
                          for k in range(8)], ["w_in_qk"] + [f"xT{4 * j + t}" for t in range(4)], [RB[bank]])
                yield
                pool(lambda e, cc=cc: e.tensor_copy(out=pconv[:, 0:3], in_=carry[:, cc, :]), ["carry"], ["pconv"])
                yield
                act(pconv[:, 3:515], PB[bank][:, :], AF.Identity, [RB[bank], "ppt"], ["pconv"],
                    bias=ppt[:, 24 + cc:25 + cc])
                yield
                pool(lambda e, cc=cc: e.tensor_copy(out=carry[:, cc, :], in_=pconv[:, 512:515]), ["pconv"], ["carry"])
                yield
                dve(lambda e, cc=cc: e.tensor_scalar(out=yconv[:, :], in0=pconv[:, 3:515], scalar1=ppt[:, cc * 4 + 3:cc * 4 + 4],
                                                      scalar2=None, op0=ALU.mult), ["pconv", "ppt"], ["yconv"])
                yield
                for jj in (2, 1, 0):
                    dve(lambda e, cc=cc, jj=jj: e.scalar_tensor_tensor(
                        out=yconv[:, :], in0=pconv[:, jj:jj + 512], scalar=ppt[:, cc * 4 + jj:cc * 4 + jj + 1],
                        in1=yconv[:, :], op0=ALU.mult, op1=ALU.add), ["pconv", "ppt", "yconv"], ["yconv"])
                    yield
                act(qkT[:, cc, :], yconv[:, :], AF.Silu, ["yconv"], ["qkT"])
                yield

        def phaseB(l, c, is_moe):
            tok0 = c * 128
            off = (c % 4) * 128
            slot = c % 2
            par = c % 2
            gog, st, vp, aqkb, u_sb, gv = gogs[par], sts[par], vps[par], aqkbs[par], u_sbs[par], gvs[par]
            v3c, v3p = c % 3, (c - 1) % 3
            WB = ["pexp0", "pexp1", "pexp2", "pT_a", "pT_b"]

            def proj_tile(bank, c0, c1):
                n = c1 - c0
                mms = [(PB[bank][:, 0:n], xT[:, k, tok0:tok0 + 128], w_in_sb[:, k, c0:c1]) for k in range(8)]
                P, o = BROW[c0]
                mms.append((PB[bank][:, 0:n], ones33[P:P + 1, :], brow[P:P + 1, o:o + n]))
                mm_group(mms, [f"xT{c}", "w_in_sb", "ones33", "brow"], [RB[bank]])

            proj_tile(0, C_MO, C_AQ)
            yield
            act(gog[:, :], PB[0][:, 0:384], AF.Tanh, ["ps0"], ["gog"], scale=0.5)
            yield
            dve(lambda e: e.tensor_tensor(out=st[:, 0:12], in0=PB[0][:, 384:396], in1=sbt[:, 910:922], op=ALU.add), ["ps0", "sbt"], ["st"])
            yield
            proj_tile(1, C_AV, C_SV)
            yield
            act(vat[:, v3c, :], PB[1][:, 0:128], AF.Copy, ["ps1"], [f"vat{v3c}"])
            yield
            act(u_sb[:, :], PB[1][:, 128:384], AF.Gelu_apprx_tanh, ["ps1"], ["u_sb"])
            yield
            proj_tile(0, C_SV, DPROJ)
            yield
            act(gv[:, :], PB[0][:, 0:256], AF.Gelu_apprx_tanh, ["ps0"], ["gv"])
            yield
            proj_tile(1, C_MV, C_MO)
            yield
            act(st[:, 12:18], st[:, 6:12], AF.Exp, ["st"], ["st"], scale=-1.0)
            yield
            act(st[:, 12:18], st[:, 12:18], AF.Ln, ["st"], ["st"], bias=1.0)
            yield
            T.group("pe", [lambda e: e.matmul(PB[0][:, 0:6], lhsT=U_f[:, :], rhs=st[:, 12:18], start=True, stop=True),
                           lambda e: e.matmul(PB[0][:, 8:14], lhsT=ones_f[:, :], rhs=st[:, 12:18], start=True, stop=True)],
                    ["U_f", "ones_f", "st"], ["ps0"])
            yield
            dve(lambda e: e.tensor_add(out=st[:, 18:24], in0=st[:, 0:6], in1=PB[0][:, 0:6]), ["st", "ps0"], ["st"])
            yield
            act(st[:, 24:30], st[:, 18:24], AF.Exp, ["st"], ["st"])
            yield
            act(st[:, 30:36], PB[0][:, 0:6], AF.Exp, ["ps0"], ["st"])
            yield
            ntot = PB[0][:, 8:14].rearrange("p (i two) -> p i two", two=2)
            act(st[0:64, 36:39], ntot[0:64, :, 0], AF.Exp, ["ps0"], ["st"], scale=-1.0)
            yield
            act(st[64:128, 36:39], ntot[64:128, :, 1], AF.Exp, ["ps0"], ["st"], scale=-1.0)
            yield
            act(st[0:64, 39:42], ntot[0:64, :, 0], AF.Exp, ["ps0"], ["st"], scale=-1.0, bias=math.log(0.125))
            yield
            act(st[64:128, 39:42], ntot[64:128, :, 1], AF.Exp, ["ps0"], ["st"], scale=-1.0, bias=math.log(0.125))
            yield
            dve(lambda e: e.scalar_tensor_tensor(out=gog[:, :], in0=gog[:, :], scalar=1.0, in1=sbt[:, 0:384],
                                                 op0=ALU.add, op1=ALU.mult), ["gog", "sbt"], ["gog"])
            yield
            dve(lambda e: e.tensor_tensor(out=vp[:, :, 0:64], in0=PB[1][:, 0:384].rearrange("p (h d) -> p h d", h=6),
                                          in1=st[:, 24:30].unsqueeze(2).to_broadcast([128, 6, 64]), op=ALU.mult),
                ["ps1", "st"], ["vp"])
            yield
            pool(lambda e: e.tensor_copy(out=vp[:, :, 64], in_=st[:, 24:30]), ["st"], ["vp"])
            yield
            proj_tile(0, C_AQ, C_AV)
            yield
            t3 = PB[0][:, :].rearrange("p (h d) -> p h d", h=8)
            cosb = cs[:, 0, c, :].unsqueeze(1).to_broadcast([128, 8, 8])
            sinb = cs[:, 1, c, :].unsqueeze(1).to_broadcast([128, 8, 8])
            dve(lambda e: e.tensor_tensor(out=tr4[:, 0], in0=t3[:, :, 0:8], in1=cosb, op=ALU.mult), ["ps0", "cs"], ["tr4a"])
            yield
            dve(lambda e: e.tensor_tensor(out=tr4[:, 1], in0=t3[:, :, 8:16], in1=sinb, op=ALU.mult), ["ps0", "cs"], ["tr4b"])
            yield
            dve(lambda e: e.tensor_tensor(out=tr4[:, 2], in0=t3[:, :, 8:16], in1=cosb, op=ALU.mult), ["ps0", "cs"], ["tr4c"])
            yield
            dve(lambda e: e.tensor_tensor(out=tr4[:, 3], in0=t3[:, :, 0:8], in1=sinb, op=ALU.mult), ["ps0", "cs"], ["tr4d"])
            yield
            aq3 = aqkb[:, :].rearrange("p (h d) -> p h d", h=8)
            act(aq3[:, :, 16:64], t3[:, :, 16:64], AF.Copy, ["ps0"], ["aqkb_c"])
            yield
            pool(lambda e: e.tensor_sub(out=aq3[:, :, 0:8], in0=tr4[:, 0], in1=tr4[:, 1]), ["tr4a", "tr4b"], ["aqkb_a"])
            yield
            pool(lambda e: e.tensor_add(out=aq3[:, :, 8:16], in0=tr4[:, 2], in1=tr4[:, 3]), ["tr4c", "tr4d"], ["aqkb_b"])
            yield
            AQKB = ["aqkb_a", "aqkb_b", "aqkb_c"]
            if c == 0 and l == 0 and DBG:
                dbgdump('g12', st[:, 0:12], ["st"])
                dbgdump('st2', st[:, 0:48], ["st"])
                dbgdump('vp', vp[:, :, :].rearrange("p h v -> p (h v)"), ["vp"])
                dbgdump('gog', gog[:, :], ["gog"])
                dbgdump('aqkb', aqkb[:, :], AQKB)
                dbgdump('u_sb', u_sb[:, :], ["u_sb"])
                dbgdump('gvg', gv[:, :], ["gv"])
            yield "B1"
            nk = 1 if c == 0 else 2

            def strandM():
                trb = pb_bf(2)
                T.group("pe", [lambda e, i=i: e.transpose(out=trb[:, i * 128:(i + 1) * 128], in_=qkT[:, 3 + i, off:off + 128],
                                                          identity=ident[:, :]) for i in range(3)],
                        ["qkT", "ident"], ["ps2"])
                yield
                act(ktok[:, :], trb[:, 0:384], AF.Copy, ["ps2"], ["ktok"])
                yield
                sc = PO[0]
                fns = []
                for h in range(6):
                    hp, i = h % 2, h // 2
                    fns.append(lambda e, h=h, hp=hp, i=i: e.matmul(
                        sc[:, hp * 512 + i * 128:hp * 512 + (i + 1) * 128], lhsT=qkT[hp * 64:(hp + 1) * 64, 3 + i, off:off + 128],
                        rhs=qkT[hp * 64:(hp + 1) * 64, i, off:off + 128], start=True, stop=True))
                T.group("pe", fns, ["qkT"], ["ps4", "ps5"])
                yield
                smT4 = smT[:, :, :].rearrange("p (i two) l -> p i two l", two=2)
                dve(lambda e: e.scalar_tensor_tensor(out=smT4[:, :, 0, :], in0=sc[:, 0:384].rearrange("p (h l) -> p h l", h=3),
                                                     scalar=0.125, in1=U_f[:, :].unsqueeze(1).to_broadcast([128, 3, 128]),
                                                     op0=ALU.mult, op1=ALU.mult), ["ps4", "U_f"], ["smT_a"])
                yield
                dve(lambda e: e.scalar_tensor_tensor(out=smT4[:, :, 1, :], in0=sc[:, 512:896].rearrange("p (h l) -> p h l", h=3),
                                                     scalar=0.125, in1=U_f[:, :].unsqueeze(1).to_broadcast([128, 3, 128]),
                                                     op0=ALU.mult, op1=ALU.mult), ["ps5", "U_f"], ["smT_b"])
                yield
                num = PB[3][:, 16:406].rearrange("p (h v) -> p h v", h=6)
                fns = []
                for h in range(6):
                    hp, i = h % 2, h // 2
                    last = (c == 0)
                    fns.append(lambda e, h=h, last=last: e.matmul(num[:, h, :], lhsT=smT[:, h, :], rhs=vp[:, h, :], start=True, stop=last))
                    if c > 0:
                        fns.append(lambda e, h=h, hp=hp, i=i: e.matmul(
                            num[:, h, :], lhsT=qkT[hp * 64:(hp + 1) * 64, i, off:off + 128],
                            rhs=CT8[hp * 64:(hp + 1) * 64, i, :], start=False, stop=True))
                T.group("pe", fns, ["smT_a", "smT_b", "vp", "qkT", "CT8"], ["ps3"])
                yield
                dcb = PB[2][:, 0:390].rearrange("p (i v) -> p i v", i=3)
                if c < NCH - 1:
                    T.group("pe", [lambda e, i=i: e.matmul(dcb[:, i, :], lhsT=ktok[:, i * 128:(i + 1) * 128],
                                                           rhs=vp[:, 2 * i:2 * i + 2, :].rearrange("p h v -> p (h v)"),
                                                           start=True, stop=True) for i in range(3)],
                            ["ktok", "vp"], ["ps2"])
                    yield
                act(st[:, 42:48], num[:, :, 64], AF.Copy, ["ps3"], ["st_d"])
                yield
                dve(lambda e: e.scalar_tensor_tensor(out=st[:, 48:54], in0=st[:, 42:48], scalar=-1.0, in1=st[:, 42:48],
                                                     op0=ALU.mult, op1=ALU.max), ["st_d"], ["st_r"])
                yield
                dve(lambda e: e.tensor_tensor(out=st[:, 48:54], in0=st[:, 48:54], in1=st[:, 30:36], op=ALU.max), ["st_r", "st"], ["st_r"])
                yield
                dve(lambda e: e.reciprocal(out=st[:, 48:54], in_=st[:, 48:54]), ["st_r"], ["st_r"])
                yield
                hm3 = hm[:, :].rearrange("p (h d) -> p h d", h=6)
                dve(lambda e: e.tensor_tensor(out=hm3, in0=num[:, :, 0:64], in1=st[:, 48:54].unsqueeze(2).to_broadcast([128, 6, 64]),
                                              op=ALU.mult), ["ps3", "st_r"], ["hm"])
                yield
                if c < NCH - 1:
                    if c == 0:
                        dve(lambda e: e.tensor_copy(out=Sf[0:64, :, :], in_=dcb[0:64, :, 0:65]), ["ps2"], ["Sf_a"])
                        yield
                        dve(lambda e: e.tensor_copy(out=Sf[64:128, :, :], in_=dcb[64:128, :, 65:130]), ["ps2"], ["Sf_b"])
                        yield
                    else:
                        dve(lambda e: e.tensor_add(out=Sf[0:64, :, :], in0=Sf[0:64, :, :], in1=dcb[0:64, :, 0:65]),
                            ["ps2", "Sf_a"], ["Sf_a"])
                        yield
                        dve(lambda e: e.tensor_add(out=Sf[64:128, :, :], in0=Sf[64:128, :, :], in1=dcb[64:128, :, 65:130]),
                            ["ps2", "Sf_b"], ["Sf_b"])
                        yield
                    pool(lambda e: e.tensor_tensor(out=CT8[:, :, :], in0=Sf[:, :, :],
                                                   in1=st[:, 39:42].unsqueeze(2).to_broadcast([128, 3, 65]), op=ALU.mult),
                         ["Sf_a", "Sf_b", "st"], ["CT8"])
                    yield
                    pool(lambda e: e.tensor_tensor(out=Sf[:, :, :], in0=Sf[:, :, :],
                                                   in1=st[:, 36:39].unsqueeze(2).to_broadcast([128, 3, 65]), op=ALU.mult),
                         ["Sf_a", "Sf_b", "st"], ["Sf_a", "Sf_b"])
                    yield
                for h in range(6):
                    dve(lambda e, h=h: e.bn_stats(out=bns[:, h, :], in_=hm[:, h * 64:(h + 1) * 64]), ["hm"], [f"bns{h}"])
                    yield
                for h in range(6):
                    dve(lambda e, h=h: e.bn_aggr(out=bna[:, h, :], in_=bns[:, h, :]), [f"bns{h}"], [f"bna{h}"])
                    yield
                BNA6 = [f"bna{h}" for h in range(6)]
                act(st[:, 54:60], bna[:, 0:6, 1], AF.Ln, BNA6, ["st_q"], bias=EPS)
                yield
                act(st[:, 54:60], st[:, 54:60], AF.Exp, ["st_q"], ["st_q"], scale=-0.5)
                yield
                dve(lambda e: e.tensor_tensor(out=hm3, in0=hm3, in1=bna[:, 0:6, 0:1].to_broadcast([128, 6, 64]), op=ALU.subtract),
                    ["hm"] + BNA6, ["hm"])
                yield
                dve(lambda e: e.tensor_tensor(out=hm3, in0=hm3, in1=st[:, 54:60].unsqueeze(2).to_broadcast([128, 6, 64]), op=ALU.mult),
                    ["hm", "st_q"], ["hm"])
                yield
                pool(lambda e: e.tensor_mul(out=hcat[:, 0:384], in0=hm[:, :], in1=gog[:, :]), ["hm", "gog"], ["hcat_a"])
                yield

            def strandW():
                po0b = PO[0][:, :].bitcast(BF16)
                trw = po0b[:, 1024:2048]
                trp = po0b[:, 0:1024]
                T.group("pe", [lambda e, h=h: e.transpose(out=trw[0:64, h * 128:(h + 1) * 128], in_=aqkb[:, h * 64:(h + 1) * 64],
                                                          identity=ident[:, :]) for h in range(8)],
                        AQKB + ["ident"], ["ps5"])
                yield
                dve(lambda e: e.tensor_copy(out=qTat[:, :, :], in_=trw[0:64, 0:768].rearrange("p (h t) -> p h t", h=6)),
                    ["ps5"], ["qTat"])
                yield
                act(kTat[:, :, slot * 128:(slot + 1) * 128], trw[0:64, 768:1024].rearrange("p (h t) -> p h t", h=2), AF.Copy,
                    ["ps5"], [f"kTat{slot}"])
                yield
                for g in range(2):
                    scb = PO[1]
                    if nk == 2:
                        kview = kTat[:, g, :]
                        mview = mask3[:, 128:384] if slot == 1 else mask3[:, 0:256]
                        W = 256
                    else:
                        kview = kTat[:, g, slot * 128:(slot + 1) * 128]
                        mview = mask3[:, 0:128]
                        W = 128
                    fns = []
                    for hh in range(3):
                        h = 3 * g + hh
                        o = scb[:, hh * 256:hh * 256 + W]
                        fns.append(lambda e, o=o, h=h, kview=kview: e.matmul(o, lhsT=qTat[:, h, :], rhs=kview, start=True, stop=False))
                        fns.append(lambda e, o=o, mview=mview: e.matmul(o, lhsT=ident[:, :], rhs=mview, start=False, stop=True))
                    T.group("pe", fns, ["qTat", "kTat0", "kTat1", "ident"] + MASK3, ["ps6", "ps7"])
                    yield
                    sc3 = scb[:, 0:768].rearrange("p (h k) -> p h k", h=3)[:, :, 0:W]
                    mxc = st[:, 70:73]
                    dve(lambda e: e.tensor_reduce(out=mxc, in_=sc3, axis=AX.X, op=ALU.max), ["ps6", "ps7"], ["st_m"])
                    yield
                    dve(lambda e: e.scalar_tensor_tensor(out=mxc, in0=mxc, scalar=0.125, in1=sbt[:, 384 + 3 * g:387 + 3 * g],
                                                         op0=ALU.mult, op1=ALU.max), ["st_m", "sbt"], ["st_m"])
                    yield
                    dve(lambda e: e.tensor_scalar(out=st[:, 73:76], in0=mxc, scalar1=-1.0, scalar2=None, op0=ALU.mult), ["st_m"], ["st_n"])
                    yield
                    for hh in range(3):
                        act(pexp[:, hh, 0:W], scb[:, hh * 256:hh * 256 + W], AF.Exp, ["ps6", "ps7", "st_n"], [f"pexp{hh}", "hcatT_a", "hcatT_b"],
                            bias=st[:, 73 + hh:74 + hh], scale=0.125, accum_out=st[:, 76 + hh:77 + hh])
                        yield
                    dve(lambda e: e.tensor_add(out=st[:, 79:82], in0=st[:, 73:76], in1=sbt[:, 384 + 3 * g:387 + 3 * g]), ["st_n", "sbt"], ["st_k"])
                    yield
                    act(st[:, 79:82], st[:, 79:82], AF.Exp, ["st_k"], ["st_k"])
                    yield
                    dve(lambda e: e.tensor_add(out=st[:, 79:82], in0=st[:, 79:82], in1=st[:, 76:79]),
                        ["st_k", "pexp0", "pexp1", "pexp2"], ["st_k"])
                    yield
                    dve(lambda e: e.reciprocal(out=st[:, 79:82], in_=st[:, 79:82]), ["st_k"], ["st_k"])
                    yield
                    fns = []
                    for hh in range(3):
                        for b in range(nk):
                            fns.append(lambda e, hh=hh, b=b: e.transpose(out=trp[:, (hh * 2 + b) * 128:(hh * 2 + b + 1) * 128],
                                                                         in_=pexp[:, hh, b * 128:(b + 1) * 128], identity=ident[:, :]))
                    T.group("pe", fns, ["pexp0", "pexp1", "pexp2", "ident"], ["ps4"])
                    yield
                    pTf = wbuf[:, 768:1536]
                    if nk == 2:
                        act(pTf[:, 0:384], trp[:, 0:384], AF.Copy, ["ps4"], ["pT_a", "hcatT_a", "hcatT_b"])
                        yield
                        dve(lambda e: e.tensor_copy(out=pTf[:, 384:768], in_=trp[:, 384:768]), ["ps4"], ["pT_b", "hcatT_a", "hcatT_b"])
                        yield
                    else:
                        act(pT[:, :, 0, :], trp[:, 0:768].rearrange("p (h b q) -> p h b q", h=3, b=2)[:, :, 0, :], AF.Copy,
                            ["ps4"], ["pT_a", "hcatT_a", "hcatT_b"])
                        yield
                    ob = PO[0][:, 512:704].rearrange("p (h d) -> p h d", h=3)
                    fns = []
                    for hh in range(3):
                        for b in range(nk):
                            sl = v3c if (nk == 1 or b == slot) else v3p
                            fns.append(lambda e, hh=hh, b=b, sl=sl, g=g: e.matmul(ob[:, hh, :], lhsT=pT[:, hh, b, :],
                                                                                  rhs=vat[:, sl, g * 64:(g + 1) * 64],
                                                                                  start=(b == 0), stop=(b == nk - 1)))
                    T.group("pe", fns, ["pT_a", "pT_b", "vat0", "vat1", "vat2"], ["ps5"])
                    yield
                    hb = hcat[:, 384 + g * 192:384 + (g + 1) * 192].rearrange("p (h d) -> p h d", h=3)
                    dve(lambda e, hb=hb, ob=ob: e.tensor_tensor(out=hb, in0=ob, in1=st[:, 79:82].unsqueeze(2).to_broadcast([128, 3, 64]),
                                                                 op=ALU.mult), ["ps5", "st_k"], [f"hcat_b{g}"])
                    yield

            def strandG():
                gv3 = gv[:, :].rearrange("p (g d) -> p g d", g=4)
                for g in range(4):
                    dve(lambda e, g=g: e.bn_stats(out=bns[:, 8 + g, :], in_=gv[:, g * 64:(g + 1) * 64]), ["gv"], [f"bns{8 + g}"])
                    yield
                for g in range(4):
                    dve(lambda e, g=g: e.bn_aggr(out=bna[:, 8 + g, :], in_=bns[:, 8 + g, :]), [f"bns{8 + g}"], [f"bna{8 + g}"])
                    yield
                BNA4 = [f"bna{8 + g}" for g in range(4)]
                act(st[:, 66:70], bna[:, 8:12, 1], AF.Ln, BNA4, ["st_s"], bias=EPS)
                yield
                act(st[:, 66:70], st[:, 66:70], AF.Exp, ["st_s"], ["st_s"], scale=-0.5)
                yield
                dve(lambda e: e.tensor_tensor(out=gv3, in0=gv3, in1=bna[:, 8:12, 0:1].to_broadcast([128, 4, 64]), op=ALU.subtract),
                    ["gv"] + BNA4, ["gv"])
                yield
                dve(lambda e: e.tensor_tensor(out=gv3, in0=gv3, in1=st[:, 66:70].unsqueeze(2).to_broadcast([128, 4, 64]), op=ALU.mult),
                    ["gv", "st_s"], ["gv"])
                yield
                pool(lambda e: e.tensor_mul(out=gv[:, :], in0=gv[:, :], in1=sbt[:, 390:646]), ["gv", "sbt"], ["gv"])
                yield
                pool(lambda e: e.tensor_add(out=vn[:, :], in0=gv[:, :], in1=sbt[:, 646:902]), ["gv", "sbt"], ["vn"])
                yield
                mxb = PO[0][:, 768:1024].rearrange("p (g d) -> p g d", g=4)
                T.group("pe", [lambda e, g=g: e.matmul(mxb[:, g, :], lhsT=wsT[:, g, :], rhs=vn[:, g * 64:(g + 1) * 64], start=True, stop=True)
                               for g in range(4)], ["wsT", "vn"], ["ps5"])
                yield
                dve(lambda e: e.tensor_tensor(out=gv3, in0=mxb, in1=ppt[:, 30:34].unsqueeze(2).to_broadcast([128, 4, 64]), op=ALU.add),
                    ["ps5", "ppt"], ["gv"])
                yield
                pool(lambda e: e.tensor_mul(out=hcat[:, 768:1024], in0=gv[:, :], in1=u_sb[:, :]), ["gv", "u_sb"], ["hcat_c"])
                yield

            strands = [strandM(), strandW(), strandG()]
            while strands:
                for sgen in list(strands):
                    try:
                        next(sgen)
                        yield
                    except StopIteration:
                        strands.remove(sgen)
            HCAT = ["hcat_a", "hcat_b0", "hcat_b1", "hcat_c"]
            yield "S"
            if c == 0 and l == 0 and DBG:
                dbgdump('hcat_a', hcat[:, 0:384], ["hcat_a"])
                dbgdump('hcat_b', hcat[:, 384:768], ["hcat_b0", "hcat_b1"])
                dbgdump('hcat_c', hcat[:, 768:1024], ["hcat_c"])
                dbgdump('st10', st[:, 0:96], ["st", "st_d", "st_r", "st_q"])
                dbgdump('st11', st[:, 0:96], ["st_k", "st_m", "st_n"])
            trb = pb_bf(2)
            T.group("pe", [lambda e, k=k: e.transpose(out=trb[:, k * 128:(k + 1) * 128], in_=hcat[:, k * 128:(k + 1) * 128],
                                                      identity=ident[:, :]) for k in range(8)], HCAT + ["ident"], ["ps2"])
            yield
            hTf = wbuf[:, 0:1024]
            act(hTf[:, 0:512], trb[:, 0:512], AF.Copy, ["ps2"], ["hcatT_a"] + WB)
            yield
            dve(lambda e: e.tensor_copy(out=hTf[:, 512:1024], in_=trb[:, 512:1024]), ["ps2"], ["hcatT_b"] + WB)
            yield
            if c == 0 and l == 0 and DBG:
                dbgdump('hcatT', wbuf[:, 0:1024], ["hcatT_a", "hcatT_b"])
                dbgdump('hcat_all', hcat[:, :], HCAT)
                dbgdump('wout0', w_out_sb[:, 0, :], ["w_out_sb"])
            mixp = PO[1]
            for n in range(2):
                mm_group([(mixp[:, n * 512:(n + 1) * 512], hcatT[:, k, :], w_out_sb[:, k, n * 512:(n + 1) * 512]) for k in range(8)],
                         ["hcatT_a", "hcatT_b", "w_out_sb"], [RO[1][n]])
                yield
            xt = x_tok[:, c, :]
            XR = f"x_tok{c}"
            if l == 0:
                dve(lambda e: e.scalar_tensor_tensor(out=xt, in0=xt, scalar=ALPHA, in1=mixp[:, :], op0=ALU.mult, op1=ALU.add),
                    ["x_tok", XR, "ps6", "ps7"], [XR])
            else:
                dve(lambda e: e.tensor_add(out=xt, in0=xt, in1=mixp[:, :]), ["x_tok", XR, "ps6", "ps7"], [XR])
            yield
            for _ in layer_norm_gen(c, XR, lnp, "lnp", to_xT=True, xscale=1.0 / ALPHA, k=0):
                yield

        lnst = sb("lnst", [128, 4, 2, 6])
        lnag = sb("lnag", [128, 4, 2])
        lnsc = sb("lnsc", [128, 4, 2])
        XNB = [(hcat[:, :], ["hcat_a", "hcat_b0", "hcat_b1", "hcat_c"]),
               (qkT[:, 4:6, :].rearrange("p a t -> p (a t)"), ["qkT"]),
               (qkT[:, 0:2, :].rearrange("p a t -> p (a t)"), ["qkT"]),
               (qkT[:, 2:4, :].rearrange("p a t -> p (a t)"), ["qkT"])]

        def layer_norm_tile(c, XR, prm, PR, to_xT, out_dma=False, xscale=1.0, k=0):
            for _ in layer_norm_gen(c, XR, prm, PR, to_xT, out_dma, xscale, k):
                pass

        def layer_norm_gen(c, XR, prm, PR, to_xT, out_dma=False, xscale=1.0, k=0):
            xt = x_tok[:, c, :]
            SA, SG, SC, SC2 = f"lnst{k}", f"lnag{k}", f"lnsc{k}", f"lnsd{k}"
            for i in range(2):
                dve(lambda e, i=i: e.bn_stats(out=lnst[:, k, i, :], in_=xt[:, i * 512:(i + 1) * 512]), [XR], [SA + "ab"[i]])
                yield
            dve(lambda e: e.bn_aggr(out=lnag[:, k, :], in_=lnst[:, k, :, :]), [SA + "a", SA + "b"], [SG])
            yield
            act(lnsc[:, k, 0:1], lnag[:, k, 1:2], AF.Ln, [SG], [SC], bias=EPS)
            yield
            act(lnsc[:, k, 0:1], lnsc[:, k, 0:1], AF.Exp, [SC], [SC], scale=-0.5)
            yield
            dve(lambda e: e.scalar_tensor_tensor(out=lnsc[:, k, 1:2], in0=lnag[:, k, 0:1], scalar=-1.0, in1=lnsc[:, k, 0:1],
                                                 op0=ALU.mult, op1=ALU.mult), [SG, SC], [SC2])
            yield
            act(xt, xt, AF.Identity, [XR, SC, SC2], [XR], scale=lnsc[:, k, 0:1], bias=lnsc[:, k, 1:2])
            yield
            dve(lambda e: e.tensor_mul(out=xt, in0=xt, in1=prm[:, 0:D]), [XR, PR], [XR])
            yield
            dve(lambda e: e.tensor_add(out=xt, in0=xt, in1=prm[:, D:2 * D]), [XR, PR], [XR])
            yield
            if to_xT:
                xnb, XN = XNB[k]
                act(xnb, xt, AF.Copy, [XR], XN, scale=xscale)
                yield
                bank = 2 + (k % 2)
                trb = pb_bf(bank)
                T.group("pe", [lambda e, kk=kk: e.transpose(out=trb[:, kk * 128:(kk + 1) * 128], in_=xnb[:, kk * 128:(kk + 1) * 128],
                                                            identity=ident[:, :]) for kk in range(8)],
                        XN + ["ident"], [RB[bank]])
                tr3 = trb[:, :].rearrange("p (k t) -> p k t", k=8)
                act(xT[:, 0:4, c * 128:(c + 1) * 128], tr3[:, 0:4, :], AF.Copy, [RB[bank]], [f"xT{c}"])
                dve(lambda e: e.tensor_copy(out=xT[:, 4:8, c * 128:(c + 1) * 128], in_=tr3[:, 4:8, :]), [RB[bank], f"xT{c}"], [f"xT{c}"])
                yield
            if out_dma:
                T.dma("sp", [(y_d[c * 128:(c + 1) * 128, :], xt)], [XR], [], "d_out")
                yield

        def dump_x():
            T.wait_all("sp", ["d_dbg"])
            for c in range(NCH):
                T.dma("sp", [(y_d[c * 128:(c + 1) * 128, :], x_tok[:, c, :])], ["x_tok", f"x_tok{c}"], [], "d_out")
            T.wait_all("sp", ["d_out"])

        rt = sb("rt", [128, 64])
        sgs = [pconv[:, 0:512], yconv[:, :]]
        SGR = ["pconv", "yconv"]
        comb = hm[:, 0:128].rearrange("p (t e) -> p t e", e=NE)

        def ffn_phase(l, is_moe, last, final_layer):
            with nc.allow_non_contiguous_dma(reason="small parameter loads"):
                T.dma("sp", [(lnp[:, :], lnp_d[l, 1].partition_broadcast(128))], [], ["lnp"], "d_lnp")
            if not final_layer:
                dve(lambda e: e.tensor_scalar(out=lnp[:, :], in0=lnp[:, :], scalar1=ALPHA, scalar2=None, op0=ALU.mult), ["lnp"], ["lnp"])
            if is_moe:
                router()
            groups = []
            if is_moe:
                for ex in range(NE):
                    for f0 in range(0, DFE, FG):
                        groups.append((mg_d[0, ex], mu_d[0, ex], md_d[0, ex], f0, min(FG, DFE - f0), ex))
            else:
                for f0 in range(0, DFF, FG):
                    groups.append((fg_d[0], fu_d[0], fd_d[0], f0, min(FG, DFF - f0), None))

            def wviews(b):
                base = b * 12288
                wg = arena[:, base:base + 4096].rearrange("p (k f) -> p k f", k=8)
                wu = arena[:, base + 4096:base + 8192].rearrange("p (k f) -> p k f", k=8)
                wd = arena[:, base + 8192:base + 12288].rearrange("p (j n) -> p j n", j=4)
                return wg, wu, wd

            def hview(i):
                base = 24576 + i * 2048
                return arena[:, base:base + 2048].rearrange("p (j t) -> p j t", j=4)

            def load(gi):
                gsrc, usrc, dsrc, f0, F, ex = groups[gi]
                b = gi % 2
                wg, wu, wd = wviews(b)
                nj = F // 128
                T.dma("pool", [(wg[:, :, 0:F], gsrc[:, f0:f0 + F].rearrange("(k p) f -> p k f", p=128))], [], [f"wg{b}", "w_in_sb", "w_in_qk", "w_out_sb"], f"d_wg{b}")
                T.dma("pool", [(wu[:, :, 0:F], usrc[:, f0:f0 + F].rearrange("(k p) f -> p k f", p=128))], [], [f"wu{b}", "w_in_sb", "w_in_qk", "w_out_sb"], f"d_wu{b}")
                T.dma("pool", [(wd[:, 0:nj, :], dsrc[f0:f0 + F, :].rearrange("(j p) n -> p j n", p=128))], [], [f"wd{b}", "w_in_sb", "w_in_qk", "w_out_sb"], f"d_wd{b}")

            steps = [(gi, tb) for gi in range(len(groups)) for tb in range(4)]
            state = {"gu": 0, "out": 0}
            ln_active = []

            def pump(n):
                for _ in range(n):
                    for g_ in list(ln_active):
                        try:
                            next(g_)
                        except StopIteration:
                            ln_active.remove(g_)

            def GU(si):
                gi, tb = steps[si]
                F = groups[gi][4]
                b = gi % 2
                wg, wu, _ = wviews(b)
                hT = hview(si % 2)
                for j in range(F // 128):
                    p = state["gu"] % 2
                    state["gu"] += 1
                    gb, ub = PB[2 * p], PB[2 * p + 1]
                    mm_group([(gb[:, :], wg[:, k, j * 128:(j + 1) * 128], xT[:, k, tb * 512:(tb + 1) * 512]) for k in range(8)],
                             [f"wg{b}"] + [f"xT{4 * tb + t}" for t in range(4)], [RB[2 * p]])
                    mm_group([(ub[:, :], wu[:, k, j * 128:(j + 1) * 128], xT[:, k, tb * 512:(tb + 1) * 512]) for k in range(8)],
                             [f"wu{b}"] + [f"xT{4 * tb + t}" for t in range(4)], [RB[2 * p + 1]])
                    act(sgs[p], gb[:, :], AF.Silu, [RB[2 * p]], [SGR[p]])
                    dve(lambda e, p=p, j=j, ub=ub, hT=hT: e.tensor_tensor(out=hT[:, j, :], in0=sgs[p], in1=ub[:, :], op=ALU.mult),
                        [SGR[p], RB[2 * p + 1]], [f"hT{si % 2}_{j}"])
                    pump(2)

            def DOWN(si):
                gi, tb = steps[si]
                F = groups[gi][4]
                ex = groups[gi][5]
                b = gi % 2
                _, _, wd = wviews(b)
                hT = hview(si % 2)
                nj = F // 128
                final = (gi == len(groups) - 1)
                for t in range(4):
                    c = tb * 4 + t
                    q = state["out"] % 2
                    state["out"] += 1
                    for n in range(2):
                        mm_group([(PO[q][:, n * 512:(n + 1) * 512], hT[:, j, t * 128:(t + 1) * 128], wd[:, j, n * 512:(n + 1) * 512])
                                  for j in range(nj)], [f"hT{si % 2}_{j}" for j in range(nj)] + [f"wd{b}"], [RO[q][n]])
                    xt = x_tok[:, c, :]
                    XR = f"x_tok{c}"
                    if ex is None:
                        dve(lambda e, xt=xt, q=q: e.tensor_add(out=xt, in0=xt, in1=PO[q][:, :]), [XR, RO[q][0], RO[q][1]], [XR])
                    else:
                        dve(lambda e, xt=xt, q=q, c=c, ex=ex: e.scalar_tensor_tensor(
                            out=xt, in0=PO[q][:, :], scalar=comb[:, c, ex:ex + 1], in1=xt, op0=ALU.mult, op1=ALU.add),
                            [XR, RO[q][0], RO[q][1], "hm"], [XR])
                    if final:
                        if len(ln_active) >= 4:
                            for _ in ln_active.pop(0):
                                pass
                        ln_active.append(layer_norm_gen(c, XR, lnp, "lnp", to_xT=(not last), out_dma=last,
                                                        xscale=(1.0 if final_layer else 1.0 / ALPHA), k=c % 4))
                    pump(3)

            load(0)
            if len(groups) > 1:
                load(1)
            for si in range(len(steps)):
                gi, tb = steps[si]
                GU(si)
                if si > 0:
                    DOWN(si - 1)
                    pgi, ptb = steps[si - 1]
                    if ptb == 3 and pgi + 2 < len(groups):
                        load(pgi + 2)
            DOWN(len(steps) - 1)
            while ln_active:
                pump(1)

        def router():
            lg = hm[:, 128:256].rearrange("p (t e) -> p t e", e=NE)
            for c in range(NCH):
                mm_group([(PB[c % 2][:, 0:NE], xT[:, k, c * 128:(c + 1) * 128], wr_sb[:, k, :]) for k in range(8)],
                         [f"xT{c}", "wr_sb"], [RB[c % 2]])
                dve(lambda e, c=c: e.tensor_tensor(out=lg[:, c, :], in0=PB[c % 2][:, 0:NE], in1=sbt[:, 902:910], op=ALU.add),
                    [RB[c % 2], "sbt"], ["hm"])
            m1 = rt[:, 0:16]
            m2 = rt[:, 16:32]
            e21 = rt[:, 32:48]
            g1 = rt[:, 48:64]
            oh1 = hm[:, 256:384].rearrange("p (t e) -> p t e", e=NE)
            l2 = gv[:, 0:128].rearrange("p (t e) -> p t e", e=NE)
            oh2 = gv[:, 128:256].rearrange("p (t e) -> p t e", e=NE)
            dve(lambda e: e.tensor_reduce(out=m1, in_=lg, axis=AX.X, op=ALU.max), ["hm"], ["rt"])
            dve(lambda e: e.tensor_tensor(out=oh1, in0=lg, in1=m1.unsqueeze(2).to_broadcast([128, NCH, NE]), op=ALU.is_equal),
                ["hm", "rt"], ["hm"])
            dve(lambda e: e.scalar_tensor_tensor(out=l2, in0=oh1, scalar=-1e30, in1=lg, op0=ALU.mult, op1=ALU.add),
                ["hm", "hm"], ["gv"])
            dve(lambda e: e.tensor_reduce(out=m2, in_=l2, axis=AX.X, op=ALU.max), ["gv"], ["rt"])
            dve(lambda e: e.tensor_tensor(out=oh2, in0=l2, in1=m2.unsqueeze(2).to_broadcast([128, NCH, NE]), op=ALU.is_equal),
                ["gv", "rt"], ["gv"])
            dve(lambda e: e.tensor_sub(out=e21, in0=m2, in1=m1), ["rt", "rt"], ["rt"])
            act(e21, e21, AF.Exp, ["rt"], ["rt"])
            dve(lambda e: e.tensor_scalar(out=g1, in0=e21, scalar1=1.0, scalar2=None, op0=ALU.add), ["rt"], ["rt"])
            dve(lambda e: e.reciprocal(out=g1, in_=g1), ["rt"], ["rt"])
            dve(lambda e: e.tensor_mul(out=e21, in0=e21, in1=g1), ["rt", "rt"], ["rt"])
            dve(lambda e: e.tensor_tensor(out=oh1, in0=oh1, in1=g1.unsqueeze(2).to_broadcast([128, NCH, NE]), op=ALU.mult),
                ["hm", "rt"], ["hm"])
            dve(lambda e: e.tensor_tensor(out=oh2, in0=oh2, in1=e21.unsqueeze(2).to_broadcast([128, NCH, NE]), op=ALU.mult),
                ["gv", "rt"], ["gv"])
            dve(lambda e: e.tensor_add(out=comb, in0=oh1, in1=oh2), ["hm", "gv"], ["hm"])

        with nc.allow_non_contiguous_dma(reason="small parameter loads"):
            T.dma("pool", [(wr_sb[:, :, :], wr_d[0].rearrange("(k p) e -> p k e", p=128))], [], ["wr_sb"], "d_wr")
        done = False
        if stage == "s0":
            dump_x()
            done = True
        for l in range(DEPTH if stage != "s0" else 0):
            is_moe = (l % 2 == 1)
            last = (l == DEPTH - 1)
            with nc.allow_non_contiguous_dma(reason="small parameter loads"):
                T.dma("sp", [(ppt[:, :], pp_d[l])], [], ["ppt"], "d_ppt")
                T.dma("sp", [(sbt[:, :], sbr_d[l].partition_broadcast(128))], [], ["sbt"], "d_sbt")
                T.dma("sp", [(lnp[:, :], lnp_d[l, 0].partition_broadcast(128))], [], ["lnp"], "d_lnp")
                load_bias_rows(l)
            if l == 0:
                for c in range(NCH):
                    T.dma("sp", [(x_tok[:, c, :], x_d[c * 128:(c + 1) * 128, :])], [], [f"x_tok{c}"], f"d_x{c}")
            dve(lambda e: e.tensor_scalar(out=lnp[:, :], in0=lnp[:, :], scalar1=ALPHA, scalar2=None, op0=ALU.mult), ["lnp"], ["lnp"])
            dve(lambda e: e.tensor_scalar(out=sbt[:, 0:384], in0=sbt[:, 0:384], scalar1=0.5, scalar2=None, op0=ALU.mult), ["sbt"], ["sbt"])
            load_mixer_weights(l)
            if l == 0:
                for c in range(NCH):
                    XR = f"x_tok{c}"
                    act(hcat[:, :], x_tok[:, c, :], AF.Copy, ["x_tok", XR], ["hcat_a", "hcat_b0", "hcat_b1", "hcat_c"])
                    trb = pb_bf(2)
                    T.group("pe", [lambda e, k=k: e.transpose(out=trb[:, k * 128:(k + 1) * 128], in_=hcat[:, k * 128:(k + 1) * 128],
                                                              identity=ident[:, :]) for k in range(8)],
                            ["hcat_a", "hcat_b0", "hcat_b1", "hcat_c", "ident"], ["ps2"])
                    tr3 = trb[:, :].rearrange("p (k t) -> p k t", k=8)
                    act(xT[:, 0:4, c * 128:(c + 1) * 128], tr3[:, 0:4, :], AF.Copy, ["ps2"], [f"xT{c}"])
                    dve(lambda e, c=c, tr3=tr3: e.tensor_copy(out=xT[:, 4:8, c * 128:(c + 1) * 128], in_=tr3[:, 4:8, :]), ["ps2", f"xT{c}"], [f"xT{c}"])
            pool(lambda e: e.memset(carry[:, :, :], 0.0), [], ["carry"])
            if stage == "s1":
                dump_x()
                done = True
                break
            def step(gen, sfx):
                T.suffix = sfx
                try:
                    r = next(gen)
                except StopIteration:
                    r = "END"
                T.suffix = "0"
                return r

            def chain2(g1, g2):
                for v in g1:
                    yield v
                for v in g2:
                    yield v

            prev = None
            for c in range(NCH):
                in_tail0 = False
                if c % 4 == 0:
                    cur = (chain2(phaseA(l, c // 4), phaseB(l, c, is_moe)), str(c % 2))
                    if prev is not None:
                        for _ in range(build.hold):
                            r = step(*prev)
                            if r == "END":
                                prev = None
                                break
                            if r == "S":
                                in_tail0 = True
                else:
                    cur = (phaseB(l, c, is_moe), str(c % 2))
                b1_done = False
                in_tail = in_tail0
                while not b1_done:
                    if prev is None:
                        if step(*cur) == "B1":
                            b1_done = True
                        continue
                    n_prev = 1 if in_tail else (build.ratio_b if c % 4 == 0 else build.ratio)
                    for _ in range(n_prev):
                        r = step(*prev)
                        if r == "END":
                            prev = None
                            break
                        if r == "S":
                            in_tail = True
                            break
                    for _ in range(build.dens if in_tail else 1):
                        if step(*cur) == "B1":
                            b1_done = True
                            break
                if prev is not None:
                    while step(*prev) != "END":
                        pass
                prev = cur
            while step(*prev) != "END":
                pass
            if stage == f"mix{l}":
                dump_x()
                done = True
                break
            ffn_phase(l, is_moe, last and stage == "full", last)
            if stage == f"l{l}" and not (last and stage == "full"):
                if DBG:
                    dbgdump('xT_k0', xT[:, 0, 0:1024], [f"xT{t}" for t in range(8)])
                    dbgdump('xT_k5', xT[:, 5, 0:1024], [f"xT{t}" for t in range(8)])
                dump_x()
                done = True
                break
        if not done:
            T.wait_all("sp", ["d_out"])
        build.ninst = T.nins
        build.sbuf_left = nc.sbuf_bytes_remaining
    return nc


_NC_CACHE = {}


def _host_layout(inp, b):
    f32 = np.float32
    pp = np.zeros((DEPTH, 128, NPP), f32)
    sbr = np.zeros((DEPTH, 1, NSB), f32)
    lnpv = np.zeros((DEPTH, 2, 1, 2 * D), f32)
    btok = np.zeros((DEPTH, 1, NTOKC), f32)
    for l in range(DEPTH):
        cw = np.asarray(inp["conv_w"][l], f32)
        pp[l, :, 0:24] = cw.reshape(4, 6, 128).transpose(2, 1, 0).reshape(128, 24)
        pp[l, :, 24:30] = np.asarray(inp["b_in"][l, 0:768], f32).reshape(6, 128).T
        pp[l, :, 30:34] = np.asarray(inp["sgu_b_s"][l], f32).T
        sbr[l, 0, 0:384] = inp["mlstm_norm_g"][l]
        sbr[l, 0, 384:390] = inp["attn_sinks"][l]
        sbr[l, 0, 390:646] = inp["sgu_norm_g"][l]
        sbr[l, 0, 646:902] = inp["sgu_norm_b"][l]
        if l % 2 == 1:
            sbr[l, 0, 902:910] = inp["moe_b_router"][l // 2]
        sbr[l, 0, 910:922] = inp["b_in"][l, C_MI:C_AQ]
        lnpv[l, 0, 0, 0:D] = inp["ln1_g"][l]
        lnpv[l, 0, 0, D:] = inp["ln1_b"][l]
        lnpv[l, 1, 0, 0:D] = inp["ln2_g"][l]
        lnpv[l, 1, 0, D:] = inp["ln2_b"][l]
        btok[l, 0, :] = inp["b_in"][l, TOKC0:]
    shared = {
        "w_in": np.ascontiguousarray(inp["w_in"], f32), "pp": pp, "sbr": sbr, "lnp": lnpv, "btok": btok,
        "sgu_w_s": np.ascontiguousarray(inp["sgu_w_s"], f32), "w_out": np.ascontiguousarray(inp["w_out"], f32),
        "ffn_w_gate": np.ascontiguousarray(inp["ffn_w_gate"], f32), "ffn_w_up": np.ascontiguousarray(inp["ffn_w_up"], f32),
        "ffn_w_down": np.ascontiguousarray(inp["ffn_w_down"], f32), "moe_w_router": np.ascontiguousarray(inp["moe_w_router"], f32),
        "moe_w_gate": np.ascontiguousarray(inp["moe_w_gate"], f32), "moe_w_up": np.ascontiguousarray(inp["moe_w_up"], f32),
        "moe_w_down": np.ascontiguousarray(inp["moe_w_down"], f32),
    }
    return shared


def make_in_maps(inp, cores):
    shared = _host_layout(inp, 0)
    maps = []
    for b in cores:
        m = dict(shared)
        m["x"] = np.ascontiguousarray(inp["x"][b], np.float32)
        m["posT"] = np.ascontiguousarray(np.asarray(inp["positions"][b], np.int32).reshape(NCH, 128).T)
        maps.append(m)
    return maps


def kernel(**inputs):
    if "full" not in _NC_CACHE:
        _NC_CACHE["full"] = build("full")
    nc = _NC_CACHE["full"]
    in_maps = make_in_maps(inputs, list(range(NCORES)))
    res = run_bass_kernel_spmd(nc, in_maps, core_ids=list(range(NCORES)))
    return np.stack([np.asarray(r["y"], np.float32) for r in res.results], axis=0)
```

```python
import math
from contextlib import ExitStack

import numpy as np
import concourse.bass as bass
import concourse.mybir as mybir
from concourse.bass_utils import run_bass_kernel_spmd

F32 = mybir.dt.float32
BF16 = mybir.dt.bfloat16
I32 = mybir.dt.int32
AF = mybir.ActivationFunctionType
ALU = mybir.AluOpType
AX = mybir.AxisListType

NCORES = 8
D = 1024
S = 2048
NCH = 16
DEPTH = 2
DPROJ = 2700
C_MQ, C_MK, C_MV, C_MO, C_MI, C_MF, C_AQ, C_AK, C_AV, C_SU, C_SV = (
    0, 384, 768, 1152, 1536, 1542, 1548, 1932, 2060, 2188, 2444)
TOKC0 = 768
NTOKC = DPROJ - TOKC0
DFF = 2816
NE = 8
DFE = 3584
ALPHA = (2.0 * DEPTH) ** 0.25
EPS = 1e-5
ROPE_THETA = 500000.0
NEG = -30000.0
NPP = 34
NSB = 922
FG = 512


class Tracker:
    def __init__(self, nc, es):
        self.nc = nc
        self.es = es
        self.eng = {"pe": nc.tensor, "dve": nc.vector, "act": nc.scalar, "pool": nc.gpsimd, "sp": nc.sync}
        self.sem = {}
        self.cnt = {}
        self.waited = {k: {} for k in self.eng}
        for k in ("pe", "dve", "act", "pool"):
            self.sem[k] = es.enter_context(nc.semaphore("sem_" + k))
            self.cnt[k] = 0
        self.res = {}
        self.nins = 0
        self.mangle_names = frozenset()
        self.suffix = "0"

    def _m(self, names):
        mn = self.mangle_names
        sfx = self.suffix
        return [n + sfx if n in mn else n for n in names]

    @staticmethod
    def _excl(reads, writes):
        ps = [r for r in reads if r.startswith("ps")]
        if ps:
            reads = [r for r in reads if not r.startswith("ps")]
            writes = list(writes) + ps
        return reads, writes

    def _deps(self, e, reads, writes):
        need = {}

        def add(tok, skip_same):
            if tok is None:
                return
            k, v = tok
            if skip_same and k == e:
                return
            if need.get(k, 0) < v:
                need[k] = v

        for r in reads:
            st = self.res.get(r)
            if st is not None:
                add(st[0], False)
        for w in writes:
            st = self.res.get(w)
            if st is not None:
                add(st[0], True)
                for k, v in st[1].items():
                    add((k, v), True)
        return need

    def _wait(self, e, need):
        wd = self.waited[e]
        for k, v in need.items():
            if wd.get(k, 0) < v:
                self.eng[e].wait_ge(self.sem[k], v)
                wd[k] = v

    def _commit(self, tok, reads, writes):
        for r in reads:
            st = self.res.get(r)
            if st is None:
                st = [None, {}]
                self.res[r] = st
            if st[1].get(tok[0], 0) < tok[1]:
                st[1][tok[0]] = tok[1]
        for w in writes:
            self.res[w] = [tok, {}]

    def op(self, e, fn, reads=(), writes=()):
        reads, writes = self._excl(self._m(reads), self._m(writes))
        self._wait(e, self._deps(e, reads, writes))
        ins = fn(self.eng[e])
        self.cnt[e] += 1
        ins.then_inc(self.sem[e], 1)
        self.nins += 1
        self._commit((e, self.cnt[e]), reads, writes)

    def group(self, e, fns, reads=(), writes=()):
        reads, writes = self._excl(self._m(reads), self._m(writes))
        self._wait(e, self._deps(e, reads, writes))
        ins = None
        for fn in fns:
            ins = fn(self.eng[e])
            self.nins += 1
        self.cnt[e] += 1
        ins.then_inc(self.sem[e], 1)
        self._commit((e, self.cnt[e]), reads, writes)

    def dma(self, q, pairs, reads, writes, key):
        if key not in self.sem:
            self.sem[key] = self.es.enter_context(self.nc.semaphore("sem_" + key))
            self.cnt[key] = 0
        reads, writes = self._m(reads), self._m(writes)
        self._wait(q, self._deps(q, reads, writes))
        for (o, i) in pairs:
            self.eng[q].dma_start(out=o, in_=i).then_inc(self.sem[key], 16)
            self.cnt[key] += 16
            self.nins += 1
        self._commit((key, self.cnt[key]), reads, writes)

    def wait_all(self, e, keys):
        for k in keys:
            if k in self.sem and self.waited[e].get(k, 0) < self.cnt[k]:
                self.eng[e].wait_ge(self.sem[k], self.cnt[k])
                self.waited[e][k] = self.cnt[k]


def build(stage="full"):
    if not hasattr(build, 'debug'):
        build.debug = False
    if not hasattr(build, 'ratio'):
        build.ratio = 30
    if not hasattr(build, 'hold'):
        build.hold = 35
    if not hasattr(build, 'ratio_b'):
        build.ratio_b = 3
    if not hasattr(build, 'dens'):
        build.dens = 2
    nc = bass.Bass("TRN2", target_bir_lowering=False)

    def din(name, shape, dt=F32):
        return nc.dram_tensor(name, shape, dt, kind="ExternalInput").ap()

    x_d = din("x", [S, D])
    pos_d = din("posT", [128, NCH], I32)
    w_in_d = din("w_in", [DEPTH, D, DPROJ])
    pp_d = din("pp", [DEPTH, 128, NPP])
    sbr_d = din("sbr", [DEPTH, 1, NSB])
    lnp_d = din("lnp", [DEPTH, 2, 1, 2 * D])
    btok_d = din("btok", [DEPTH, 1, NTOKC])
    ws_d = din("sgu_w_s", [DEPTH, 4, 128, 128])
    w_out_d = din("w_out", [DEPTH, D, D])
    fg_d = din("ffn_w_gate", [1, D, DFF])
    fu_d = din("ffn_w_up", [1, D, DFF])
    fd_d = din("ffn_w_down", [1, DFF, D])
    wr_d = din("moe_w_router", [1, D, NE])
    mg_d = din("moe_w_gate", [1, NE, D, DFE])
    mu_d = din("moe_w_up", [1, NE, D, DFE])
    md_d = din("moe_w_down", [1, NE, DFE, D])
    y_d = nc.dram_tensor("y", [S, D], F32, kind="ExternalOutput").ap()
    DBG = build.debug
    if DBG:
        dbg_d = nc.dram_tensor("dbg", [128, 16384], F32, kind="ExternalOutput").ap()
    build.dbg_slots = {}
    dbg_state = {"off": 0}

    with ExitStack() as es:
        T = Tracker(nc, es)
        T.mangle_names = frozenset(["gog", "vp", "u_sb", "gv", "aqkb_a", "aqkb_b", "aqkb_c", "st", "st_d", "st_r", "st_q",
                                    "st_s", "st_m", "st_n", "st_k", "st_l", "st_l2"])

        def sb(name, shape, dt=F32):
            return es.enter_context(nc.sbuf_tensor(name, shape, dt))

        x_tok = sb("x_tok", [128, NCH, D])
        xT = sb("xT", [128, 8, S], BF16)
        arena = sb("arena", [128, 29792], BF16)
        qkT = sb("qkT", [128, 6, 512], BF16)
        lnp = sb("lnp_sb", [128, 2 * D])
        w_in_sb = arena[:, 0:8 * DPROJ].rearrange("p (k n) -> p k n", k=8)
        w_out_sb = arena[:, 8 * DPROJ:8 * DPROJ + 8 * D].rearrange("p (k n) -> p k n", k=8)

        ident = sb("ident", [128, 128], BF16)
        U_f = sb("U_f", [128, 128])
        ones_f = sb("ones_f", [128, 128])
        mask3 = sb("mask3", [128, 384], BF16)
        cs = sb("cs", [128, 2, NCH, 8])
        ppt = sb("ppt", [128, NPP])
        sbt = sb("sbt", [128, NSB])
        brow = sb("brow", [65, 780], BF16)
        ones33 = sb("ones33", [65, 128], BF16)
        wsT = sb("wsT", [128, 4, 128], BF16)
        wr_sb = sb("wr_sb", [128, 8, NE], BF16)
        pconv = sb("pconv", [128, 515])
        yconv = sb("yconv", [128, 512])
        carry = sb("carry", [128, 6, 3])
        gogs = [sb(f"gog{i}", [128, 384], BF16) for i in range(2)]
        sts = [sb(f"st{i}", [128, 96]) for i in range(2)]
        gog, st = gogs[0], sts[0]
        bns = sb("bns", [128, 12, 6])
        bna = sb("bna", [128, 12, 2])
        vps = [sb(f"vp{i}", [128, 6, 65], BF16) for i in range(2)]
        vp = vps[0]
        ktok = sb("ktok", [128, 384], BF16)
        smT = sb("smT", [128, 6, 128], BF16)
        hm = sb("hm", [128, 384])
        Sf = sb("Sf", [128, 3, 65])
        CT8 = sb("CT8", [128, 3, 65], BF16)
        tr4 = sb("tr4", [128, 4, 8, 8])
        aqkbs = [sb(f"aqkb{i}", [128, 512], BF16) for i in range(2)]
        aqkb = aqkbs[0]
        qTat = sb("qTat", [64, 6, 128], BF16)
        kTat = sb("kTat", [64, 2, 256], BF16)
        vat = sb("vat", [128, 3, 128], BF16)
        wbuf = sb("wbuf", [128, 1536], BF16)
        pexp = wbuf[:, 0:768].rearrange("p (h k) -> p h k", h=3)
        pT = wbuf[:, 768:1536].rearrange("p (h b q) -> p h b q", h=3, b=2)
        hcatT = wbuf[:, 0:1024].rearrange("p (k t) -> p k t", k=8)
        u_sbs = [sb(f"u_sb{i}", [128, 256], BF16) for i in range(2)]
        gvs = [sb(f"gv{i}", [128, 256]) for i in range(2)]
        u_sb, gv = u_sbs[0], gvs[0]
        vn = sb("vn", [128, 256], BF16)
        hcat = sb("hcat", [128, D], BF16)

        PB = [es.enter_context(nc.psum_tensor(f"pb{i}", [128, 512], F32)) for i in range(4)]
        PO = [es.enter_context(nc.psum_tensor(f"po{i}", [128, 1024], F32)) for i in range(2)]
        RB = ["ps0", "ps1", "ps2", "ps3"]
        RO = [("ps4", "ps5"), ("ps6", "ps7")]

        def pb_bf(i):
            return PB[i][:, :].bitcast(BF16)

        def dbgdump(name, ap2d, reads):
            if not DBG or name in build.dbg_slots:
                return
            P, n = ap2d.shape[0], ap2d.shape[1]
            o = dbg_state["off"]
            build.dbg_slots[name] = (o, P, n)
            dbg_state["off"] = o + n
            T.dma("pool", [(dbg_d[0:P, o:o + n], ap2d)], reads, [], "d_dbg")

        def act(out, in_, func, reads, writes, **kw):
            T.op("act", lambda e: e.activation(out=out, in_=in_, func=func, **kw), reads, writes)

        def dve(fn, reads, writes):
            T.op("dve", fn, reads, writes)

        def pool(fn, reads, writes):
            T.op("pool", fn, reads, writes)

        def mm_group(mms, reads, writes):
            n = len(mms)
            fns = []
            for i, (o, l, r) in enumerate(mms):
                fns.append(lambda e, o=o, l=l, r=r, i=i: e.matmul(o, lhsT=l, rhs=r, start=(i == 0), stop=(i == n - 1)))
            T.group("pe", fns, reads, writes)

        with nc.allow_non_contiguous_dma(reason="small parameter loads"):
            posi = sb("posi", [128, NCH], I32)
            T.dma("sp", [(posi[:, :], pos_d)], [], ["posi"], "d_pos")

        pool(lambda e: e.memset(ones_f[:, :], 1.0), [], ["ones_f"])
        pool(lambda e: e.affine_select(out=U_f[:, :], in_=ones_f[:, :], pattern=[[1, 128]], compare_op=ALU.is_ge,
                                       fill=0.0, base=0, channel_multiplier=-1), ["ones_f"], ["U_f"])
        idf = hm[:, 0:128]
        pool(lambda e: e.affine_select(out=idf, in_=ones_f[:, :], pattern=[[1, 128]], compare_op=ALU.is_equal,
                                       fill=0.0, base=0, channel_multiplier=-1), ["ones_f"], ["hm"])
        pool(lambda e: e.tensor_copy(out=ident[:, :], in_=idf), ["hm"], ["ident"])
        mtmp = hm[:, 128:384]
        pool(lambda e: e.memset(mtmp, 0.0), [], ["hm"])
        curm = gv[:, 0:128]
        prevm = gv[:, 128:256]
        pool(lambda e: e.affine_select(out=curm, in_=mtmp[:, 0:128], pattern=[[-1, 128]], compare_op=ALU.is_ge,
                                       fill=NEG, base=0, channel_multiplier=1), ["hm"], ["gv"])
        pool(lambda e: e.affine_select(out=prevm, in_=mtmp[:, 0:128], pattern=[[1, 128]], compare_op=ALU.is_gt,
                                       fill=NEG, base=0, channel_multiplier=-1), ["hm"], ["gv"])
        pool(lambda e: e.tensor_copy(out=mask3[:, 0:128], in_=curm), ["gv"], ["mask3a"])
        pool(lambda e: e.tensor_copy(out=mask3[:, 128:256], in_=prevm), ["gv"], ["mask3b"])
        pool(lambda e: e.tensor_copy(out=mask3[:, 256:384], in_=curm), ["gv"], ["mask3c"])
        MASK3 = ["mask3a", "mask3b", "mask3c"]
        pool(lambda e: e.memset(ones33[:, :], 1.0), [], ["ones33"])
        pool(lambda e: e.memset(carry[:, :, :], 0.0), [], ["carry"])

        posf = st[:, 0:16]
        dve(lambda e: e.tensor_copy(out=posf, in_=posi[:, :]), ["posi"], ["st"])
        invf = st[:, 16:24]
        for j in range(8):
            v = float(np.float32(ROPE_THETA) ** np.float32(-(2.0 * j) / 16.0))
            pool(lambda e, j=j, v=v: e.memset(st[:, 16 + j:17 + j], v), [], [f"invf{j}"])
        INVF = [f"invf{j}" for j in range(8)]
        ang = hm[:, 0:128].rearrange("p (c j) -> p c j", j=8)
        dve(lambda e: e.tensor_tensor(out=ang, in0=posf.unsqueeze(2).to_broadcast([128, NCH, 8]),
                                      in1=invf.unsqueeze(1).to_broadcast([128, NCH, 8]), op=ALU.mult),
            ["st"] + INVF + ["ident"], ["hm"])
        C1 = 6.28125
        C2 = 2.0 * math.pi - 6.28125
        kint = gvs[1][:, 0:128].bitcast(I32)
        for which in range(2):
            shift = math.pi / 2 if which == 0 else 0.0
            a2 = hm[:, 128:256]
            kf = hm[:, 256:384]
            a1 = hm[:, 0:128]
            dve(lambda e: e.tensor_scalar(out=a2, in0=a1, scalar1=shift, scalar2=None, op0=ALU.add), ["hm"], ["hm"])
            dve(lambda e: e.tensor_scalar(out=kint[:, :], in0=a2, scalar1=1.0 / (2.0 * math.pi), scalar2=None, op0=ALU.mult),
                ["hm"], ["kint"])
            dve(lambda e: e.tensor_copy(out=kf, in_=kint[:, :]), ["kint"], ["hm"])
            dve(lambda e: e.scalar_tensor_tensor(out=a2, in0=kf, scalar=-C1, in1=a2, op0=ALU.mult, op1=ALU.add),
                ["hm"], ["hm"])
            dve(lambda e: e.scalar_tensor_tensor(out=a2, in0=kf, scalar=-C2, in1=a2, op0=ALU.mult, op1=ALU.add),
                ["hm"], ["hm"])
            dve(lambda e: e.tensor_scalar(out=a2, in0=a2, scalar1=-3.1415925, scalar2=3.1415925, op0=ALU.max, op1=ALU.min),
                ["hm"], ["hm"])
            act(cs[:, which, :, :].rearrange("p c j -> p (c j)"), a2, AF.Sin, ["hm"], ["cs"])

        BROW = {C_MO: (0, 0), C_MV: (0, 396), C_AQ: (32, 0), C_SV: (32, 512), C_AV: (64, 0)}
        TILE_END = {C_MO: C_AQ, C_MV: C_MO, C_AQ: C_AV, C_SV: DPROJ, C_AV: C_SV}

        def load_bias_rows(l):
            prs = []
            for c0, (P, o) in BROW.items():
                n = TILE_END[c0] - c0
                prs.append((brow[P:P + 1, o:o + n], btok_d[l][:, c0 - TOKC0:c0 - TOKC0 + n]))
            T.dma("pool", prs, [], ["brow"], "d_brow")
            pool(lambda e: e.memset(brow[0:1, 384:396], 0.0), ["brow"], ["brow"])

        HCATALL = ["hcat_a", "hcat_b0", "hcat_b1", "hcat_c"]
        ARENA_FFN = [f"w{x}{b}" for x in "gud" for b in range(2)] + [f"hT{i}_{j}" for i in range(2) for j in range(4)]

        def load_mixer_weights(l):
            wtmp = hcat[:, 0:512].rearrange("p (g s) -> p g s", g=4)
            T.dma("pool", [(wtmp, ws_d[l].rearrange("g t s -> t g s"))], [], HCATALL, "d_ws")
            srcw = w_in_d[l].rearrange("(k p) n -> p k n", p=128)
            T.dma("pool", [(w_in_sb[:, :, 0:TOKC0], srcw[:, :, 0:TOKC0])], [], ["w_in_qk"] + ARENA_FFN, "d_winqk")
            T.dma("pool", [(w_in_sb[:, k, TOKC0:DPROJ], srcw[:, k, TOKC0:DPROJ]) for k in range(8)], [], ["w_in_sb"] + ARENA_FFN, "d_win")
            srco = w_out_d[l].rearrange("(k p) n -> p k n", p=128)
            T.dma("pool", [(w_out_sb[:, 0:4, :], srco[:, 0:4, :]), (w_out_sb[:, 4:8, :], srco[:, 4:8, :])],
                  [], ["w_out_sb"] + ARENA_FFN, "d_wout")
            trb = pb_bf(2)
            T.group("pe", [lambda e, g=g: e.transpose(out=trb[:, g * 128:(g + 1) * 128], in_=wtmp[:, g, :], identity=ident[:, :])
                           for g in range(4)], HCATALL + ["ident"], ["ps2"])
            dve(lambda e: e.tensor_tensor(out=wsT[:, :, :], in0=trb[:, 0:512].rearrange("p (g t) -> p g t", g=4),
                                          in1=U_f[:, :].unsqueeze(1).to_broadcast([128, 4, 128]), op=ALU.mult),
                ["ps2", "U_f"], ["wsT"])

        def phaseA(l, j):
            for cc in range(6):
                bank = cc % 2
                mm_group([(PB[bank][:, :], w_in_sb[:, k, cc * 128:(cc + 1) * 128], xT[:, k, j * 512:(j + 1) * 512])
                          for k in range(8)], ["w_in_qk"] + [f"xT{4 * j + t}" for t in range(4)], [RB[bank]])
                yield
                pool(lambda e, cc=cc: e.tensor_copy(out=pconv[:, 0:3], in_=carry[:, cc, :]), ["carry"], ["pconv"])
                yield
                act(pconv[:, 3:515], PB[bank][:, :], AF.Identity, [RB[bank], "ppt"], ["pconv"],
                    bias=ppt[:, 24 + cc:25 + cc])
                yield
                pool(lambda e, cc=cc: e.tensor_copy(out=carry[:, cc, :], in_=pconv[:, 512:515]), ["pconv"], ["carry"])
                yield
                dve(lambda e, cc=cc: e.tensor_scalar(out=yconv[:, :], in0=pconv[:, 3:515], scalar1=ppt[:, cc * 4 + 3:cc * 4 + 4],
                                                      scalar2=None, op0=ALU.mult), ["pconv", "ppt"], ["yconv"])
                yield
                for jj in (2, 1, 0):
                    dve(lambda e, cc=cc, jj=jj: e.scalar_tensor_tensor(
                        out=yconv[:, :], in0=pconv[:, jj:jj + 512], scalar=ppt[:, cc * 4 + jj:cc * 4 + jj + 1],
                        in1=yconv[:, :], op0=ALU.mult, op1=ALU.add), ["pconv", "ppt", "yconv"], ["yconv"])
                    yield
                act(qkT[:, cc, :], yconv[:, :], AF.Silu, ["yconv"], ["qkT"])
                yield

        def phaseB(l, c, is_moe):
            tok0 = c * 128
            off = (c % 4) * 128
            slot = c % 2
            par = c % 2
            gog, st, vp, aqkb, u_sb, gv = gogs[par], sts[par], vps[par], aqkbs[par], u_sbs[par], gvs[par]
            v3c, v3p = c % 3, (c - 1) % 3
            WB = ["pexp0", "pexp1", "pexp2", "pT_a", "pT_b"]

            def proj_tile(bank, c0, c1):
                n = c1 - c0
                mms = [(PB[bank][:, 0:n], xT[:, k, tok0:tok0 + 128], w_in_sb[:, k, c0:c1]) for k in range(8)]
                P, o = BROW[c0]
                mms.append((PB[bank][:, 0:n], ones33[P:P + 1, :], brow[P:P + 1, o:o + n]))
                mm_group(mms, [f"xT{c}", "w_in_sb", "ones33", "brow"], [RB[bank]])

            proj_tile(0, C_MO, C_AQ)
            yield
            act(gog[:, :], PB[0][:, 0:384], AF.Tanh, ["ps0"], ["gog"], scale=0.5)
            yield
            dve(lambda e: e.tensor_tensor(out=st[:, 0:12], in0=PB[0][:, 384:396], in1=sbt[:, 910:922], op=ALU.add), ["ps0", "sbt"], ["st"])
            yield
            proj_tile(1, C_AV, C_SV)
            yield
            act(vat[:, v3c, :], PB[1][:, 0:128], AF.Copy, ["ps1"], [f"vat{v3c}"])
            yield
            act(u_sb[:, :], PB[1][:, 128:384], AF.Gelu_apprx_tanh, ["ps1"], ["u_sb"])
            yield
            proj_tile(0, C_SV, DPROJ)
            yield
            act(gv[:, :], PB[0][:, 0:256], AF.Gelu_apprx_tanh, ["ps0"], ["gv"])
            yield
            proj_tile(1, C_MV, C_MO)
            yield
            act(st[:, 12:18], st[:, 6:12], AF.Exp, ["st"], ["st"], scale=-1.0)
            yield
            act(st[:, 12:18], st[:, 12:18], AF.Ln, ["st"], ["st"], bias=1.0)
            yield
            T.group("pe", [lambda e: e.matmul(PB[0][:, 0:6], lhsT=U_f[:, :], rhs=st[:, 12:18], start=True, stop=True),
                           lambda e: e.matmul(PB[0][:, 8:14], lhsT=ones_f[:, :], rhs=st[:, 12:18], start=True, stop=True)],
                    ["U_f", "ones_f", "st"], ["ps0"])
            yield
            dve(lambda e: e.tensor_add(out=st[:, 18:24], in0=st[:, 0:6], in1=PB[0][:, 0:6]), ["st", "ps0"], ["st"])
            yield
            act(st[:, 24:30], st[:, 18:24], AF.Exp, ["st"], ["st"])
            yield
            act(st[:, 30:36], PB[0][:, 0:6], AF.Exp, ["ps0"], ["st"])
            yield
            ntot = PB[0][:, 8:14].rearrange("p (i two) -> p i two", two=2)
            act(st[0:64, 36:39], ntot[0:64, :, 0], AF.Exp, ["ps0"], ["st"], scale=-1.0)
            yield
            act(st[64:128, 36:39], ntot[64:128, :, 1], AF.Exp, ["ps0"], ["st"], scale=-1.0)
            yield
            act(st[0:64, 39:42], ntot[0:64, :, 0], AF.Exp, ["ps0"], ["st"], scale=-1.0, bias=math.log(0.125))
            yield
            act(st[64:128, 39:42], ntot[64:128, :, 1], AF.Exp, ["ps0"], ["st"], scale=-1.0, bias=math.log(0.125))
            yield
            dve(lambda e: e.scalar_tensor_tensor(out=gog[:, :], in0=gog[:, :], scalar=1.0, in1=sbt[:, 0:384],
                                                 op0=ALU.add, op1=ALU.mult), ["gog", "sbt"], ["gog"])
            yield
            dve(lambda e: e.tensor_tensor(out=vp[:, :, 0:64], in0=PB[1][:, 0:384].rearrange("p (h d) -> p h d", h=6),
                                          in1=st[:, 24:30].unsqueeze(2).to_broadcast([128, 6, 64]), op=ALU.mult),
                ["ps1", "st"], ["vp"])
            yield
            pool(lambda e: e.tensor_copy(out=vp[:, :, 64], in_=st[:, 24:30]), ["st"], ["vp"])
            yield
            proj_tile(0, C_AQ, C_AV)
            yield
            t3 = PB[0][:, :].rearrange("p (h d) -> p h d", h=8)
            cosb = cs[:, 0, c, :].unsqueeze(1).to_broadcast([128, 8, 8])
            sinb = cs[:, 1, c, :].unsqueeze(1).to_broadcast([128, 8, 8])
            dve(lambda e: e.tensor_tensor(out=tr4[:, 0], in0=t3[:, :, 0:8], in1=cosb, op=ALU.mult), ["ps0", "cs"], ["tr4a"])
            yield
            dve(lambda e: e.tensor_tensor(out=tr4[:, 1], in0=t3[:, :, 8:16], in1=sinb, op=ALU.mult), ["ps0", "cs"], ["tr4b"])
            yield
            dve(lambda e: e.tensor_tensor(out=tr4[:, 2], in0=t3[:, :, 8:16], in1=cosb, op=ALU.mult), ["ps0", "cs"], ["tr4c"])
            yield
            dve(lambda e: e.tensor_tensor(out=tr4[:, 3], in0=t3[:, :, 0:8], in1=sinb, op=ALU.mult), ["ps0", "cs"], ["tr4d"])
            yield
            aq3 = aqkb[:, :].rearrange("p (h d) -> p h d", h=8)
            act(aq3[:, :, 16:64], t3[:, :, 16:64], AF.Copy, ["ps0"], ["aqkb_c"])
            yield
            pool(lambda e: e.tensor_sub(out=aq3[:, :, 0:8], in0=tr4[:, 0], in1=tr4[:, 1]), ["tr4a", "tr4b"], ["aqkb_a"])
            yield
            pool(lambda e: e.tensor_add(out=aq3[:, :, 8:16], in0=tr4[:, 2], in1=tr4[:, 3]), ["tr4c", "tr4d"], ["aqkb_b"])
            yield
            AQKB = ["aqkb_a", "aqkb_b", "aqkb_c"]
            if c == 0 and l == 0 and DBG:
                dbgdump('g12', st[:, 0:12], ["st"])
                dbgdump('st2', st[:, 0:48], ["st"])
                dbgdump('vp', vp[:, :, :].rearrange("p h v -> p (h v)"), ["vp"])
                dbgdump('gog', gog[:, :], ["gog"])
                dbgdump('aqkb', aqkb[:, :], AQKB)
                dbgdump('u_sb', u_sb[:, :], ["u_sb"])
                dbgdump('gvg', gv[:, :], ["gv"])
            yield "B1"
            nk = 1 if c == 0 else 2

            def strandM():
                trb = pb_bf(2)
                T.group("pe", [lambda e, i=i: e.transpose(out=trb[:, i * 128:(i + 1) * 128], in_=qkT[:, 3 + i, off:off + 128],
                                                          identity=ident[:, :]) for i in range(3)],
                        ["qkT", "ident"], ["ps2"])
                yield
                act(ktok[:, :], trb[:, 0:384], AF.Copy, ["ps2"], ["ktok"])
                yield
                sc = PO[0]
                fns = []
                for h in range(6):
                    hp, i = h % 2, h // 2
                    fns.append(lambda e, h=h, hp=hp, i=i: e.matmul(
                        sc[:, hp * 512 + i * 128:hp * 512 + (i + 1) * 128], lhsT=qkT[hp * 64:(hp + 1) * 64, 3 + i, off:off + 128],
                        rhs=qkT[hp * 64:(hp + 1) * 64, i, off:off + 128], start=True, stop=True))
                T.group("pe", fns, ["qkT"], ["ps4", "ps5"])
                yield
                smT4 = smT[:, :, :].rearrange("p (i two) l -> p i two l", two=2)
                dve(lambda e: e.scalar_tensor_tensor(out=smT4[:, :, 0, :], in0=sc[:, 0:384].rearrange("p (h l) -> p h l", h=3),
                                                     scalar=0.125, in1=U_f[:, :].unsqueeze(1).to_broadcast([128, 3, 128]),
                                                     op0=ALU.mult, op1=ALU.mult), ["ps4", "U_f"], ["smT_a"])
                yield
                dve(lambda e: e.scalar_tensor_tensor(out=smT4[:, :, 1, :], in0=sc[:, 512:896].rearrange("p (h l) -> p h l", h=3),
                                                     scalar=0.125, in1=U_f[:, :].unsqueeze(1).to_broadcast([128, 3, 128]),
                                                     op0=ALU.mult, op1=ALU.mult), ["ps5", "U_f"], ["smT_b"])
                yield
                num = PB[3][:, 16:406].rearrange("p (h v) -> p h v", h=6)
                fns = []
                for h in range(6):
                    hp, i = h % 2, h // 2
                    last = (c == 0)
                    fns.append(lambda e, h=h, last=last: e.matmul(num[:, h, :], lhsT=smT[:, h, :], rhs=vp[:, h, :], start=True, stop=last))
                    if c > 0:
                        fns.append(lambda e, h=h, hp=hp, i=i: e.matmul(
                            num[:, h, :], lhsT=qkT[hp * 64:(hp + 1) * 64, i, off:off + 128],
                            rhs=CT8[hp * 64:(hp + 1) * 64, i, :], start=False, stop=True))
                T.group("pe", fns, ["smT_a", "smT_b", "vp", "qkT", "CT8"], ["ps3"])
                yield
                dcb = PB[2][:, 0:390].rearrange("p (i v) -> p i v", i=3)
                if c < NCH - 1:
                    T.group("pe", [lambda e, i=i: e.matmul(dcb[:, i, :], lhsT=ktok[:, i * 128:(i + 1) * 128],
                                                           rhs=vp[:, 2 * i:2 * i + 2, :].rearrange("p h v -> p (h v)"),
                                                           start=True, stop=True) for i in range(3)],
                            ["ktok", "vp"], ["ps2"])
                    yield
                act(st[:, 42:48], num[:, :, 64], AF.Copy, ["ps3"], ["st_d"])
                yield
                dve(lambda e: e.scalar_tensor_tensor(out=st[:, 48:54], in0=st[:, 42:48], scalar=-1.0, in1=st[:, 42:48],
                                                     op0=ALU.mult, op1=ALU.max), ["st_d"], ["st_r"])
                yield
                dve(lambda e: e.tensor_tensor(out=st[:, 48:54], in0=st[:, 48:54], in1=st[:, 30:36], op=ALU.max), ["st_r", "st"], ["st_r"])
                yield
                dve(lambda e: e.reciprocal(out=st[:, 48:54], in_=st[:, 48:54]), ["st_r"], ["st_r"])
                yield
                hm3 = hm[:, :].rearrange("p (h d) -> p h d", h=6)
                dve(lambda e: e.tensor_tensor(out=hm3, in0=num[:, :, 0:64], in1=st[:, 48:54].unsqueeze(2).to_broadcast([128, 6, 64]),
                                              op=ALU.mult), ["ps3", "st_r"], ["hm"])
                yield
                if c < NCH - 1:
                    if c == 0:
                        dve(lambda e: e.tensor_copy(out=Sf[0:64, :, :], in_=dcb[0:64, :, 0:65]), ["ps2"], ["Sf_a"])
                        yield
                        dve(lambda e: e.tensor_copy(out=Sf[64:128, :, :], in_=dcb[64:128, :, 65:130]), ["ps2"], ["Sf_b"])
                        yield
                    else:
                        dve(lambda e: e.tensor_add(out=Sf[0:64, :, :], in0=Sf[0:64, :, :], in1=dcb[0:64, :, 0:65]),
                            ["ps2", "Sf_a"], ["Sf_a"])
                        yield
                        dve(lambda e: e.tensor_add(out=Sf[64:128, :, :], in0=Sf[64:128, :, :], in1=dcb[64:128, :, 65:130]),
                            ["ps2", "Sf_b"], ["Sf_b"])
                        yield
                    pool(lambda e: e.tensor_tensor(out=CT8[:, :, :], in0=Sf[:, :, :],
                                                   in1=st[:, 39:42].unsqueeze(2).to_broadcast([128, 3, 65]), op=ALU.mult),
                         ["Sf_a", "Sf_b", "st"], ["CT8"])
                    yield
                    pool(lambda e: e.tensor_tensor(out=Sf[:, :, :], in0=Sf[:, :, :],
                                                   in1=st[:, 36:39].unsqueeze(2).to_broadcast([128, 3, 65]), op=ALU.mult),
                         ["Sf_a", "Sf_b", "st"], ["Sf_a", "Sf_b"])
                    yield
                for h in range(6):
                    dve(lambda e, h=h: e.bn_stats(out=bns[:, h, :], in_=hm[:, h * 64:(h + 1) * 64]), ["hm"], [f"bns{h}"])
                    yield
                for h in range(6):
                    dve(lambda e, h=h: e.bn_aggr(out=bna[:, h, :], in_=bns[:, h, :]), [f"bns{h}"], [f"bna{h}"])
                    yield
                BNA6 = [f"bna{h}" for h in range(6)]
                act(st[:, 54:60], bna[:, 0:6, 1], AF.Ln, BNA6, ["st_q"], bias=EPS)
                yield
                act(st[:, 54:60], st[:, 54:60], AF.Exp, ["st_q"], ["st_q"], scale=-0.5)
                yield
                dve(lambda e: e.tensor_tensor(out=hm3, in0=hm3, in1=bna[:, 0:6, 0:1].to_broadcast([128, 6, 64]), op=ALU.subtract),
                    ["hm"] + BNA6, ["hm"])
                yield
                dve(lambda e: e.tensor_tensor(out=hm3, in0=hm3, in1=st[:, 54:60].unsqueeze(2).to_broadcast([128, 6, 64]), op=ALU.mult),
                    ["hm", "st_q"], ["hm"])
                yield
                pool(lambda e: e.tensor_mul(out=hcat[:, 0:384], in0=hm[:, :], in1=gog[:, :]), ["hm", "gog"], ["hcat_a"])
                yield

            def strandW():
                po0b = PO[0][:, :].bitcast(BF16)
                trw = po0b[:, 1024:2048]
                trp = po0b[:, 0:1024]
                T.group("pe", [lambda e, h=h: e.transpose(out=trw[0:64, h * 128:(h + 1) * 128], in_=aqkb[:, h * 64:(h + 1) * 64],
                                                          identity=ident[:, :]) for h in range(8)],
                        AQKB + ["ident"], ["ps5"])
                yield
                dve(lambda e: e.tensor_copy(out=qTat[:, :, :], in_=trw[0:64, 0:768].rearrange("p (h t) -> p h t", h=6)),
                    ["ps5"], ["qTat"])
                yield
                act(kTat[:, :, slot * 128:(slot + 1) * 128], trw[0:64, 768:1024].rearrange("p (h t) -> p h t", h=2), AF.Copy,
                    ["ps5"], [f"kTat{slot}"])
                yield
                for g in range(2):
                    scb = PO[1]
                    if nk == 2:
                        kview = kTat[:, g, :]
                        mview = mask3[:, 128:384] if slot == 1 else mask3[:, 0:256]
                        W = 256
                    else:
                        kview = kTat[:, g, slot * 128:(slot + 1) * 128]
                        mview = mask3[:, 0:128]
                        W = 128
                    fns = []
                    for hh in range(3):
                        h = 3 * g + hh
                        o = scb[:, hh * 256:hh * 256 + W]
                        fns.append(lambda e, o=o, h=h, kview=kview: e.matmul(o, lhsT=qTat[:, h, :], rhs=kview, start=True, stop=False))
                        fns.append(lambda e, o=o, mview=mview: e.matmul(o, lhsT=ident[:, :], rhs=mview, start=False, stop=True))
                    T.group("pe", fns, ["qTat", "kTat0", "kTat1", "ident"] + MASK3, ["ps6", "ps7"])
                    yield
                    sc3 = scb[:, 0:768].rearrange("p (h k) -> p h k", h=3)[:, :, 0:W]
                    mxc = st[:, 70:73]
                    dve(lambda e: e.tensor_reduce(out=mxc, in_=sc3, axis=AX.X, op=ALU.max), ["ps6", "ps7"], ["st_m"])
                    yield
                    dve(lambda e: e.scalar_tensor_tensor(out=mxc, in0=mxc, scalar=0.125, in1=sbt[:, 384 + 3 * g:387 + 3 * g],
                                                         op0=ALU.mult, op1=ALU.max), ["st_m", "sbt"], ["st_m"])
                    yield
                    dve(lambda e: e.tensor_scalar(out=st[:, 73:76], in0=mxc, scalar1=-1.0, scalar2=None, op0=ALU.mult), ["st_m"], ["st_n"])
                    yield
                    for hh in range(3):
                        act(pexp[:, hh, 0:W], scb[:, hh * 256:hh * 256 + W], AF.Exp, ["ps6", "ps7", "st_n"], [f"pexp{hh}", "hcatT_a", "hcatT_b"],
                            bias=st[:, 73 + hh:74 + hh], scale=0.125, accum_out=st[:, 76 + hh:77 + hh])
                        yield
                    dve(lambda e: e.tensor_add(out=st[:, 79:82], in0=st[:, 73:76], in1=sbt[:, 384 + 3 * g:387 + 3 * g]), ["st_n", "sbt"], ["st_k"])
                    yield
                    act(st[:, 79:82], st[:, 79:82], AF.Exp, ["st_k"], ["st_k"])
                    yield
                    dve(lambda e: e.tensor_add(out=st[:, 79:82], in0=st[:, 79:82], in1=st[:, 76:79]),
                        ["st_k", "pexp0", "pexp1", "pexp2"], ["st_k"])
                    yield
                    dve(lambda e: e.reciprocal(out=st[:, 79:82], in_=st[:, 79:82]), ["st_k"], ["st_k"])
                    yield
                    fns = []
                    for hh in range(3):
                        for b in range(nk):
                            fns.append(lambda e, hh=hh, b=b: e.transpose(out=trp[:, (hh * 2 + b) * 128:(hh * 2 + b + 1) * 128],
                                                                         in_=pexp[:, hh, b * 128:(b + 1) * 128], identity=ident[:, :]))
                    T.group("pe", fns, ["pexp0", "pexp1", "pexp2", "ident"], ["ps4"])
                    yield
                    pTf = wbuf[:, 768:1536]
                    if nk == 2:
                        act(pTf[:, 0:384], trp[:, 0:384], AF.Copy, ["ps4"], ["pT_a", "hcatT_a", "hcatT_b"])
                        yield
                        dve(lambda e: e.tensor_copy(out=pTf[:, 384:768], in_=trp[:, 384:768]), ["ps4"], ["pT_b", "hcatT_a", "hcatT_b"])
                        yield
                    else:
                        act(pT[:, :, 0, :], trp[:, 0:768].rearrange("p (h b q) -> p h b q", h=3, b=2)[:, :, 0, :], AF.Copy,
                            ["ps4"], ["pT_a", "hcatT_a", "hcatT_b"])
                        yield
                    ob = PO[0][:, 512:704].rearrange("p (h d) -> p h d", h=3)
                    fns = []
                    for hh in range(3):
                        for b in range(nk):
                            sl = v3c if (nk == 1 or b == slot) else v3p
                            fns.append(lambda e, hh=hh, b=b, sl=sl, g=g: e.matmul(ob[:, hh, :], lhsT=pT[:, hh, b, :],
                                                                                  rhs=vat[:, sl, g * 64:(g + 1) * 64],
                                                                                  start=(b == 0), stop=(b == nk - 1)))
                    T.group("pe", fns, ["pT_a", "pT_b", "vat0", "vat1", "vat2"], ["ps5"])
                    yield
                    hb = hcat[:, 384 + g * 192:384 + (g + 1) * 192].rearrange("p (h d) -> p h d", h=3)
                    dve(lambda e, hb=hb, ob=ob: e.tensor_tensor(out=hb, in0=ob, in1=st[:, 79:82].unsqueeze(2).to_broadcast([128, 3, 64]),
                                                                 op=ALU.mult), ["ps5", "st_k"], [f"hcat_b{g}"])
                    yield

            def strandG():
                gv3 = gv[:, :].rearrange("p (g d) -> p g d", g=4)
                for g in range(4):
                    dve(lambda e, g=g: e.bn_stats(out=bns[:, 8 + g, :], in_=gv[:, g * 64:(g + 1) * 64]), ["gv"], [f"bns{8 + g}"])
                    yield
                for g in range(4):
                    dve(lambda e, g=g: e.bn_aggr(out=bna[:, 8 + g, :], in_=bns[:, 8 + g, :]), [f"bns{8 + g}"], [f"bna{8 + g}"])
                    yield
                BNA4 = [f"bna{8 + g}" for g in range(4)]
                act(st[:, 66:70], bna[:, 8:12, 1], AF.Ln, BNA4, ["st_s"], bias=EPS)
                yield
                act(st[:, 66:70], st[:, 66:70], AF.Exp, ["st_s"], ["st_s"], scale=-0.5)
                yield
                dve(lambda e: e.tensor_tensor(out=gv3, in0=gv3, in1=bna[:, 8:12, 0:1].to_broadcast([128, 4, 64]), op=ALU.subtract),
                    ["gv"] + BNA4, ["gv"])
                yield
                dve(lambda e: e.tensor_tensor(out=gv3, in0=gv3, in1=st[:, 66:70].unsqueeze(2).to_broadcast([128, 4, 64]), op=ALU.mult),
                    ["gv", "st_s"], ["gv"])
                yield
                pool(lambda e: e.tensor_mul(out=gv[:, :], in0=gv[:, :], in1=sbt[:, 390:646]), ["gv", "sbt"], ["gv"])
                yield
                pool(lambda e: e.tensor_add(out=vn[:, :], in0=gv[:, :], in1=sbt[:, 646:902]), ["gv", "sbt"], ["vn"])
                yield
                mxb = PO[0][:, 768:1024].rearrange("p (g d) -> p g d", g=4)
                T.group("pe", [lambda e, g=g: e.matmul(mxb[:, g, :], lhsT=wsT[:, g, :], rhs=vn[:, g * 64:(g + 1) * 64], start=True, stop=True)
                               for g in range(4)], ["wsT", "vn"], ["ps5"])
                yield
                dve(lambda e: e.tensor_tensor(out=gv3, in0=mxb, in1=ppt[:, 30:34].unsqueeze(2).to_broadcast([128, 4, 64]), op=ALU.add),
                    ["ps5", "ppt"], ["gv"])
                yield
                pool(lambda e: e.tensor_mul(out=hcat[:, 768:1024], in0=gv[:, :], in1=u_sb[:, :]), ["gv", "u_sb"], ["hcat_c"])
                yield

            strands = [strandM(), strandW(), strandG()]
            while strands:
                for sgen in list(strands):
                    try:
                        next(sgen)
                        yield
                    except StopIteration:
                        strands.remove(sgen)
            HCAT = ["hcat_a", "hcat_b0", "hcat_b1", "hcat_c"]
            yield "S"
            if c == 0 and l == 0 and DBG:
                dbgdump('hcat_a', hcat[:, 0:384], ["hcat_a"])
                dbgdump('hcat_b', hcat[:, 384:768], ["hcat_b0", "hcat_b1"])
                dbgdump('hcat_c', hcat[:, 768:1024], ["hcat_c"])
                dbgdump('st10', st[:, 0:96], ["st", "st_d", "st_r", "st_q"])
                dbgdump('st11', st[:, 0:96], ["st_k", "st_m", "st_n"])
            trb = pb_bf(2)
            T.group("pe", [lambda e, k=k: e.transpose(out=trb[:, k * 128:(k + 1) * 128], in_=hcat[:, k * 128:(k + 1) * 128],
                                                      identity=ident[:, :]) for k in range(8)], HCAT + ["ident"], ["ps2"])
            yield
            hTf = wbuf[:, 0:1024]
            act(hTf[:, 0:512], trb[:, 0:512], AF.Copy, ["ps2"], ["hcatT_a"] + WB)
            yield
            dve(lambda e: e.tensor_copy(out=hTf[:, 512:1024], in_=trb[:, 512:1024]), ["ps2"], ["hcatT_b"] + WB)
            yield
            if c == 0 and l == 0 and DBG:
                dbgdump('hcatT', wbuf[:, 0:1024], ["hcatT_a", "hcatT_b"])
                dbgdump('hcat_all', hcat[:, :], HCAT)
                dbgdump('wout0', w_out_sb[:, 0, :], ["w_out_sb"])
            mixp = PO[1]
            for n in range(2):
                mm_group([(mixp[:, n * 512:(n + 1) * 512], hcatT[:, k, :], w_out_sb[:, k, n * 512:(n + 1) * 512]) for k in range(8)],
                         ["hcatT_a", "hcatT_b", "w_out_sb"], [RO[1][n]])
                yield
            xt = x_tok[:, c, :]
            XR = f"x_tok{c}"
            if l == 0:
                dve(lambda e: e.scalar_tensor_tensor(out=xt, in0=xt, scalar=ALPHA, in1=mixp[:, :], op0=ALU.mult, op1=ALU.add),
                    ["x_tok", XR, "ps6", "ps7"], [XR])
            else:
                dve(lambda e: e.tensor_add(out=xt, in0=xt, in1=mixp[:, :]), ["x_tok", XR, "ps6", "ps7"], [XR])
            yield
            for _ in layer_norm_gen(c, XR, lnp, "lnp", to_xT=True, xscale=1.0 / ALPHA, k=0):
                yield

        lnst = sb("lnst", [128, 4, 2, 6])
        lnag = sb("lnag", [128, 4, 2])
        lnsc = sb("lnsc", [128, 4, 2])
        XNB = [(hcat[:, :], ["hcat_a", "hcat_b0", "hcat_b1", "hcat_c"]),
               (qkT[:, 4:6, :].rearrange("p a t -> p (a t)"), ["qkT"]),
               (qkT[:, 0:2, :].rearrange("p a t -> p (a t)"), ["qkT"]),
               (qkT[:, 2:4, :].rearrange("p a t -> p (a t)"), ["qkT"])]

        def layer_norm_tile(c, XR, prm, PR, to_xT, out_dma=False, xscale=1.0, k=0):
            for _ in layer_norm_gen(c, XR, prm, PR, to_xT, out_dma, xscale, k):
                pass

        def layer_norm_gen(c, XR, prm, PR, to_xT, out_dma=False, xscale=1.0, k=0):
            xt = x_tok[:, c, :]
            SA, SG, SC, SC2 = f"lnst{k}", f"lnag{k}", f"lnsc{k}", f"lnsd{k}"
            for i in range(2):
                dve(lambda e, i=i: e.bn_stats(out=lnst[:, k, i, :], in_=xt[:, i * 512:(i + 1) * 512]), [XR], [SA + "ab"[i]])
                yield
            dve(lambda e: e.bn_aggr(out=lnag[:, k, :], in_=lnst[:, k, :, :]), [SA + "a", SA + "b"], [SG])
            yield
            act(lnsc[:, k, 0:1], lnag[:, k, 1:2], AF.Ln, [SG], [SC], bias=EPS)
            yield
            act(lnsc[:, k, 0:1], lnsc[:, k, 0:1], AF.Exp, [SC], [SC], scale=-0.5)
            yield
            dve(lambda e: e.scalar_tensor_tensor(out=lnsc[:, k, 1:2], in0=lnag[:, k, 0:1], scalar=-1.0, in1=lnsc[:, k, 0:1],
                                                 op0=ALU.mult, op1=ALU.mult), [SG, SC], [SC2])
            yield
            act(xt, xt, AF.Identity, [XR, SC, SC2], [XR], scale=lnsc[:, k, 0:1], bias=lnsc[:, k, 1:2])
            yield
            dve(lambda e: e.tensor_mul(out=xt, in0=xt, in1=prm[:, 0:D]), [XR, PR], [XR])
            yield
            dve(lambda e: e.tensor_add(out=xt, in0=xt, in1=prm[:, D:2 * D]), [XR, PR], [XR])
            yield
            if to_xT:
                xnb, XN = XNB[k]
                act(xnb, xt, AF.Copy, [XR], XN, scale=xscale)
                yield
                bank = 2 + (k % 2)
                trb = pb_bf(bank)
                T.group("pe", [lambda e, kk=kk: e.transpose(out=trb[:, kk * 128:(kk + 1) * 128], in_=xnb[:, kk * 128:(kk + 1) * 128],
                                                            identity=ident[:, :]) for kk in range(8)],
                        XN + ["ident"], [RB[bank]])
                tr3 = trb[:, :].rearrange("p (k t) -> p k t", k=8)
                act(xT[:, 0:4, c * 128:(c + 1) * 128], tr3[:, 0:4, :], AF.Copy, [RB[bank]], [f"xT{c}"])
                dve(lambda e: e.tensor_copy(out=xT[:, 4:8, c * 128:(c + 1) * 128], in_=tr3[:, 4:8, :]), [RB[bank], f"xT{c}"], [f"xT{c}"])
                yield
            if out_dma:
                T.dma("sp", [(y_d[c * 128:(c + 1) * 128, :], xt)], [XR], [], "d_out")
                yield

        def dump_x():
            T.wait_all("sp", ["d_dbg"])
            for c in range(NCH):
                T.dma("sp", [(y_d[c * 128:(c + 1) * 128, :], x_tok[:, c, :])], ["x_tok", f"x_tok{c}"], [], "d_out")
            T.wait_all("sp", ["d_out"])

        rt = sb("rt", [128, 64])
        sgs = [pconv[:, 0:512], yconv[:, :]]
        SGR = ["pconv", "yconv"]
        comb = hm[:, 0:128].rearrange("p (t e) -> p t e", e=NE)

        def ffn_phase(l, is_moe, last, final_layer):
            with nc.allow_non_contiguous_dma(reason="small parameter loads"):
                T.dma("sp", [(lnp[:, :], lnp_d[l, 1].partition_broadcast(128))], [], ["lnp"], "d_lnp")
            if not final_layer:
                dve(lambda e: e.tensor_scalar(out=lnp[:, :], in0=lnp[:, :], scalar1=ALPHA, scalar2=None, op0=ALU.mult), ["lnp"], ["lnp"])
            if is_moe:
                router()
            groups = []
            if is_moe:
                for ex in range(NE):
                    for f0 in range(0, DFE, FG):
                        groups.append((mg_d[0, ex], mu_d[0, ex], md_d[0, ex], f0, min(FG, DFE - f0), ex))
            else:
                for f0 in range(0, DFF, FG):
                    groups.append((fg_d[0], fu_d[0], fd_d[0], f0, min(FG, DFF - f0), None))

            def wviews(b):
                base = b * 12288
                wg = arena[:, base:base + 4096].rearrange("p (k f) -> p k f", k=8)
                wu = arena[:, base + 4096:base + 8192].rearrange("p (k f) -> p k f", k=8)
                wd = arena[:, base + 8192:base + 12288].rearrange("p (j n) -> p j n", j=4)
                return wg, wu, wd

            def hview(i):
                base = 24576 + i * 2048
                return arena[:, base:base + 2048].rearrange("p (j t) -> p j t", j=4)

            def load(gi):
                gsrc, usrc, dsrc, f0, F, ex = groups[gi]
                b = gi % 2
                wg, wu, wd = wviews(b)
                nj = F // 128
                T.dma("pool", [(wg[:, :, 0:F], gsrc[:, f0:f0 + F].rearrange("(k p) f -> p k f", p=128))], [], [f"wg{b}", "w_in_sb", "w_in_qk", "w_out_sb"], f"d_wg{b}")
                T.dma("pool", [(wu[:, :, 0:F], usrc[:, f0:f0 + F].rearrange("(k p) f -> p k f", p=128))], [], [f"wu{b}", "w_in_sb", "w_in_qk", "w_out_sb"], f"d_wu{b}")
                T.dma("pool", [(wd[:, 0:nj, :], dsrc[f0:f0 + F, :].rearrange("(j p) n -> p j n", p=128))], [], [f"wd{b}", "w_in_sb", "w_in_qk", "w_out_sb"], f"d_wd{b}")

            steps = [(gi, tb) for gi in range(len(groups)) for tb in range(4)]
            state = {"gu": 0, "out": 0}
            ln_active = []

            def pump(n):
                for _ in range(n):
                    for g_ in list(ln_active):
                        try:
                            next(g_)
                        except StopIteration:
                            ln_active.remove(g_)

            def GU(si):
                gi, tb = steps[si]
                F = groups[gi][4]
                b = gi % 2
                wg, wu, _ = wviews(b)
                hT = hview(si % 2)
                for j in range(F // 128):
                    p = state["gu"] % 2
                    state["gu"] += 1
                    gb, ub = PB[2 * p], PB[2 * p + 1]
                    mm_group([(gb[:, :], wg[:, k, j * 128:(j + 1) * 128], xT[:, k, tb * 512:(tb + 1) * 512]) for k in range(8)],
                             [f"wg{b}"] + [f"xT{4 * tb + t}" for t in range(4)], [RB[2 * p]])
                    mm_group([(ub[:, :], wu[:, k, j * 128:(j + 1) * 128], xT[:, k, tb * 512:(tb + 1) * 512]) for k in range(8)],
                             [f"wu{b}"] + [f"xT{4 * tb + t}" for t in range(4)], [RB[2 * p + 1]])
                    act(sgs[p], gb[:, :], AF.Silu, [RB[2 * p]], [SGR[p]])
                    dve(lambda e, p=p, j=j, ub=ub, hT=hT: e.tensor_tensor(out=hT[:, j, :], in0=sgs[p], in1=ub[:, :], op=ALU.mult),
                        [SGR[p], RB[2 * p + 1]], [f"hT{si % 2}_{j}"])
                    pump(1)

            def DOWN(si):
                gi, tb = steps[si]
                F = groups[gi][4]
                ex = groups[gi][5]
                b = gi % 2
                _, _, wd = wviews(b)
                hT = hview(si % 2)
                nj = F // 128
                final = (gi == len(groups) - 1)
                for t in range(4):
                    c = tb * 4 + t
                    q = state["out"] % 2
                    state["out"] += 1
                    for n in range(2):
                        mm_group([(PO[q][:, n * 512:(n + 1) * 512], hT[:, j, t * 128:(t + 1) * 128], wd[:, j, n * 512:(n + 1) * 512])
                                  for j in range(nj)], [f"hT{si % 2}_{j}" for j in range(nj)] + [f"wd{b}"], [RO[q][n]])
                    xt = x_tok[:, c, :]
                    XR = f"x_tok{c}"
                    if ex is None:
                        dve(lambda e, xt=xt, q=q: e.tensor_add(out=xt, in0=xt, in1=PO[q][:, :]), [XR, RO[q][0], RO[q][1]], [XR])
                    else:
                        dve(lambda e, xt=xt, q=q, c=c, ex=ex: e.scalar_tensor_tensor(
                            out=xt, in0=PO[q][:, :], scalar=comb[:, c, ex:ex + 1], in1=xt, op0=ALU.mult, op1=ALU.add),
                            [XR, RO[q][0], RO[q][1], "hm"], [XR])
                    if final:
                        if len(ln_active) >= 4:
                            for _ in ln_active.pop(0):
                                pass
                        ln_active.append(layer_norm_gen(c, XR, lnp, "lnp", to_xT=(not last), out_dma=last,
                                                        xscale=(1.0 if final_layer else 1.0 / ALPHA), k=c % 4))
                    pump(2)

            load(0)
            if len(groups) > 1:
                load(1)
            for si in range(len(steps)):
                gi, tb = steps[si]
                GU(si)
                if si > 0:
                    DOWN(si - 1)
                    pgi, ptb = steps[si - 1]
                    if ptb == 3 and pgi + 2 < len(groups):
                        load(pgi + 2)
            DOWN(len(steps) - 1)
            while ln_active:
                pump(1)

        def router():
            lg = hm[:, 128:256].rearrange("p (t e) -> p t e", e=NE)
            for c in range(NCH):
                mm_group([(PB[c % 2][:, 0:NE], xT[:, k, c * 128:(c + 1) * 128], wr_sb[:, k, :]) for k in range(8)],
                         [f"xT{c}", "wr_sb"], [RB[c % 2]])
                dve(lambda e, c=c: e.tensor_tensor(out=lg[:, c, :], in0=PB[c % 2][:, 0:NE], in1=sbt[:, 902:910], op=ALU.add),
                    [RB[c % 2], "sbt"], ["hm"])
            m1 = rt[:, 0:16]
            m2 = rt[:, 16:32]
            e21 = rt[:, 32:48]
            g1 = rt[:, 48:64]
            oh1 = hm[:, 256:384].rearrange("p (t e) -> p t e", e=NE)
            l2 = gv[:, 0:128].rearrange("p (t e) -> p t e", e=NE)
            oh2 = gv[:, 128:256].rearrange("p (t e) -> p t e", e=NE)
            dve(lambda e: e.tensor_reduce(out=m1, in_=lg, axis=AX.X, op=ALU.max), ["hm"], ["rt"])
            dve(lambda e: e.tensor_tensor(out=oh1, in0=lg, in1=m1.unsqueeze(2).to_broadcast([128, NCH, NE]), op=ALU.is_equal),
                ["hm", "rt"], ["hm"])
            dve(lambda e: e.scalar_tensor_tensor(out=l2, in0=oh1, scalar=-1e30, in1=lg, op0=ALU.mult, op1=ALU.add),
                ["hm", "hm"], ["gv"])
            dve(lambda e: e.tensor_reduce(out=m2, in_=l2, axis=AX.X, op=ALU.max), ["gv"], ["rt"])
            dve(lambda e: e.tensor_tensor(out=oh2, in0=l2, in1=m2.unsqueeze(2).to_broadcast([128, NCH, NE]), op=ALU.is_equal),
                ["gv", "rt"], ["gv"])
            dve(lambda e: e.tensor_sub(out=e21, in0=m2, in1=m1), ["rt", "rt"], ["rt"])
            act(e21, e21, AF.Exp, ["rt"], ["rt"])
            dve(lambda e: e.tensor_scalar(out=g1, in0=e21, scalar1=1.0, scalar2=None, op0=ALU.add), ["rt"], ["rt"])
            dve(lambda e: e.reciprocal(out=g1, in_=g1), ["rt"], ["rt"])
            dve(lambda e: e.tensor_mul(out=e21, in0=e21, in1=g1), ["rt", "rt"], ["rt"])
            dve(lambda e: e.tensor_tensor(out=oh1, in0=oh1, in1=g1.unsqueeze(2).to_broadcast([128, NCH, NE]), op=ALU.mult),
                ["hm", "rt"], ["hm"])
            dve(lambda e: e.tensor_tensor(out=oh2, in0=oh2, in1=e21.unsqueeze(2).to_broadcast([128, NCH, NE]), op=ALU.mult),
                ["gv", "rt"], ["gv"])
            dve(lambda e: e.tensor_add(out=comb, in0=oh1, in1=oh2), ["hm", "gv"], ["hm"])

        with nc.allow_non_contiguous_dma(reason="small parameter loads"):
            T.dma("pool", [(wr_sb[:, :, :], wr_d[0].rearrange("(k p) e -> p k e", p=128))], [], ["wr_sb"], "d_wr")
        done = False
        if stage == "s0":
            dump_x()
            done = True
        for l in range(DEPTH if stage != "s0" else 0):
            is_moe = (l % 2 == 1)
            last = (l == DEPTH - 1)
            with nc.allow_non_contiguous_dma(reason="small parameter loads"):
                T.dma("sp", [(ppt[:, :], pp_d[l])], [], ["ppt"], "d_ppt")
                T.dma("sp", [(sbt[:, :], sbr_d[l].partition_broadcast(128))], [], ["sbt"], "d_sbt")
                T.dma("sp", [(lnp[:, :], lnp_d[l, 0].partition_broadcast(128))], [], ["lnp"], "d_lnp")
                load_bias_rows(l)
            if l == 0:
                for c in range(NCH):
                    T.dma("sp", [(x_tok[:, c, :], x_d[c * 128:(c + 1) * 128, :])], [], [f"x_tok{c}"], f"d_x{c}")
            dve(lambda e: e.tensor_scalar(out=lnp[:, :], in0=lnp[:, :], scalar1=ALPHA, scalar2=None, op0=ALU.mult), ["lnp"], ["lnp"])
            dve(lambda e: e.tensor_scalar(out=sbt[:, 0:384], in0=sbt[:, 0:384], scalar1=0.5, scalar2=None, op0=ALU.mult), ["sbt"], ["sbt"])
            load_mixer_weights(l)
            if l == 0:
                for c in range(NCH):
                    XR = f"x_tok{c}"
                    act(hcat[:, :], x_tok[:, c, :], AF.Copy, ["x_tok", XR], ["hcat_a", "hcat_b0", "hcat_b1", "hcat_c"])
                    trb = pb_bf(2)
                    T.group("pe", [lambda e, k=k: e.transpose(out=trb[:, k * 128:(k + 1) * 128], in_=hcat[:, k * 128:(k + 1) * 128],
                                                              identity=ident[:, :]) for k in range(8)],
                            ["hcat_a", "hcat_b0", "hcat_b1", "hcat_c", "ident"], ["ps2"])
                    tr3 = trb[:, :].rearrange("p (k t) -> p k t", k=8)
                    act(xT[:, 0:4, c * 128:(c + 1) * 128], tr3[:, 0:4, :], AF.Copy, ["ps2"], [f"xT{c}"])
                    dve(lambda e, c=c, tr3=tr3: e.tensor_copy(out=xT[:, 4:8, c * 128:(c + 1) * 128], in_=tr3[:, 4:8, :]), ["ps2", f"xT{c}"], [f"xT{c}"])
            pool(lambda e: e.memset(carry[:, :, :], 0.0), [], ["carry"])
            if stage == "s1":
                dump_x()
                done = True
                break
            def step(gen, sfx):
                T.suffix = sfx
                try:
                    r = next(gen)
                except StopIteration:
                    r = "END"
                T.suffix = "0"
                return r

            def chain2(g1, g2):
                for v in g1:
                    yield v
                for v in g2:
                    yield v

            prev = None
            for c in range(NCH):
                in_tail0 = False
                if c % 4 == 0:
                    cur = (chain2(phaseA(l, c // 4), phaseB(l, c, is_moe)), str(c % 2))
                    if prev is not None:
                        for _ in range(build.hold):
                            r = step(*prev)
                            if r == "END":
                                prev = None
                                break
                            if r == "S":
                                in_tail0 = True
                else:
                    cur = (phaseB(l, c, is_moe), str(c % 2))
                b1_done = False
                in_tail = in_tail0
                while not b1_done:
                    if prev is None:
                        if step(*cur) == "B1":
                            b1_done = True
                        continue
                    n_prev = 1 if in_tail else (build.ratio_b if c % 4 == 0 else build.ratio)
                    for _ in range(n_prev):
                        r = step(*prev)
                        if r == "END":
                            prev = None
                            break
                        if r == "S":
                            in_tail = True
                            break
                    for _ in range(build.dens if in_tail else 1):
                        if step(*cur) == "B1":
                            b1_done = True
                            break
                if prev is not None:
                    while step(*prev) != "END":
                        pass
                prev = cur
            while step(*prev) != "END":
                pass
            if stage == f"mix{l}":
                dump_x()
                done = True
                break
            ffn_phase(l, is_moe, last and stage == "full", last)
            if stage == f"l{l}" and not (last and stage == "full"):
                if DBG:
                    dbgdump('xT_k0', xT[:, 0, 0:1024], [f"xT{t}" for t in range(8)])
                    dbgdump('xT_k5', xT[:, 5, 0:1024], [f"xT{t}" for t in range(8)])
                dump_x()
                done = True
                break
        if not done:
            T.wait_all("sp", ["d_out"])
        build.ninst = T.nins
        build.sbuf_left = nc.sbuf_bytes_remaining
    return nc


_NC_CACHE = {}


def _host_layout(inp, b):
    f32 = np.float32
    pp = np.zeros((DEPTH, 128, NPP), f32)
    sbr = np.zeros((DEPTH, 1, NSB), f32)
    lnpv = np.zeros((DEPTH, 2, 1, 2 * D), f32)
    btok = np.zeros((DEPTH, 1, NTOKC), f32)
    for l in range(DEPTH):
        cw = np.asarray(inp["conv_w"][l], f32)
        pp[l, :, 0:24] = cw.reshape(4, 6, 128).transpose(2, 1, 0).reshape(128, 24)
        pp[l, :, 24:30] = np.asarray(inp["b_in"][l, 0:768], f32).reshape(6, 128).T
        pp[l, :, 30:34] = np.asarray(inp["sgu_b_s"][l], f32).T
        sbr[l, 0, 0:384] = inp["mlstm_norm_g"][l]
        sbr[l, 0, 384:390] = inp["attn_sinks"][l]
        sbr[l, 0, 390:646] = inp["sgu_norm_g"][l]
        sbr[l, 0, 646:902] = inp["sgu_norm_b"][l]
        if l % 2 == 1:
            sbr[l, 0, 902:910] = inp["moe_b_router"][l // 2]
        sbr[l, 0, 910:922] = inp["b_in"][l, C_MI:C_AQ]
        lnpv[l, 0, 0, 0:D] = inp["ln1_g"][l]
        lnpv[l, 0, 0, D:] = inp["ln1_b"][l]
        lnpv[l, 1, 0, 0:D] = inp["ln2_g"][l]
        lnpv[l, 1, 0, D:] = inp["ln2_b"][l]
        btok[l, 0, :] = inp["b_in"][l, TOKC0:]
    shared = {
        "w_in": np.ascontiguousarray(inp["w_in"], f32), "pp": pp, "sbr": sbr, "lnp": lnpv, "btok": btok,
        "sgu_w_s": np.ascontiguousarray(inp["sgu_w_s"], f32), "w_out": np.ascontiguousarray(inp["w_out"], f32),
        "ffn_w_gate": np.ascontiguousarray(inp["ffn_w_gate"], f32), "ffn_w_up": np.ascontiguousarray(inp["ffn_w_up"], f32),
        "ffn_w_down": np.ascontiguousarray(inp["ffn_w_down"], f32), "moe_w_router": np.ascontiguousarray(inp["moe_w_router"], f32),
        "moe_w_gate": np.ascontiguousarray(inp["moe_w_gate"], f32), "moe_w_up": np.ascontiguousarray(inp["moe_w_up"], f32),
        "moe_w_down": np.ascontiguousarray(inp["moe_w_down"], f32),
    }
    return shared


def make_in_maps(inp, cores):
    shared = _host_layout(inp, 0)
    maps = []
    for b in cores:
        m = dict(shared)
        m["x"] = np.ascontiguousarray(inp["x"][b], np.float32)
        m["posT"] = np.ascontiguousarray(np.asarray(inp["positions"][b], np.int32).reshape(NCH, 128).T)
        maps.append(m)
    return maps


def kernel(**inputs):
    if "full" not in _NC_CACHE:
        _NC_CACHE["full"] = build("full")
    nc = _NC_CACHE["full"]
    in_maps = make_in_maps(inputs, list(range(NCORES)))
    res = run_bass_kernel_spmd(nc, in_maps, core_ids=list(range(NCORES)))
    return np.stack([np.asarray(r["y"], np.float32) for r in res.results], axis=0)
```

```python
import math
from contextlib import ExitStack

import numpy as np
import concourse.bass as bass
import concourse.mybir as mybir
from concourse.bass_utils import run_bass_kernel_spmd

F32 = mybir.dt.float32
BF16 = mybir.dt.bfloat16
I32 = mybir.dt.int32
AF = mybir.ActivationFunctionType
ALU = mybir.AluOpType
AX = mybir.AxisListType

NCORES = 8
D = 1024
S = 2048
NCH = 16
DEPTH = 2
DPROJ = 2700
C_MQ, C_MK, C_MV, C_MO, C_MI, C_MF, C_AQ, C_AK, C_AV, C_SU, C_SV = (
    0, 384, 768, 1152, 1536, 1542, 1548, 1932, 2060, 2188, 2444)
TOKC0 = 768
NTOKC = DPROJ - TOKC0
DFF = 2816
NE = 8
DFE = 3584
ALPHA = (2.0 * DEPTH) ** 0.25
EPS = 1e-5
ROPE_THETA = 500000.0
NEG = -30000.0
NPP = 34
NSB = 922
FG = 512


class Tracker:
    def __init__(self, nc, es):
        self.nc = nc
        self.es = es
        self.eng = {"pe": nc.tensor, "dve": nc.vector, "act": nc.scalar, "pool": nc.gpsimd, "sp": nc.sync}
        self.sem = {}
        self.cnt = {}
        self.waited = {k: {} for k in self.eng}
        for k in ("pe", "dve", "act", "pool"):
            self.sem[k] = es.enter_context(nc.semaphore("sem_" + k))
            self.cnt[k] = 0
        self.res = {}
        self.nins = 0
        self.mangle_names = frozenset()
        self.suffix = "0"

    def _m(self, names):
        mn = self.mangle_names
        sfx = self.suffix
        return [n + sfx if n in mn else n for n in names]

    @staticmethod
    def _excl(reads, writes):
        ps = [r for r in reads if r.startswith("ps")]
        if ps:
            reads = [r for r in reads if not r.startswith("ps")]
            writes = list(writes) + ps
        return reads, writes

    def _deps(self, e, reads, writes):
        need = {}

        def add(tok, skip_same):
            if tok is None:
                return
            k, v = tok
            if skip_same and k == e:
                return
            if need.get(k, 0) < v:
                need[k] = v

        for r in reads:
            st = self.res.get(r)
            if st is not None:
                add(st[0], False)
        for w in writes:
            st = self.res.get(w)
            if st is not None:
                add(st[0], True)
                for k, v in st[1].items():
                    add((k, v), True)
        return need

    def _wait(self, e, need):
        wd = self.waited[e]
        for k, v in need.items():
            if wd.get(k, 0) < v:
                self.eng[e].wait_ge(self.sem[k], v)
                wd[k] = v

    def _commit(self, tok, reads, writes):
        for r in reads:
            st = self.res.get(r)
            if st is None:
                st = [None, {}]
                self.res[r] = st
            if st[1].get(tok[0], 0) < tok[1]:
                st[1][tok[0]] = tok[1]
        for w in writes:
            self.res[w] = [tok, {}]

    def op(self, e, fn, reads=(), writes=()):
        reads, writes = self._excl(self._m(reads), self._m(writes))
        self._wait(e, self._deps(e, reads, writes))
        ins = fn(self.eng[e])
        self.cnt[e] += 1
        ins.then_inc(self.sem[e], 1)
        self.nins += 1
        self._commit((e, self.cnt[e]), reads, writes)

    def group(self, e, fns, reads=(), writes=()):
        reads, writes = self._excl(self._m(reads), self._m(writes))
        self._wait(e, self._deps(e, reads, writes))
        ins = None
        for fn in fns:
            ins = fn(self.eng[e])
            self.nins += 1
        self.cnt[e] += 1
        ins.then_inc(self.sem[e], 1)
        self._commit((e, self.cnt[e]), reads, writes)

    def dma(self, q, pairs, reads, writes, key):
        if key not in self.sem:
            self.sem[key] = self.es.enter_context(self.nc.semaphore("sem_" + key))
            self.cnt[key] = 0
        reads, writes = self._m(reads), self._m(writes)
        self._wait(q, self._deps(q, reads, writes))
        for (o, i) in pairs:
            self.eng[q].dma_start(out=o, in_=i).then_inc(self.sem[key], 16)
            self.cnt[key] += 16
            self.nins += 1
        self._commit((key, self.cnt[key]), reads, writes)

    def wait_all(self, e, keys):
        for k in keys:
            if k in self.sem and self.waited[e].get(k, 0) < self.cnt[k]:
                self.eng[e].wait_ge(self.sem[k], self.cnt[k])
                self.waited[e][k] = self.cnt[k]


def build(stage="full"):
    if not hasattr(build, 'debug'):
        build.debug = False
    if not hasattr(build, 'ratio'):
        build.ratio = 30
    if not hasattr(build, 'hold'):
        build.hold = 35
    if not hasattr(build, 'ratio_b'):
        build.ratio_b = 3
    if not hasattr(build, 'dens'):
        build.dens = 2
    nc = bass.Bass("TRN2", target_bir_lowering=False)

    def din(name, shape, dt=F32):
        return nc.dram_tensor(name, shape, dt, kind="ExternalInput").ap()

    x_d = din("x", [S, D])
    pos_d = din("posT", [128, NCH], I32)
    w_in_d = din("w_in", [DEPTH, D, DPROJ])
    pp_d = din("pp", [DEPTH, 128, NPP])
    sbr_d = din("sbr", [DEPTH, 1, NSB])
    lnp_d = din("lnp", [DEPTH, 2, 1, 2 * D])
    btok_d = din("btok", [DEPTH, 1, NTOKC])
    ws_d = din("sgu_w_s", [DEPTH, 4, 128, 128])
    w_out_d = din("w_out", [DEPTH, D, D])
    fg_d = din("ffn_w_gate", [1, D, DFF])
    fu_d = din("ffn_w_up", [1, D, DFF])
    fd_d = din("ffn_w_down", [1, DFF, D])
    wr_d = din("moe_w_router", [1, D, NE])
    mg_d = din("moe_w_gate", [1, NE, D, DFE])
    mu_d = din("moe_w_up", [1, NE, D, DFE])
    md_d = din("moe_w_down", [1, NE, DFE, D])
    y_d = nc.dram_tensor("y", [S, D], F32, kind="ExternalOutput").ap()
    DBG = build.debug
    if DBG:
        dbg_d = nc.dram_tensor("dbg", [128, 16384], F32, kind="ExternalOutput").ap()
    build.dbg_slots = {}
    dbg_state = {"off": 0}

    with ExitStack() as es:
        T = Tracker(nc, es)
        T.mangle_names = frozenset(["gog", "vp", "u_sb", "gv", "aqkb_a", "aqkb_b", "aqkb_c", "st", "st_d", "st_r", "st_q",
                                    "st_s", "st_m", "st_n", "st_k", "st_l", "st_l2"])

        def sb(name, shape, dt=F32):
            return es.enter_context(nc.sbuf_tensor(name, shape, dt))

        x_tok = sb("x_tok", [128, NCH, D])
        xT = sb("xT", [128, 8, S], BF16)
        arena = sb("arena", [128, 29792], BF16)
        qkT = sb("qkT", [128, 6, 512], BF16)
        lnp = sb("lnp_sb", [128, 2 * D])
        w_in_sb = arena[:, 0:8 * DPROJ].rearrange("p (k n) -> p k n", k=8)
        w_out_sb = arena[:, 8 * DPROJ:8 * DPROJ + 8 * D].rearrange("p (k n) -> p k n", k=8)

        ident = sb("ident", [128, 128], BF16)
        U_f = sb("U_f", [128, 128])
        ones_f = sb("ones_f", [128, 128])
        mask3 = sb("mask3", [128, 384], BF16)
        cs = sb("cs", [128, 2, NCH, 8])
        ppt = sb("ppt", [128, NPP])
        sbt = sb("sbt", [128, NSB])
        brow = sb("brow", [65, 780], BF16)
        ones33 = sb("ones33", [65, 128], BF16)
        wsT = sb("wsT", [128, 4, 128], BF16)
        wr_sb = sb("wr_sb", [128, 8, NE], BF16)
        pconv = sb("pconv", [128, 515])
        yconv = sb("yconv", [128, 512])
        carry = sb("carry", [128, 6, 3])
        gogs = [sb(f"gog{i}", [128, 384], BF16) for i in range(2)]
        sts = [sb(f"st{i}", [128, 96]) for i in range(2)]
        gog, st = gogs[0], sts[0]
        bns = sb("bns", [128, 12, 6])
        bna = sb("bna", [128, 12, 2])
        vps = [sb(f"vp{i}", [128, 6, 65], BF16) for i in range(2)]
        vp = vps[0]
        ktok = sb("ktok", [128, 384], BF16)
        smT = sb("smT", [128, 6, 128], BF16)
        hm = sb("hm", [128, 384])
        Sf = sb("Sf", [128, 3, 65])
        CT8 = sb("CT8", [128, 3, 65], BF16)
        tr4 = sb("tr4", [128, 4, 8, 8])
        aqkbs = [sb(f"aqkb{i}", [128, 512], BF16) for i in range(2)]
        aqkb = aqkbs[0]
        qTat = sb("qTat", [64, 6, 128], BF16)
        kTat = sb("kTat", [64, 2, 256], BF16)
        vat = sb("vat", [128, 3, 128], BF16)
        wbuf = sb("wbuf", [128, 1536], BF16)
        pexp = wbuf[:, 0:768].rearrange("p (h k) -> p h k", h=3)
        pT = wbuf[:, 768:1536].rearrange("p (h b q) -> p h b q", h=3, b=2)
        hcatT = wbuf[:, 0:1024].rearrange("p (k t) -> p k t", k=8)
        u_sbs = [sb(f"u_sb{i}", [128, 256], BF16) for i in range(2)]
        gvs = [sb(f"gv{i}", [128, 256]) for i in range(2)]
        u_sb, gv = u_sbs[0], gvs[0]
        vn = sb("vn", [128, 256], BF16)
        hcat = sb("hcat", [128, D], BF16)

        PB = [es.enter_context(nc.psum_tensor(f"pb{i}", [128, 512], F32)) for i in range(4)]
        PO = [es.enter_context(nc.psum_tensor(f"po{i}", [128, 1024], F32)) for i in range(2)]
        RB = ["ps0", "ps1", "ps2", "ps3"]
        RO = [("ps4", "ps5"), ("ps6", "ps7")]

        def pb_bf(i):
            return PB[i][:, :].bitcast(BF16)

        def dbgdump(name, ap2d, reads):
            if not DBG or name in build.dbg_slots:
                return
            P, n = ap2d.shape[0], ap2d.shape[1]
            o = dbg_state["off"]
            build.dbg_slots[name] = (o, P, n)
            dbg_state["off"] = o + n
            T.dma("pool", [(dbg_d[0:P, o:o + n], ap2d)], reads, [], "d_dbg")

        def act(out, in_, func, reads, writes, **kw):
            T.op("act", lambda e: e.activation(out=out, in_=in_, func=func, **kw), reads, writes)

        def dve(fn, reads, writes):
            T.op("dve", fn, reads, writes)

        def pool(fn, reads, writes):
            T.op("pool", fn, reads, writes)

        def mm_group(mms, reads, writes):
            n = len(mms)
            fns = []
            for i, (o, l, r) in enumerate(mms):
                fns.append(lambda e, o=o, l=l, r=r, i=i: e.matmul(o, lhsT=l, rhs=r, start=(i == 0), stop=(i == n - 1)))
            T.group("pe", fns, reads, writes)

        with nc.allow_non_contiguous_dma(reason="small parameter loads"):
            posi = sb("posi", [128, NCH], I32)
            T.dma("sp", [(posi[:, :], pos_d)], [], ["posi"], "d_pos")

        pool(lambda e: e.memset(ones_f[:, :], 1.0), [], ["ones_f"])
        pool(lambda e: e.affine_select(out=U_f[:, :], in_=ones_f[:, :], pattern=[[1, 128]], compare_op=ALU.is_ge,
                                       fill=0.0, base=0, channel_multiplier=-1), ["ones_f"], ["U_f"])
        idf = hm[:, 0:128]
        pool(lambda e: e.affine_select(out=idf, in_=ones_f[:, :], pattern=[[1, 128]], compare_op=ALU.is_equal,
                                       fill=0.0, base=0, channel_multiplier=-1), ["ones_f"], ["hm"])
        pool(lambda e: e.tensor_copy(out=ident[:, :], in_=idf), ["hm"], ["ident"])
        mtmp = hm[:, 128:384]
        pool(lambda e: e.memset(mtmp, 0.0), [], ["hm"])
        curm = gv[:, 0:128]
        prevm = gv[:, 128:256]
        pool(lambda e: e.affine_select(out=curm, in_=mtmp[:, 0:128], pattern=[[-1, 128]], compare_op=ALU.is_ge,
                                       fill=NEG, base=0, channel_multiplier=1), ["hm"], ["gv"])
        pool(lambda e: e.affine_select(out=prevm, in_=mtmp[:, 0:128], pattern=[[1, 128]], compare_op=ALU.is_gt,
                                       fill=NEG, base=0, channel_multiplier=-1), ["hm"], ["gv"])
        pool(lambda e: e.tensor_copy(out=mask3[:, 0:128], in_=curm), ["gv"], ["mask3a"])
        pool(lambda e: e.tensor_copy(out=mask3[:, 128:256], in_=prevm), ["gv"], ["mask3b"])
        pool(lambda e: e.tensor_copy(out=mask3[:, 256:384], in_=curm), ["gv"], ["mask3c"])
        MASK3 = ["mask3a", "mask3b", "mask3c"]
        pool(lambda e: e.memset(ones33[:, :], 1.0), [], ["ones33"])
        pool(lambda e: e.memset(carry[:, :, :], 0.0), [], ["carry"])

        posf = st[:, 0:16]
        dve(lambda e: e.tensor_copy(out=posf, in_=posi[:, :]), ["posi"], ["st"])
        invf = st[:, 16:24]
        for j in range(8):
            v = float(np.float32(ROPE_THETA) ** np.float32(-(2.0 * j) / 16.0))
            pool(lambda e, j=j, v=v: e.memset(st[:, 16 + j:17 + j], v), [], [f"invf{j}"])
        INVF = [f"invf{j}" for j in range(8)]
        ang = hm[:, 0:128].rearrange("p (c j) -> p c j", j=8)
        dve(lambda e: e.tensor_tensor(out=ang, in0=posf.unsqueeze(2).to_broadcast([128, NCH, 8]),
                                      in1=invf.unsqueeze(1).to_broadcast([128, NCH, 8]), op=ALU.mult),
            ["st"] + INVF + ["ident"], ["hm"])
        C1 = 6.28125
        C2 = 2.0 * math.pi - 6.28125
        kint = gvs[1][:, 0:128].bitcast(I32)
        for which in range(2):
            shift = math.pi / 2 if which == 0 else 0.0
            a2 = hm[:, 128:256]
            kf = hm[:, 256:384]
            a1 = hm[:, 0:128]
            dve(lambda e: e.tensor_scalar(out=a2, in0=a1, scalar1=shift, scalar2=None, op0=ALU.add), ["hm"], ["hm"])
            dve(lambda e: e.tensor_scalar(out=kint[:, :], in0=a2, scalar1=1.0 / (2.0 * math.pi), scalar2=None, op0=ALU.mult),
                ["hm"], ["kint"])
            dve(lambda e: e.tensor_copy(out=kf, in_=kint[:, :]), ["kint"], ["hm"])
            dve(lambda e: e.scalar_tensor_tensor(out=a2, in0=kf, scalar=-C1, in1=a2, op0=ALU.mult, op1=ALU.add),
                ["hm"], ["hm"])
            dve(lambda e: e.scalar_tensor_tensor(out=a2, in0=kf, scalar=-C2, in1=a2, op0=ALU.mult, op1=ALU.add),
                ["hm"], ["hm"])
            dve(lambda e: e.tensor_scalar(out=a2, in0=a2, scalar1=-3.1415925, scalar2=3.1415925, op0=ALU.max, op1=ALU.min),
                ["hm"], ["hm"])
            act(cs[:, which, :, :].rearrange("p c j -> p (c j)"), a2, AF.Sin, ["hm"], ["cs"])

        BROW = {C_MO: (0, 0), C_MV: (0, 396), C_AQ: (32, 0), C_SV: (32, 512), C_AV: (64, 0)}
        TILE_END = {C_MO: C_AQ, C_MV: C_MO, C_AQ: C_AV, C_SV: DPROJ, C_AV: C_SV}

        def load_bias_rows(l):
            prs = []
            for c0, (P, o) in BROW.items():
                n = TILE_END[c0] - c0
                prs.append((brow[P:P + 1, o:o + n], btok_d[l][:, c0 - TOKC0:c0 - TOKC0 + n]))
            T.dma("pool", prs, [], ["brow"], "d_brow")
            pool(lambda e: e.memset(brow[0:1, 384:396], 0.0), ["brow"], ["brow"])

        HCATALL = ["hcat_a", "hcat_b0", "hcat_b1", "hcat_c"]
        ARENA_FFN = [f"w{x}{b}" for x in "gud" for b in range(2)] + [f"hT{i}_{j}" for i in range(2) for j in range(4)]

        def load_mixer_weights(l):
            wtmp = hcat[:, 0:512].rearrange("p (g s) -> p g s", g=4)
            T.dma("pool", [(wtmp, ws_d[l].rearrange("g t s -> t g s"))], [], HCATALL, "d_ws")
            srcw = w_in_d[l].rearrange("(k p) n -> p k n", p=128)
            T.dma("pool", [(w_in_sb[:, :, 0:TOKC0], srcw[:, :, 0:TOKC0])], [], ["w_in_qk"] + ARENA_FFN, "d_winqk")
            T.dma("pool", [(w_in_sb[:, k, TOKC0:DPROJ], srcw[:, k, TOKC0:DPROJ]) for k in range(8)], [], ["w_in_sb"] + ARENA_FFN, "d_win")
            srco = w_out_d[l].rearrange("(k p) n -> p k n", p=128)
            T.dma("pool", [(w_out_sb[:, 0:4, :], srco[:, 0:4, :]), (w_out_sb[:, 4:8, :], srco[:, 4:8, :])],
                  [], ["w_out_sb"] + ARENA_FFN, "d_wout")
            trb = pb_bf(2)
            T.group("pe", [lambda e, g=g: e.transpose(out=trb[:, g * 128:(g + 1) * 128], in_=wtmp[:, g, :], identity=ident[:, :])
                           for g in range(4)], HCATALL + ["ident"], ["ps2"])
            dve(lambda e: e.tensor_tensor(out=wsT[:, :, :], in0=trb[:, 0:512].rearrange("p (g t) -> p g t", g=4),
                                          in1=U_f[:, :].unsqueeze(1).to_broadcast([128, 4, 128]), op=ALU.mult),
                ["ps2", "U_f"], ["wsT"])

        def phaseA(l, j):
            for cc in range(6):
                bank = cc % 2
                mm_group([(PB[bank][:, :], w_in_sb[:, k, cc * 128:(cc + 1) * 128], xT[:, k, j * 512:(j + 1) * 512])
                          for k in range(8)], ["w_in_qk"] + [f"xT{4 * j + t}" for t in range(4)], [RB[bank]])
                yield
                pool(lambda e, cc=cc: e.tensor_copy(out=pconv[:, 0:3], in_=carry[:, cc, :]), ["carry"], ["pconv"])
                yield
                act(pconv[:, 3:515], PB[bank][:, :], AF.Identity, [RB[bank], "ppt"], ["pconv"],
                    bias=ppt[:, 24 + cc:25 + cc])
                yield
                pool(lambda e, cc=cc: e.tensor_copy(out=carry[:, cc, :], in_=pconv[:, 512:515]), ["pconv"], ["carry"])
                yield
                dve(lambda e, cc=cc: e.tensor_scalar(out=yconv[:, :], in0=pconv[:, 3:515], scalar1=ppt[:, cc * 4 + 3:cc * 4 + 4],
                                                      scalar2=None, op0=ALU.mult), ["pconv", "ppt"], ["yconv"])
                yield
                for jj in (2, 1, 0):
                    dve(lambda e, cc=cc, jj=jj: e.scalar_tensor_tensor(
                        out=yconv[:, :], in0=pconv[:, jj:jj + 512], scalar=ppt[:, cc * 4 + jj:cc * 4 + jj + 1],
                        in1=yconv[:, :], op0=ALU.mult, op1=ALU.add), ["pconv", "ppt", "yconv"], ["yconv"])
                    yield
                act(qkT[:, cc, :], yconv[:, :], AF.Silu, ["yconv"], ["qkT"])
                yield

        def phaseB(l, c, is_moe):
            tok0 = c * 128
            off = (c % 4) * 128
            slot = c % 2
            par = c % 2
            gog, st, vp, aqkb, u_sb, gv = gogs[par], sts[par], vps[par], aqkbs[par], u_sbs[par], gvs[par]
            v3c, v3p = c % 3, (c - 1) % 3
            WB = ["pexp0", "pexp1", "pexp2", "pT_a", "pT_b"]

            def proj_tile(bank, c0, c1):
                n = c1 - c0
                mms = [(PB[bank][:, 0:n], xT[:, k, tok0:tok0 + 128], w_in_sb[:, k, c0:c1]) for k in range(8)]
                P, o = BROW[c0]
                mms.append((PB[bank][:, 0:n], ones33[P:P + 1, :], brow[P:P + 1, o:o + n]))
                mm_group(mms, [f"xT{c}", "w_in_sb", "ones33", "brow"], [RB[bank]])

            proj_tile(0, C_MO, C_AQ)
            yield
            act(gog[:, :], PB[0][:, 0:384], AF.Tanh, ["ps0"], ["gog"], scale=0.5)
            yield
            dve(lambda e: e.tensor_tensor(out=st[:, 0:12], in0=PB[0][:, 384:396], in1=sbt[:, 910:922], op=ALU.add), ["ps0", "sbt"], ["st"])
            yield
            proj_tile(1, C_AV, C_SV)
            yield
            act(vat[:, v3c, :], PB[1][:, 0:128], AF.Copy, ["ps1"], [f"vat{v3c}"])
            yield
            act(u_sb[:, :], PB[1][:, 128:384], AF.Gelu_apprx_tanh, ["ps1"], ["u_sb"])
            yield
            proj_tile(0, C_SV, DPROJ)
            yield
            act(gv[:, :], PB[0][:, 0:256], AF.Gelu_apprx_tanh, ["ps0"], ["gv"])
            yield
            proj_tile(1, C_MV, C_MO)
            yield
            act(st[:, 12:18], st[:, 6:12], AF.Exp, ["st"], ["st"], scale=-1.0)
            yield
            act(st[:, 12:18], st[:, 12:18], AF.Ln, ["st"], ["st"], bias=1.0)
            yield
            T.group("pe", [lambda e: e.matmul(PB[0][:, 0:6], lhsT=U_f[:, :], rhs=st[:, 12:18], start=True, stop=True),
                           lambda e: e.matmul(PB[0][:, 8:14], lhsT=ones_f[:, :], rhs=st[:, 12:18], start=True, stop=True)],
                    ["U_f", "ones_f", "st"], ["ps0"])
            yield
            dve(lambda e: e.tensor_add(out=st[:, 18:24], in0=st[:, 0:6], in1=PB[0][:, 0:6]), ["st", "ps0"], ["st"])
            yield
            act(st[:, 24:30], st[:, 18:24], AF.Exp, ["st"], ["st"])
            yield
            act(st[:, 30:36], PB[0][:, 0:6], AF.Exp, ["ps0"], ["st"])
            yield
            ntot = PB[0][:, 8:14].rearrange("p (i two) -> p i two", two=2)
            act(st[0:64, 36:39], ntot[0:64, :, 0], AF.Exp, ["ps0"], ["st"], scale=-1.0)
            yield
            act(st[64:128, 36:39], ntot[64:128, :, 1], AF.Exp, ["ps0"], ["st"], scale=-1.0)
            yield
            act(st[0:64, 39:42], ntot[0:64, :, 0], AF.Exp, ["ps0"], ["st"], scale=-1.0, bias=math.log(0.125))
            yield
            act(st[64:128, 39:42], ntot[64:128, :, 1], AF.Exp, ["ps0"], ["st"], scale=-1.0, bias=math.log(0.125))
            yield
            dve(lambda e: e.scalar_tensor_tensor(out=gog[:, :], in0=gog[:, :], scalar=1.0, in1=sbt[:, 0:384],
                                                 op0=ALU.add, op1=ALU.mult), ["gog", "sbt"], ["gog"])
            yield
            dve(lambda e: e.tensor_tensor(out=vp[:, :, 0:64], in0=PB[1][:, 0:384].rearrange("p (h d) -> p h d", h=6),
                                          in1=st[:, 24:30].unsqueeze(2).to_broadcast([128, 6, 64]), op=ALU.mult),
                ["ps1", "st"], ["vp"])
            yield
            pool(lambda e: e.tensor_copy(out=vp[:, :, 64], in_=st[:, 24:30]), ["st"], ["vp"])
            yield
            proj_tile(0, C_AQ, C_AV)
            yield
            t3 = PB[0][:, :].rearrange("p (h d) -> p h d", h=8)
            cosb = cs[:, 0, c, :].unsqueeze(1).to_broadcast([128, 8, 8])
            sinb = cs[:, 1, c, :].unsqueeze(1).to_broadcast([128, 8, 8])
            dve(lambda e: e.tensor_tensor(out=tr4[:, 0], in0=t3[:, :, 0:8], in1=cosb, op=ALU.mult), ["ps0", "cs"], ["tr4a"])
            yield
            dve(lambda e: e.tensor_tensor(out=tr4[:, 1], in0=t3[:, :, 8:16], in1=sinb, op=ALU.mult), ["ps0", "cs"], ["tr4b"])
            yield
            dve(lambda e: e.tensor_tensor(out=tr4[:, 2], in0=t3[:, :, 8:16], in1=cosb, op=ALU.mult), ["ps0", "cs"], ["tr4c"])
            yield
            dve(lambda e: e.tensor_tensor(out=tr4[:, 3], in0=t3[:, :, 0:8], in1=sinb, op=ALU.mult), ["ps0", "cs"], ["tr4d"])
            yield
            aq3 = aqkb[:, :].rearrange("p (h d) -> p h d", h=8)
            act(aq3[:, :, 16:64], t3[:, :, 16:64], AF.Copy, ["ps0"], ["aqkb_c"])
            yield
            pool(lambda e: e.tensor_sub(out=aq3[:, :, 0:8], in0=tr4[:, 0], in1=tr4[:, 1]), ["tr4a", "tr4b"], ["aqkb_a"])
            yield
            pool(lambda e: e.tensor_add(out=aq3[:, :, 8:16], in0=tr4[:, 2], in1=tr4[:, 3]), ["tr4c", "tr4d"], ["aqkb_b"])
            yield
            AQKB = ["aqkb_a", "aqkb_b", "aqkb_c"]
            if c == 0 and l == 0 and DBG:
                dbgdump('g12', st[:, 0:12], ["st"])
                dbgdump('st2', st[:, 0:48], ["st"])
                dbgdump('vp', vp[:, :, :].rearrange("p h v -> p (h v)"), ["vp"])
                dbgdump('gog', gog[:, :], ["gog"])
                dbgdump('aqkb', aqkb[:, :], AQKB)
                dbgdump('u_sb', u_sb[:, :], ["u_sb"])
                dbgdump('gvg', gv[:, :], ["gv"])
            yield "B1"
            nk = 1 if c == 0 else 2

            def strandM():
                trb = pb_bf(2)
                T.group("pe", [lambda e, i=i: e.transpose(out=trb[:, i * 128:(i + 1) * 128], in_=qkT[:, 3 + i, off:off + 128],
                                                          identity=ident[:, :]) for i in range(3)],
                        ["qkT", "ident"], ["ps2"])
                yield
                act(ktok[:, :], trb[:, 0:384], AF.Copy, ["ps2"], ["ktok"])
                yield
                sc = PO[0]
                fns = []
                for h in range(6):
                    hp, i = h % 2, h // 2
                    fns.append(lambda e, h=h, hp=hp, i=i: e.matmul(
                        sc[:, hp * 512 + i * 128:hp * 512 + (i + 1) * 128], lhsT=qkT[hp * 64:(hp + 1) * 64, 3 + i, off:off + 128],
                        rhs=qkT[hp * 64:(hp + 1) * 64, i, off:off + 128], start=True, stop=True))
                T.group("pe", fns, ["qkT"], ["ps4", "ps5"])
                yield
                smT4 = smT[:, :, :].rearrange("p (i two) l -> p i two l", two=2)
                dve(lambda e: e.scalar_tensor_tensor(out=smT4[:, :, 0, :], in0=sc[:, 0:384].rearrange("p (h l) -> p h l", h=3),
                                                     scalar=0.125, in1=U_f[:, :].unsqueeze(1).to_broadcast([128, 3, 128]),
                                                     op0=ALU.mult, op1=ALU.mult), ["ps4", "U_f"], ["smT_a"])
                yield
                dve(lambda e: e.scalar_tensor_tensor(out=smT4[:, :, 1, :], in0=sc[:, 512:896].rearrange("p (h l) -> p h l", h=3),
                                                     scalar=0.125, in1=U_f[:, :].unsqueeze(1).to_broadcast([128, 3, 128]),
                                                     op0=ALU.mult, op1=ALU.mult), ["ps5", "U_f"], ["smT_b"])
                yield
                num = PB[3][:, 16:406].rearrange("p (h v) -> p h v", h=6)
                fns = []
                for h in range(6):
                    hp, i = h % 2, h // 2
                    last = (c == 0)
                    fns.append(lambda e, h=h, last=last: e.matmul(num[:, h, :], lhsT=smT[:, h, :], rhs=vp[:, h, :], start=True, stop=last))
                    if c > 0:
                        fns.append(lambda e, h=h, hp=hp, i=i: e.matmul(
                            num[:, h, :], lhsT=qkT[hp * 64:(hp + 1) * 64, i, off:off + 128],
                            rhs=CT8[hp * 64:(hp + 1) * 64, i, :], start=False, stop=True))
                T.group("pe", fns, ["smT_a", "smT_b", "vp", "qkT", "CT8"], ["ps3"])
                yield
                dcb = PB[2][:, 0:390].rearrange("p (i v) -> p i v", i=3)
                if c < NCH - 1:
                    T.group("pe", [lambda e, i=i: e.matmul(dcb[:, i, :], lhsT=ktok[:, i * 128:(i + 1) * 128],
                                                           rhs=vp[:, 2 * i:2 * i + 2, :].rearrange("p h v -> p (h v)"),
                                                           start=True, stop=True) for i in range(3)],
                            ["ktok", "vp"], ["ps2"])
                    yield
                act(st[:, 42:48], num[:, :, 64], AF.Copy, ["ps3"], ["st_d"])
                yield
                dve(lambda e: e.scalar_tensor_tensor(out=st[:, 48:54], in0=st[:, 42:48], scalar=-1.0, in1=st[:, 42:48],
                                                     op0=ALU.mult, op1=ALU.max), ["st_d"], ["st_r"])
                yield
                dve(lambda e: e.tensor_tensor(out=st[:, 48:54], in0=st[:, 48:54], in1=st[:, 30:36], op=ALU.max), ["st_r", "st"], ["st_r"])
                yield
                dve(lambda e: e.reciprocal(out=st[:, 48:54], in_=st[:, 48:54]), ["st_r"], ["st_r"])
                yield
                hm3 = hm[:, :].rearrange("p (h d) -> p h d", h=6)
                dve(lambda e: e.tensor_tensor(out=hm3, in0=num[:, :, 0:64], in1=st[:, 48:54].unsqueeze(2).to_broadcast([128, 6, 64]),
                                              op=ALU.mult), ["ps3", "st_r"], ["hm"])
                yield
                if c < NCH - 1:
                    if c == 0:
                        dve(lambda e: e.tensor_copy(out=Sf[0:64, :, :], in_=dcb[0:64, :, 0:65]), ["ps2"], ["Sf_a"])
                        yield
                        dve(lambda e: e.tensor_copy(out=Sf[64:128, :, :], in_=dcb[64:128, :, 65:130]), ["ps2"], ["Sf_b"])
                        yield
                    else:
                        dve(lambda e: e.tensor_add(out=Sf[0:64, :, :], in0=Sf[0:64, :, :], in1=dcb[0:64, :, 0:65]),
                            ["ps2", "Sf_a"], ["Sf_a"])
                        yield
                        dve(lambda e: e.tensor_add(out=Sf[64:128, :, :], in0=Sf[64:128, :, :], in1=dcb[64:128, :, 65:130]),
                            ["ps2", "Sf_b"], ["Sf_b"])
                        yield
                    pool(lambda e: e.tensor_tensor(out=CT8[:, :, :], in0=Sf[:, :, :],
                                                   in1=st[:, 39:42].unsqueeze(2).to_broadcast([128, 3, 65]), op=ALU.mult),
                         ["Sf_a", "Sf_b", "st"], ["CT8"])
                    yield
                    pool(lambda e: e.tensor_tensor(out=Sf[:, :, :], in0=Sf[:, :, :],
                                                   in1=st[:, 36:39].unsqueeze(2).to_broadcast([128, 3, 65]), op=ALU.mult),
                         ["Sf_a", "Sf_b", "st"], ["Sf_a", "Sf_b"])
                    yield
                for h in range(6):
                    dve(lambda e, h=h: e.bn_stats(out=bns[:, h, :], in_=hm[:, h * 64:(h + 1) * 64]), ["hm"], [f"bns{h}"])
                    yield
                for h in range(6):
                    dve(lambda e, h=h: e.bn_aggr(out=bna[:, h, :], in_=bns[:, h, :]), [f"bns{h}"], [f"bna{h}"])
                    yield
                BNA6 = [f"bna{h}" for h in range(6)]
                act(st[:, 54:60], bna[:, 0:6, 1], AF.Ln, BNA6, ["st_q"], bias=EPS)
                yield
                act(st[:, 54:60], st[:, 54:60], AF.Exp, ["st_q"], ["st_q"], scale=-0.5)
                yield
                dve(lambda e: e.tensor_tensor(out=hm3, in0=hm3, in1=bna[:, 0:6, 0:1].to_broadcast([128, 6, 64]), op=ALU.subtract),
                    ["hm"] + BNA6, ["hm"])
                yield
                dve(lambda e: e.tensor_tensor(out=hm3, in0=hm3, in1=st[:, 54:60].unsqueeze(2).to_broadcast([128, 6, 64]), op=ALU.mult),
                    ["hm", "st_q"], ["hm"])
                yield
                pool(lambda e: e.tensor_mul(out=hcat[:, 0:384], in0=hm[:, :], in1=gog[:, :]), ["hm", "gog"], ["hcat_a"])
                yield

            def strandW():
                po0b = PO[0][:, :].bitcast(BF16)
                trw = po0b[:, 1024:2048]
                trp = po0b[:, 0:1024]
                T.group("pe", [lambda e, h=h: e.transpose(out=trw[0:64, h * 128:(h + 1) * 128], in_=aqkb[:, h * 64:(h + 1) * 64],
                                                          identity=ident[:, :]) for h in range(8)],
                        AQKB + ["ident"], ["ps5"])
                yield
                dve(lambda e: e.tensor_copy(out=qTat[:, :, :], in_=trw[0:64, 0:768].rearrange("p (h t) -> p h t", h=6)),
                    ["ps5"], ["qTat"])
                yield
                act(kTat[:, :, slot * 128:(slot + 1) * 128], trw[0:64, 768:1024].rearrange("p (h t) -> p h t", h=2), AF.Copy,
                    ["ps5"], [f"kTat{slot}"])
                yield
                for g in range(2):
                    scb = PO[1]
                    if nk == 2:
                        kview = kTat[:, g, :]
                        mview = mask3[:, 128:384] if slot == 1 else mask3[:, 0:256]
                        W = 256
                    else:
                        kview = kTat[:, g, slot * 128:(slot + 1) * 128]
                        mview = mask3[:, 0:128]
                        W = 128
                    fns = []
                    for hh in range(3):
                        h = 3 * g + hh
                        o = scb[:, hh * 256:hh * 256 + W]
                        fns.append(lambda e, o=o, h=h, kview=kview: e.matmul(o, lhsT=qTat[:, h, :], rhs=kview, start=True, stop=False))
                        fns.append(lambda e, o=o, mview=mview: e.matmul(o, lhsT=ident[:, :], rhs=mview, start=False, stop=True))
                    T.group("pe", fns, ["qTat", "kTat0", "kTat1", "ident"] + MASK3, ["ps6", "ps7"])
                    yield
                    sc3 = scb[:, 0:768].rearrange("p (h k) -> p h k", h=3)[:, :, 0:W]
                    mxc = st[:, 70:73]
                    dve(lambda e: e.tensor_reduce(out=mxc, in_=sc3, axis=AX.X, op=ALU.max), ["ps6", "ps7"], ["st_m"])
                    yield
                    dve(lambda e: e.scalar_tensor_tensor(out=mxc, in0=mxc, scalar=0.125, in1=sbt[:, 384 + 3 * g:387 + 3 * g],
                                                         op0=ALU.mult, op1=ALU.max), ["st_m", "sbt"], ["st_m"])
                    yield
                    dve(lambda e: e.tensor_scalar(out=st[:, 73:76], in0=mxc, scalar1=-1.0, scalar2=None, op0=ALU.mult), ["st_m"], ["st_n"])
                    yield
                    for hh in range(3):
                        act(pexp[:, hh, 0:W], scb[:, hh * 256:hh * 256 + W], AF.Exp, ["ps6", "ps7", "st_n"], [f"pexp{hh}", "hcatT_a", "hcatT_b"],
                            bias=st[:, 73 + hh:74 + hh], scale=0.125, accum_out=st[:, 76 + hh:77 + hh])
                        yield
                    dve(lambda e: e.tensor_add(out=st[:, 79:82], in0=st[:, 73:76], in1=sbt[:, 384 + 3 * g:387 + 3 * g]), ["st_n", "sbt"], ["st_k"])
                    yield
                    act(st[:, 79:82], st[:, 79:82], AF.Exp, ["st_k"], ["st_k"])
                    yield
                    dve(lambda e: e.tensor_add(out=st[:, 79:82], in0=st[:, 79:82], in1=st[:, 76:79]),
                        ["st_k", "pexp0", "pexp1", "pexp2"], ["st_k"])
                    yield
                    dve(lambda e: e.reciprocal(out=st[:, 79:82], in_=st[:, 79:82]), ["st_k"], ["st_k"])
                    yield
                    fns = []
                    for hh in range(3):
                        for b in range(nk):
                            fns.append(lambda e, hh=hh, b=b: e.transpose(out=trp[:, (hh * 2 + b) * 128:(hh * 2 + b + 1) * 128],
                                                                         in_=pexp[:, hh, b * 128:(b + 1) * 128], identity=ident[:, :]))
                    T.group("pe", fns, ["pexp0", "pexp1", "pexp2", "ident"], ["ps4"])
                    yield
                    pTf = wbuf[:, 768:1536]
                    if nk == 2:
                        act(pTf[:, 0:384], trp[:, 0:384], AF.Copy, ["ps4"], ["pT_a", "hcatT_a", "hcatT_b"])
                        yield
                        dve(lambda e: e.tensor_copy(out=pTf[:, 384:768], in_=trp[:, 384:768]), ["ps4"], ["pT_b", "hcatT_a", "hcatT_b"])
                        yield
                    else:
                        act(pT[:, :, 0, :], trp[:, 0:768].rearrange("p (h b q) -> p h b q", h=3, b=2)[:, :, 0, :], AF.Copy,
                            ["ps4"], ["pT_a", "hcatT_a", "hcatT_b"])
                        yield
                    ob = PO[0][:, 512:704].rearrange("p (h d) -> p h d", h=3)
                    fns = []
                    for hh in range(3):
                        for b in range(nk):
                            sl = v3c if (nk == 1 or b == slot) else v3p
                            fns.append(lambda e, hh=hh, b=b, sl=sl, g=g: e.matmul(ob[:, hh, :], lhsT=pT[:, hh, b, :],
                                                                                  rhs=vat[:, sl, g * 64:(g + 1) * 64],
                                                                                  start=(b == 0), stop=(b == nk - 1)))
                    T.group("pe", fns, ["pT_a", "pT_b", "vat0", "vat1", "vat2"], ["ps5"])
                    yield
                    hb = hcat[:, 384 + g * 192:384 + (g + 1) * 192].rearrange("p (h d) -> p h d", h=3)
                    dve(lambda e, hb=hb, ob=ob: e.tensor_tensor(out=hb, in0=ob, in1=st[:, 79:82].unsqueeze(2).to_broadcast([128, 3, 64]),
                                                                 op=ALU.mult), ["ps5", "st_k"], [f"hcat_b{g}"])
                    yield

            def strandG():
                gv3 = gv[:, :].rearrange("p (g d) -> p g d", g=4)
                for g in range(4):
                    dve(lambda e, g=g: e.bn_stats(out=bns[:, 8 + g, :], in_=gv[:, g * 64:(g + 1) * 64]), ["gv"], [f"bns{8 + g}"])
                    yield
                for g in range(4):
                    dve(lambda e, g=g: e.bn_aggr(out=bna[:, 8 + g, :], in_=bns[:, 8 + g, :]), [f"bns{8 + g}"], [f"bna{8 + g}"])
                    yield
                BNA4 = [f"bna{8 + g}" for g in range(4)]
                act(st[:, 66:70], bna[:, 8:12, 1], AF.Ln, BNA4, ["st_s"], bias=EPS)
                yield
                act(st[:, 66:70], st[:, 66:70], AF.Exp, ["st_s"], ["st_s"], scale=-0.5)
                yield
                dve(lambda e: e.tensor_tensor(out=gv3, in0=gv3, in1=bna[:, 8:12, 0:1].to_broadcast([128, 4, 64]), op=ALU.subtract),
                    ["gv"] + BNA4, ["gv"])
                yield
                dve(lambda e: e.tensor_tensor(out=gv3, in0=gv3, in1=st[:, 66:70].unsqueeze(2).to_broadcast([128, 4, 64]), op=ALU.mult),
                    ["gv", "st_s"], ["gv"])
                yield
                pool(lambda e: e.tensor_mul(out=gv[:, :], in0=gv[:, :], in1=sbt[:, 390:646]), ["gv", "sbt"], ["gv"])
                yield
                pool(lambda e: e.tensor_add(out=vn[:, :], in0=gv[:, :], in1=sbt[:, 646:902]), ["gv", "sbt"], ["vn"])
                yield
                mxb = PO[0][:, 768:1024].rearrange("p (g d) -> p g d", g=4)
                T.group("pe", [lambda e, g=g: e.matmul(mxb[:, g, :], lhsT=wsT[:, g, :], rhs=vn[:, g * 64:(g + 1) * 64], start=True, stop=True)
                               for g in range(4)], ["wsT", "vn"], ["ps5"])
                yield
                dve(lambda e: e.tensor_tensor(out=gv3, in0=mxb, in1=ppt[:, 30:34].unsqueeze(2).to_broadcast([128, 4, 64]), op=ALU.add),
                    ["ps5", "ppt"], ["gv"])
                yield
                pool(lambda e: e.tensor_mul(out=hcat[:, 768:1024], in0=gv[:, :], in1=u_sb[:, :]), ["gv", "u_sb"], ["hcat_c"])
                yield

            strands = [strandM(), strandW(), strandG()]
            while strands:
                for sgen in list(strands):
                    try:
                        next(sgen)
                        yield
                    except StopIteration:
                        strands.remove(sgen)
            HCAT = ["hcat_a", "hcat_b0", "hcat_b1", "hcat_c"]
            yield "S"
            if c == 0 and l == 0 and DBG:
                dbgdump('hcat_a', hcat[:, 0:384], ["hcat_a"])
                dbgdump('hcat_b', hcat[:, 384:768], ["hcat_b0", "hcat_b1"])
                dbgdump('hcat_c', hcat[:, 768:1024], ["hcat_c"])
                dbgdump('st10', st[:, 0:96], ["st", "st_d", "st_r", "st_q"])
                dbgdump('st11', st[:, 0:96], ["st_k", "st_m", "st_n"])
            trb = pb_bf(2)
            T.group("pe", [lambda e, k=k: e.transpose(out=trb[:, k * 128:(k + 1) * 128], in_=hcat[:, k * 128:(k + 1) * 128],
                                                      identity=ident[:, :]) for k in range(8)], HCAT + ["ident"], ["ps2"])
            yield
            hTf = wbuf[:, 0:1024]
            act(hTf[:, 0:512], trb[:, 0:512], AF.Copy, ["ps2"], ["hcatT_a"] + WB)
            yield
            dve(lambda e: e.tensor_copy(out=hTf[:, 512:1024], in_=trb[:, 512:1024]), ["ps2"], ["hcatT_b"] + WB)
            yield
            if c == 0 and l == 0 and DBG:
                dbgdump('hcatT', wbuf[:, 0:1024], ["hcatT_a", "hcatT_b"])
                dbgdump('hcat_all', hcat[:, :], HCAT)
                dbgdump('wout0', w_out_sb[:, 0, :], ["w_out_sb"])
            mixp = PO[1]
            for n in range(2):
                mm_group([(mixp[:, n * 512:(n + 1) * 512], hcatT[:, k, :], w_out_sb[:, k, n * 512:(n + 1) * 512]) for k in range(8)],
                         ["hcatT_a", "hcatT_b", "w_out_sb"], [RO[1][n]])
                yield
            xt = x_tok[:, c, :]
            XR = f"x_tok{c}"
            if l == 0:
                dve(lambda e: e.scalar_tensor_tensor(out=xt, in0=xt, scalar=ALPHA, in1=mixp[:, :], op0=ALU.mult, op1=ALU.add),
                    ["x_tok", XR, "ps6", "ps7"], [XR])
            else:
                dve(lambda e: e.tensor_add(out=xt, in0=xt, in1=mixp[:, :]), ["x_tok", XR, "ps6", "ps7"], [XR])
            yield
            for _ in layer_norm_gen(c, XR, lnp, "lnp", to_xT=True, xscale=1.0 / ALPHA, k=0):
                yield

        lnst = sb("lnst", [128, 4, 2, 6])
        lnag = sb("lnag", [128, 4, 2])
        lnsc = sb("lnsc", [128, 4, 2])
        XNB = [(hcat[:, :], ["hcat_a", "hcat_b0", "hcat_b1", "hcat_c"]),
               (qkT[:, 4:6, :].rearrange("p a t -> p (a t)"), ["qkT"]),
               (qkT[:, 0:2, :].rearrange("p a t -> p (a t)"), ["qkT"]),
               (qkT[:, 2:4, :].rearrange("p a t -> p (a t)"), ["qkT"])]

        def layer_norm_tile(c, XR, prm, PR, to_xT, out_dma=False, xscale=1.0, k=0):
            for _ in layer_norm_gen(c, XR, prm, PR, to_xT, out_dma, xscale, k):
                pass

        def layer_norm_gen(c, XR, prm, PR, to_xT, out_dma=False, xscale=1.0, k=0):
            xt = x_tok[:, c, :]
            SA, SG, SC, SC2 = f"lnst{k}", f"lnag{k}", f"lnsc{k}", f"lnsd{k}"
            for i in range(2):
                dve(lambda e, i=i: e.bn_stats(out=lnst[:, k, i, :], in_=xt[:, i * 512:(i + 1) * 512]), [XR], [SA + "ab"[i]])
                yield
            dve(lambda e: e.bn_aggr(out=lnag[:, k, :], in_=lnst[:, k, :, :]), [SA + "a", SA + "b"], [SG])
            yield
            act(lnsc[:, k, 0:1], lnag[:, k, 1:2], AF.Ln, [SG], [SC], bias=EPS)
            yield
            act(lnsc[:, k, 0:1], lnsc[:, k, 0:1], AF.Exp, [SC], [SC], scale=-0.5)
            yield
            dve(lambda e: e.scalar_tensor_tensor(out=lnsc[:, k, 1:2], in0=lnag[:, k, 0:1], scalar=-1.0, in1=lnsc[:, k, 0:1],
                                                 op0=ALU.mult, op1=ALU.mult), [SG, SC], [SC2])
            yield
            act(xt, xt, AF.Identity, [XR, SC, SC2], [XR], scale=lnsc[:, k, 0:1], bias=lnsc[:, k, 1:2])
            yield
            dve(lambda e: e.tensor_mul(out=xt, in0=xt, in1=prm[:, 0:D]), [XR, PR], [XR])
            yield
            dve(lambda e: e.tensor_add(out=xt, in0=xt, in1=prm[:, D:2 * D]), [XR, PR], [XR])
            yield
            if to_xT:
                xnb, XN = XNB[k]
                act(xnb, xt, AF.Copy, [XR], XN, scale=xscale)
                yield
                bank = 2 + (k % 2)
                trb = pb_bf(bank)
                T.group("pe", [lambda e, kk=kk: e.transpose(out=trb[:, kk * 128:(kk + 1) * 128], in_=xnb[:, kk * 128:(kk + 1) * 128],
                                                            identity=ident[:, :]) for kk in range(8)],
                        XN + ["ident"], [RB[bank]])
                tr3 = trb[:, :].rearrange("p (k t) -> p k t", k=8)
                act(xT[:, 0:4, c * 128:(c + 1) * 128], tr3[:, 0:4, :], AF.Copy, [RB[bank]], [f"xT{c}"])
                dve(lambda e: e.tensor_copy(out=xT[:, 4:8, c * 128:(c + 1) * 128], in_=tr3[:, 4:8, :]), [RB[bank], f"xT{c}"], [f"xT{c}"])
                yield
            if out_dma:
                T.dma("sp", [(y_d[c * 128:(c + 1) * 128, :], xt)], [XR], [], "d_out")
                yield

        def dump_x():
            T.wait_all("sp", ["d_dbg"])
            for c in range(NCH):
                T.dma("sp", [(y_d[c * 128:(c + 1) * 128, :], x_tok[:, c, :])], ["x_tok", f"x_tok{c}"], [], "d_out")
            T.wait_all("sp", ["d_out"])

        rt = sb("rt", [128, 64])
        sgs = [pconv[:, 0:512], yconv[:, :]]
        SGR = ["pconv", "yconv"]
        comb = hm[:, 0:128].rearrange("p (t e) -> p t e", e=NE)

        def ffn_phase(l, is_moe, last, final_layer):
            with nc.allow_non_contiguous_dma(reason="small parameter loads"):
                T.dma("sp", [(lnp[:, :], lnp_d[l, 1].partition_broadcast(128))], [], ["lnp"], "d_lnp")
            if not final_layer:
                dve(lambda e: e.tensor_scalar(out=lnp[:, :], in0=lnp[:, :], scalar1=ALPHA, scalar2=None, op0=ALU.mult), ["lnp"], ["lnp"])
            if is_moe:
                router()
            groups = []
            if is_moe:
                for ex in range(NE):
                    for f0 in range(0, DFE, FG):
                        groups.append((mg_d[0, ex], mu_d[0, ex], md_d[0, ex], f0, min(FG, DFE - f0), ex))
            else:
                rem = DFF % FG
                f0s = ([(0, rem)] if rem else []) + [(f0, FG) for f0 in range(rem, DFF, FG)]
                for f0, F in f0s:
                    groups.append((fg_d[0], fu_d[0], fd_d[0], f0, F, None))

            def wviews(b):
                base = b * 12288
                wg = arena[:, base:base + 4096].rearrange("p (k f) -> p k f", k=8)
                wu = arena[:, base + 4096:base + 8192].rearrange("p (k f) -> p k f", k=8)
                wd = arena[:, base + 8192:base + 12288].rearrange("p (j n) -> p j n", j=4)
                return wg, wu, wd

            def hview(i):
                base = 24576 + i * 2048
                return arena[:, base:base + 2048].rearrange("p (j t) -> p j t", j=4)

            def load(gi):
                gsrc, usrc, dsrc, f0, F, ex = groups[gi]
                b = gi % 2
                wg, wu, wd = wviews(b)
                nj = F // 128
                T.dma("pool", [(wg[:, :, 0:F], gsrc[:, f0:f0 + F].rearrange("(k p) f -> p k f", p=128))], [], [f"wg{b}", "w_in_sb", "w_in_qk", "w_out_sb"], f"d_wg{b}")
                T.dma("pool", [(wu[:, :, 0:F], usrc[:, f0:f0 + F].rearrange("(k p) f -> p k f", p=128))], [], [f"wu{b}", "w_in_sb", "w_in_qk", "w_out_sb"], f"d_wu{b}")
                T.dma("pool", [(wd[:, 0:nj, :], dsrc[f0:f0 + F, :].rearrange("(j p) n -> p j n", p=128))], [], [f"wd{b}", "w_in_sb", "w_in_qk", "w_out_sb"], f"d_wd{b}")

            steps = [(gi, tb) for gi in range(len(groups)) for tb in range(4)]
            state = {"gu": 0, "out": 0}
            ln_active = []

            def pump(n):
                for _ in range(n):
                    for g_ in list(ln_active):
                        try:
                            next(g_)
                        except StopIteration:
                            ln_active.remove(g_)

            def GU(si):
                gi, tb = steps[si]
                F = groups[gi][4]
                b = gi % 2
                wg, wu, _ = wviews(b)
                hT = hview(si % 2)
                for j in range(F // 128):
                    p = state["gu"] % 2
                    state["gu"] += 1
                    gb, ub = PB[2 * p], PB[2 * p + 1]
                    mm_group([(gb[:, :], wg[:, k, j * 128:(j + 1) * 128], xT[:, k, tb * 512:(tb + 1) * 512]) for k in range(8)],
                             [f"wg{b}"] + [f"xT{4 * tb + t}" for t in range(4)], [RB[2 * p]])
                    mm_group([(ub[:, :], wu[:, k, j * 128:(j + 1) * 128], xT[:, k, tb * 512:(tb + 1) * 512]) for k in range(8)],
                             [f"wu{b}"] + [f"xT{4 * tb + t}" for t in range(4)], [RB[2 * p + 1]])
                    act(sgs[p], gb[:, :], AF.Silu, [RB[2 * p]], [SGR[p]])
                    dve(lambda e, p=p, j=j, ub=ub, hT=hT: e.tensor_tensor(out=hT[:, j, :], in0=sgs[p], in1=ub[:, :], op=ALU.mult),
                        [SGR[p], RB[2 * p + 1]], [f"hT{si % 2}_{j}"])
                    pump(1)

            def DOWN(si):
                gi, tb = steps[si]
                F = groups[gi][4]
                ex = groups[gi][5]
                b = gi % 2
                _, _, wd = wviews(b)
                hT = hview(si % 2)
                nj = F // 128
                final = (gi == len(groups) - 1)
                for t in range(4):
                    c = tb * 4 + t
                    q = state["out"] % 2
                    state["out"] += 1
                    for n in range(2):
                        mm_group([(PO[q][:, n * 512:(n + 1) * 512], hT[:, j, t * 128:(t + 1) * 128], wd[:, j, n * 512:(n + 1) * 512])
                                  for j in range(nj)], [f"hT{si % 2}_{j}" for j in range(nj)] + [f"wd{b}"], [RO[q][n]])
                    xt = x_tok[:, c, :]
                    XR = f"x_tok{c}"
                    if ex is None:
                        dve(lambda e, xt=xt, q=q: e.tensor_add(out=xt, in0=xt, in1=PO[q][:, :]), [XR, RO[q][0], RO[q][1]], [XR])
                    else:
                        dve(lambda e, xt=xt, q=q, c=c, ex=ex: e.scalar_tensor_tensor(
                            out=xt, in0=PO[q][:, :], scalar=comb[:, c, ex:ex + 1], in1=xt, op0=ALU.mult, op1=ALU.add),
                            [XR, RO[q][0], RO[q][1], "hm"], [XR])
                    if final:
                        if len(ln_active) >= 4:
                            for _ in ln_active.pop(0):
                                pass
                        ln_active.append(layer_norm_gen(c, XR, lnp, "lnp", to_xT=(not last), out_dma=last,
                                                        xscale=(1.0 if final_layer else 1.0 / ALPHA), k=c % 4))
                    pump(2)

            load(0)
            if len(groups) > 1:
                load(1)
            for si in range(len(steps)):
                gi, tb = steps[si]
                GU(si)
                if si > 0:
                    DOWN(si - 1)
                    pgi, ptb = steps[si - 1]
                    if ptb == 3 and pgi + 2 < len(groups):
                        load(pgi + 2)
            DOWN(len(steps) - 1)
            while ln_active:
                pump(1)

        def router():
            lg = hm[:, 128:256].rearrange("p (t e) -> p t e", e=NE)
            for c in range(NCH):
                mm_group([(PB[c % 2][:, 0:NE], xT[:, k, c * 128:(c + 1) * 128], wr_sb[:, k, :]) for k in range(8)],
                         [f"xT{c}", "wr_sb"], [RB[c % 2]])
                dve(lambda e, c=c: e.tensor_tensor(out=lg[:, c, :], in0=PB[c % 2][:, 0:NE], in1=sbt[:, 902:910], op=ALU.add),
                    [RB[c % 2], "sbt"], ["hm"])
            m1 = rt[:, 0:16]
            m2 = rt[:, 16:32]
            e21 = rt[:, 32:48]
            g1 = rt[:, 48:64]
            oh1 = hm[:, 256:384].rearrange("p (t e) -> p t e", e=NE)
            l2 = gv[:, 0:128].rearrange("p (t e) -> p t e", e=NE)
            oh2 = gv[:, 128:256].rearrange("p (t e) -> p t e", e=NE)
            dve(lambda e: e.tensor_reduce(out=m1, in_=lg, axis=AX.X, op=ALU.max), ["hm"], ["rt"])
            dve(lambda e: e.tensor_tensor(out=oh1, in0=lg, in1=m1.unsqueeze(2).to_broadcast([128, NCH, NE]), op=ALU.is_equal),
                ["hm", "rt"], ["hm"])
            dve(lambda e: e.scalar_tensor_tensor(out=l2, in0=oh1, scalar=-1e30, in1=lg, op0=ALU.mult, op1=ALU.add),
                ["hm", "hm"], ["gv"])
            dve(lambda e: e.tensor_reduce(out=m2, in_=l2, axis=AX.X, op=ALU.max), ["gv"], ["rt"])
            dve(lambda e: e.tensor_tensor(out=oh2, in0=l2, in1=m2.unsqueeze(2).to_broadcast([128, NCH, NE]), op=ALU.is_equal),
                ["gv", "rt"], ["gv"])
            dve(lambda e: e.tensor_sub(out=e21, in0=m2, in1=m1), ["rt", "rt"], ["rt"])
            act(e21, e21, AF.Exp, ["rt"], ["rt"])
            dve(lambda e: e.tensor_scalar(out=g1, in0=e21, scalar1=1.0, scalar2=None, op0=ALU.add), ["rt"], ["rt"])
            dve(lambda e: e.reciprocal(out=g1, in_=g1), ["rt"], ["rt"])
            dve(lambda e: e.tensor_mul(out=e21, in0=e21, in1=g1), ["rt", "rt"], ["rt"])
            dve(lambda e: e.tensor_tensor(out=oh1, in0=oh1, in1=g1.unsqueeze(2).to_broadcast([128, NCH, NE]), op=ALU.mult),
                ["hm", "rt"], ["hm"])
            dve(lambda e: e.tensor_tensor(out=oh2, in0=oh2, in1=e21.unsqueeze(2).to_broadcast([128, NCH, NE]), op=ALU.mult),
                ["gv", "rt"], ["gv"])
            dve(lambda e: e.tensor_add(out=comb, in0=oh1, in1=oh2), ["hm", "gv"], ["hm"])

        with nc.allow_non_contiguous_dma(reason="small parameter loads"):
            T.dma("pool", [(wr_sb[:, :, :], wr_d[0].rearrange("(k p) e -> p k e", p=128))], [], ["wr_sb"], "d_wr")
        done = False
        if stage == "s0":
            dump_x()
            done = True
        for l in range(DEPTH if stage != "s0" else 0):
            is_moe = (l % 2 == 1)
            last = (l == DEPTH - 1)
            with nc.allow_non_contiguous_dma(reason="small parameter loads"):
                T.dma("sp", [(ppt[:, :], pp_d[l])], [], ["ppt"], "d_ppt")
                T.dma("sp", [(sbt[:, :], sbr_d[l].partition_broadcast(128))], [], ["sbt"], "d_sbt")
                T.dma("sp", [(lnp[:, :], lnp_d[l, 0].partition_broadcast(128))], [], ["lnp"], "d_lnp")
                load_bias_rows(l)
            if l == 0:
                for c in range(NCH):
                    T.dma("sp", [(x_tok[:, c, :], x_d[c * 128:(c + 1) * 128, :])], [], [f"x_tok{c}"], f"d_x{c}")
            dve(lambda e: e.tensor_scalar(out=lnp[:, :], in0=lnp[:, :], scalar1=ALPHA, scalar2=None, op0=ALU.mult), ["lnp"], ["lnp"])
            dve(lambda e: e.tensor_scalar(out=sbt[:, 0:384], in0=sbt[:, 0:384], scalar1=0.5, scalar2=None, op0=ALU.mult), ["sbt"], ["sbt"])
            load_mixer_weights(l)
            if l == 0:
                for c in range(NCH):
                    XR = f"x_tok{c}"
                    act(hcat[:, :], x_tok[:, c, :], AF.Copy, ["x_tok", XR], ["hcat_a", "hcat_b0", "hcat_b1", "hcat_c"])
                    trb = pb_bf(2)
                    T.group("pe", [lambda e, k=k: e.transpose(out=trb[:, k * 128:(k + 1) * 128], in_=hcat[:, k * 128:(k + 1) * 128],
                                                              identity=ident[:, :]) for k in range(8)],
                            ["hcat_a", "hcat_b0", "hcat_b1", "hcat_c", "ident"], ["ps2"])
                    tr3 = trb[:, :].rearrange("p (k t) -> p k t", k=8)
                    act(xT[:, 0:4, c * 128:(c + 1) * 128], tr3[:, 0:4, :], AF.Copy, ["ps2"], [f"xT{c}"])
                    dve(lambda e, c=c, tr3=tr3: e.tensor_copy(out=xT[:, 4:8, c * 128:(c + 1) * 128], in_=tr3[:, 4:8, :]), ["ps2", f"xT{c}"], [f"xT{c}"])
            pool(lambda e: e.memset(carry[:, :, :], 0.0), [], ["carry"])
            if stage == "s1":
                dump_x()
                done = True
                break
            def step(gen, sfx):
                T.suffix = sfx
                try:
                    r = next(gen)
                except StopIteration:
                    r = "END"
                T.suffix = "0"
                return r

            def chain2(g1, g2):
                for v in g1:
                    yield v
                for v in g2:
                    yield v

            prev = None
            for c in range(NCH):
                in_tail0 = False
                if c % 4 == 0:
                    cur = (chain2(phaseA(l, c // 4), phaseB(l, c, is_moe)), str(c % 2))
                    if prev is not None:
                        for _ in range(build.hold):
                            r = step(*prev)
                            if r == "END":
                                prev = None
                                break
                            if r == "S":
                                in_tail0 = True
                else:
                    cur = (phaseB(l, c, is_moe), str(c % 2))
                b1_done = False
                in_tail = in_tail0
                while not b1_done:
                    if prev is None:
                        if step(*cur) == "B1":
                            b1_done = True
                        continue
                    n_prev = 1 if in_tail else (build.ratio_b if c % 4 == 0 else build.ratio)
                    for _ in range(n_prev):
                        r = step(*prev)
                        if r == "END":
                            prev = None
                            break
                        if r == "S":
                            in_tail = True
                            break
                    for _ in range(build.dens if in_tail else 1):
                        if step(*cur) == "B1":
                            b1_done = True
                            break
                if prev is not None:
                    while step(*prev) != "END":
                        pass
                prev = cur
            while step(*prev) != "END":
                pass
            if stage == f"mix{l}":
                dump_x()
                done = True
                break
            ffn_phase(l, is_moe, last and stage == "full", last)
            if stage == f"l{l}" and not (last and stage == "full"):
                if DBG:
                    dbgdump('xT_k0', xT[:, 0, 0:1024], [f"xT{t}" for t in range(8)])
                    dbgdump('xT_k5', xT[:, 5, 0:1024], [f"xT{t}" for t in range(8)])
                dump_x()
                done = True
                break
        if not done:
            T.wait_all("sp", ["d_out"])
        build.ninst = T.nins
        build.sbuf_left = nc.sbuf_bytes_remaining
    return nc


_NC_CACHE = {}


def _host_layout(inp, b):
    f32 = np.float32
    pp = np.zeros((DEPTH, 128, NPP), f32)
    sbr = np.zeros((DEPTH, 1, NSB), f32)
    lnpv = np.zeros((DEPTH, 2, 1, 2 * D), f32)
    btok = np.zeros((DEPTH, 1, NTOKC), f32)
    for l in range(DEPTH):
        cw = np.asarray(inp["conv_w"][l], f32)
        pp[l, :, 0:24] = cw.reshape(4, 6, 128).transpose(2, 1, 0).reshape(128, 24)
        pp[l, :, 24:30] = np.asarray(inp["b_in"][l, 0:768], f32).reshape(6, 128).T
        pp[l, :, 30:34] = np.asarray(inp["sgu_b_s"][l], f32).T
        sbr[l, 0, 0:384] = inp["mlstm_norm_g"][l]
        sbr[l, 0, 384:390] = inp["attn_sinks"][l]
        sbr[l, 0, 390:646] = inp["sgu_norm_g"][l]
        sbr[l, 0, 646:902] = inp["sgu_norm_b"][l]
        if l % 2 == 1:
            sbr[l, 0, 902:910] = inp["moe_b_router"][l // 2]
        sbr[l, 0, 910:922] = inp["b_in"][l, C_MI:C_AQ]
        lnpv[l, 0, 0, 0:D] = inp["ln1_g"][l]
        lnpv[l, 0, 0, D:] = inp["ln1_b"][l]
        lnpv[l, 1, 0, 0:D] = inp["ln2_g"][l]
        lnpv[l, 1, 0, D:] = inp["ln2_b"][l]
        btok[l, 0, :] = inp["b_in"][l, TOKC0:]
    shared = {
        "w_in": np.ascontiguousarray(inp["w_in"], f32), "pp": pp, "sbr": sbr, "lnp": lnpv, "btok": btok,
        "sgu_w_s": np.ascontiguousarray(inp["sgu_w_s"], f32), "w_out": np.ascontiguousarray(inp["w_out"], f32),
        "ffn_w_gate": np.ascontiguousarray(inp["ffn_w_gate"], f32), "ffn_w_up": np.ascontiguousarray(inp["ffn_w_up"], f32),
        "ffn_w_down": np.ascontiguousarray(inp["ffn_w_down"], f32), "moe_w_router": np.ascontiguousarray(inp["moe_w_router"], f32),
        "moe_w_gate": np.ascontiguousarray(inp["moe_w_gate"], f32), "moe_w_up": np.ascontiguousarray(inp["moe_w_up"], f32),
        "moe_w_down": np.ascontiguousarray(inp["moe_w_down"], f32),
    }
    return shared


def make_in_maps(inp, cores):
    shared = _host_layout(inp, 0)
    maps = []
    for b in cores:
        m = dict(shared)
        m["x"] = np.ascontiguousarray(inp["x"][b], np.float32)
        m["posT"] = np.ascontiguousarray(np.asarray(inp["positions"][b], np.int32).reshape(NCH, 128).T)
        maps.append(m)
    return maps


def kernel(**inputs):
    if "full" not in _NC_CACHE:
        _NC_CACHE["full"] = build("full")
    nc = _NC_CACHE["full"]
    in_maps = make_in_maps(inputs, list(range(NCORES)))
    res = run_bass_kernel_spmd(nc, in_maps, core_ids=list(range(NCORES)))
    return np.stack([np.asarray(r["y"], np.float32) for r in res.results], axis=0)
```
